# Optimizing a Trainium2 kernel written in Bass

```python
import math
import jax, jax.numpy as jnp
from jax import lax
import numpy as np

D_MODEL = 2048
BATCH = 2
SEQ = 4096
DEPTH = 4

GRID_W = 64
CTX_LEN = 256
HEAD_DIM = 128
Q_BLOCK = 128
ROPE_BASE = 10000.0
EPS = 1e-6
NEG_INF = -1e30

A_HEADS = 8
A_QK_DIM = 64
A_V_DIM = 128
B_HEADS = 8
B_KV_HEADS = 2
WINDOW = 128
NBAND = (WINDOW + Q_BLOCK - 1) // Q_BLOCK
PAD = NBAND * Q_BLOCK
BAND = (2 * NBAND + 1) * Q_BLOCK
C_HEADS = 8
C_KV_HEADS = 2
D_HEADS = 8
D_Q_RANK = 512
D_KV_RANK = 256
D_NOPE = 128
D_ROPE = 64
D_V = 128

A_Q = A_HEADS * 2 * A_QK_DIM
A_K = A_HEADS * 2 * A_QK_DIM
A_V = A_HEADS * A_V_DIM
A_IN = A_Q + A_K + A_V
B_Q = B_HEADS * HEAD_DIM
B_K = B_KV_HEADS * HEAD_DIM
B_V = B_KV_HEADS * HEAD_DIM
B_IN = B_Q + B_K + B_V
C_Q = C_HEADS * HEAD_DIM
C_K = C_KV_HEADS * HEAD_DIM
C_V = C_KV_HEADS * HEAD_DIM
C_IN = C_Q + C_K + C_V
D_IN = D_Q_RANK + D_KV_RANK + D_ROPE
MIX_WIDTH = A_HEADS * A_V_DIM + B_HEADS * HEAD_DIM
EVEN_IN = A_IN + B_IN + MIX_WIDTH
ODD_IN = C_IN + D_IN + MIX_WIDTH
N_EVEN = (DEPTH + 1) // 2
N_ODD = DEPTH // 2

kernel_name = "hybrid_diff_window_qknorm_mla_prefix_trunk"


def rmsnorm(x, g):
    xf = x.astype(jnp.float32)
    y = xf * lax.rsqrt(jnp.mean(xf * xf, axis=-1, keepdims=True) + EPS)
    return (y * g.astype(jnp.float32)).astype(x.dtype)


def axial_angles(n_tok, dim):
    rows = n_tok // GRID_W
    row = jnp.repeat(jnp.arange(rows), GRID_W).astype(jnp.float32)
    col = jnp.tile(jnp.arange(GRID_W), rows).astype(jnp.float32)
    n_freq = dim // 4
    inv = ROPE_BASE ** (-jnp.arange(n_freq, dtype=jnp.float32) / n_freq)
    ang = jnp.concatenate([row[:, None] * inv[None], col[:, None] * inv[None]], axis=-1)
    return jnp.cos(ang), jnp.sin(ang)


def apply_rope(x, cs):
    cos, sin = cs
    shape = (1, cos.shape[0]) + (1,) * (x.ndim - 3) + (cos.shape[1],)
    cos = cos.reshape(shape).astype(x.dtype)
    sin = sin.reshape(shape).astype(x.dtype)
    x1, x2 = jnp.split(x, 2, axis=-1)
    return jnp.concatenate([x1 * cos - x2 * sin, x1 * sin + x2 * cos], axis=-1)


def softmax_with_sink(s, sink):
    full = jnp.concatenate([s, jnp.broadcast_to(sink, s.shape[:-1] + (1,)).astype(jnp.float32)], axis=-1)
    return jax.nn.softmax(full, axis=-1)[..., :-1]


def attend(q, k, v, scale):
    b, lq, hq, dk = q.shape
    hk = k.shape[2]
    qg = q.reshape(b, lq, hk, hq // hk, dk)
    s = jnp.einsum('bqhgd,bkhd->bhgqk', qg, k).astype(jnp.float32) * scale
    p = jax.nn.softmax(s, axis=-1).astype(v.dtype)
    o = jnp.einsum('bhgqk,bkhd->bqhgd', p, v)
    return o.reshape(b, lq, hq, v.shape[-1])


def diff_attend(q1, q2, k1, k2, v, lam, scale):
    p1 = jax.nn.softmax(jnp.einsum('bqhd,bkhd->bhqk', q1, k1).astype(jnp.float32) * scale, axis=-1)
    p2 = jax.nn.softmax(jnp.einsum('bqhd,bkhd->bhqk', q2, k2).astype(jnp.float32) * scale, axis=-1)
    p = (p1 - lam * p2).astype(v.dtype)
    return jnp.einsum('bhqk,bkhd->bqhd', p, v)


def over_query_blocks(fn, *qs):
    b, s = qs[0].shape[:2]
    nb = s // Q_BLOCK
    blocks = tuple(jnp.moveaxis(q.reshape((b, nb, Q_BLOCK) + q.shape[2:]), 1, 0) for q in qs)
    out = lax.map(lambda args: fn(*args), blocks)
    return jnp.moveaxis(out, 0, 1).reshape((b, s) + out.shape[3:])


def band_mask(nb, s):
    qpos = jnp.arange(nb)[:, None, None] * Q_BLOCK + jnp.arange(Q_BLOCK)[None, :, None]
    kpos = jnp.arange(nb)[:, None, None] * Q_BLOCK - PAD + jnp.arange(BAND)[None, None, :]
    return (jnp.abs(kpos - qpos) <= WINDOW) & (kpos >= 0) & (kpos < s)


def mixer_a(u_lat, u_ctx, cs64, lam, lam_init, sub_g, with_ctx):
    def heads(u):
        b, l = u.shape[:2]
        q, k, v = jnp.split(u, [A_Q, A_Q + A_K], axis=-1)
        q = q.reshape(b, l, A_HEADS, 2 * A_QK_DIM)
        k = k.reshape(b, l, A_HEADS, 2 * A_QK_DIM)
        v = v.reshape(b, l, A_HEADS, A_V_DIM)
        return q[..., :A_QK_DIM], q[..., A_QK_DIM:], k[..., :A_QK_DIM], k[..., A_QK_DIM:], v

    q1, q2, k1, k2, v = heads(u_lat)
    q1, q2, k1, k2 = (apply_rope(t, cs64) for t in (q1, q2, k1, k2))
    cq1, cq2, ck1, ck2, cv = heads(u_ctx)
    k1a = jnp.concatenate([ck1, k1], axis=1)
    k2a = jnp.concatenate([ck2, k2], axis=1)
    va = jnp.concatenate([cv, v], axis=1)
    scale = A_QK_DIM ** -0.5

    def post(o):
        b, l = o.shape[:2]
        return (rmsnorm(o, sub_g) * (1.0 - lam_init)).reshape(b, l, A_HEADS * A_V_DIM)

    o = over_query_blocks(lambda a1, a2: diff_attend(a1, a2, k1a, k2a, va, lam, scale), q1, q2)
    y = post(o)
    yc = post(diff_attend(cq1, cq2, ck1, ck2, cv, lam, scale)) if with_ctx else None
    return y, yc


def mixer_b(u_lat, u_ctx, cs128, sink, with_ctx):
    g = B_HEADS // B_KV_HEADS

    def heads(u):
        b, l = u.shape[:2]
        q, k, v = jnp.split(u, [B_Q, B_Q + B_K], axis=-1)
        return (q.reshape(b, l, B_KV_HEADS, g, HEAD_DIM),
                k.reshape(b, l, B_KV_HEADS, HEAD_DIM),
                v.reshape(b, l, B_KV_HEADS, HEAD_DIM))

    q, k, v = heads(u_lat)
    q, k = apply_rope(q, cs128), apply_rope(k, cs128)
    cq, ck, cv = heads(u_ctx)
    b, s = q.shape[:2]
    lc = ck.shape[1]
    nb = s // Q_BLOCK
    scale = HEAD_DIM ** -0.5
    sink_hg = sink.reshape(B_KV_HEADS, g)

    def band(t):
        tp = jnp.pad(t, ((0, 0), (PAD, PAD), (0, 0), (0, 0)))
        tp = tp.reshape((b, nb + 2 * NBAND, Q_BLOCK) + t.shape[2:])
        return jnp.concatenate([tp[:, i:i + nb] for i in range(2 * NBAND + 1)], axis=2)

    qb = q.reshape(b, nb, Q_BLOCK, B_KV_HEADS, g, HEAD_DIM)
    kb, vb = band(k), band(v)
    s_win = jnp.einsum('bnqhgd,bnkhd->bnhgqk', qb, kb).astype(jnp.float32) * scale
    s_win = jnp.where(band_mask(nb, s)[None, :, None, None], s_win, NEG_INF)
    s_ctx = jnp.einsum('bnqhgd,bkhd->bnhgqk', qb, ck).astype(jnp.float32) * scale
    p = softmax_with_sink(jnp.concatenate([s_ctx, s_win], axis=-1),
                          sink_hg[None, None, :, :, None, None]).astype(v.dtype)
    o = (jnp.einsum('bnhgqk,bkhd->bnqhgd', p[..., :lc], cv)
         + jnp.einsum('bnhgqk,bnkhd->bnqhgd', p[..., lc:], vb))
    y = o.reshape(b, s, B_HEADS * HEAD_DIM)
    yc = None
    if with_ctx:
        sc = jnp.einsum('bqhgd,bkhd->bhgqk', cq, ck).astype(jnp.float32) * scale
        pc = softmax_with_sink(sc, sink_hg[None, :, :, None, None]).astype(cv.dtype)
        yc = jnp.einsum('bhgqk,bkhd->bqhgd', pc, cv).reshape(b, lc, B_HEADS * HEAD_DIM)
    return y, yc


def mixer_c(u_lat, u_ctx, cs128, q_g, k_g, with_ctx):
    def heads(u):
        b, l = u.shape[:2]
        q, k, v = jnp.split(u, [C_Q, C_Q + C_K], axis=-1)
        q = rmsnorm(q.reshape(b, l, C_HEADS, HEAD_DIM), q_g)
        k = rmsnorm(k.reshape(b, l, C_KV_HEADS, HEAD_DIM), k_g)
        return q, k, v.reshape(b, l, C_KV_HEADS, HEAD_DIM)

    q, k, v = heads(u_lat)
    q, k = apply_rope(q, cs128), apply_rope(k, cs128)
    cq, ck, cv = heads(u_ctx)
    ka = jnp.concatenate([ck, k], axis=1)
    va = jnp.concatenate([cv, v], axis=1)
    scale = HEAD_DIM ** -0.5
    b, s = q.shape[:2]
    y = over_query_blocks(lambda qb: attend(qb, ka, va, scale), q).reshape(b, s, C_HEADS * HEAD_DIM)
    yc = attend(cq, ck, cv, scale).reshape(b, ck.shape[1], C_HEADS * HEAD_DIM) if with_ctx else None
    return y, yc


def mixer_d(u_lat, u_ctx, cs64, q_a_g, kv_a_g, w_q_b, w_kv_b, with_ctx):
    def heads(u, rope):
        b, l = u.shape[:2]
        cq, ckv, kpe = jnp.split(u, [D_Q_RANK, D_Q_RANK + D_KV_RANK], axis=-1)
        q = (rmsnorm(cq, q_a_g) @ w_q_b).reshape(b, l, D_HEADS, D_NOPE + D_ROPE)
        kv = (rmsnorm(ckv, kv_a_g) @ w_kv_b).reshape(b, l, D_HEADS, D_NOPE + D_V)
        q_nope, q_pe = q[..., :D_NOPE], q[..., D_NOPE:]
        k_nope, v = kv[..., :D_NOPE], kv[..., D_NOPE:]
        kpe = kpe.reshape(b, l, 1, D_ROPE)
        if rope:
            q_pe, kpe = apply_rope(q_pe, cs64), apply_rope(kpe, cs64)
        q = jnp.concatenate([q_nope, q_pe], axis=-1)
        k = jnp.concatenate([k_nope, jnp.broadcast_to(kpe, (b, l, D_HEADS, D_ROPE))], axis=-1)
        return q, k, v

    q, k, v = heads(u_lat, True)
    cq, ck, cv = heads(u_ctx, False)
    ka = jnp.concatenate([ck, k], axis=1)
    va = jnp.concatenate([cv, v], axis=1)
    scale = (D_NOPE + D_ROPE) ** -0.5
    b, s = q.shape[:2]
    y = over_query_blocks(lambda qb: attend(qb, ka, va, scale), q).reshape(b, s, D_HEADS * D_V)
    yc = attend(cq, ck, cv, scale).reshape(b, ck.shape[1], D_HEADS * D_V) if with_ctx else None
    return y, yc


def setup_inputs(seed: int = 0) -> dict:
    key = jax.random.key(seed)
    ks = jax.random.split(key, 23)
    n = lambda k, shape, s: jax.random.normal(k, shape, jnp.float32) * s
    gain = lambda k, shape: 1.0 + 0.02 * jax.random.normal(k, shape, jnp.float32)
    return {
        "x": n(ks[0], (BATCH, SEQ, D_MODEL), 1.0),
        "c": n(ks[1], (BATCH, D_MODEL), 1.0),
        "ctx": n(ks[2], (BATCH, CTX_LEN, D_MODEL), 1.0),
        "c_ctx": n(ks[3], (D_MODEL,), 1.0),
        "w_mod": n(ks[4], (DEPTH, D_MODEL, 3 * D_MODEL), 0.5 * D_MODEL ** -0.5),
        "b_mod": n(ks[5], (DEPTH, 3 * D_MODEL), 0.01),
        "norm_g": gain(ks[6], (DEPTH, D_MODEL)),
        "w_o": n(ks[7], (DEPTH, MIX_WIDTH, D_MODEL), MIX_WIDTH ** -0.5),
        "final_g": gain(ks[8], (D_MODEL,)),
        "e_w_in": n(ks[9], (N_EVEN, D_MODEL, EVEN_IN), D_MODEL ** -0.5),
        "a_lam_q1": n(ks[10], (N_EVEN, A_QK_DIM), 0.1),
        "a_lam_k1": n(ks[11], (N_EVEN, A_QK_DIM), 0.1),
        "a_lam_q2": n(ks[12], (N_EVEN, A_QK_DIM), 0.1),
        "a_lam_k2": n(ks[13], (N_EVEN, A_QK_DIM), 0.1),
        "a_sub_g": gain(ks[14], (N_EVEN, A_V_DIM)),
        "b_sink": n(ks[15], (N_EVEN, B_HEADS), 1.0),
        "o_w_in": n(ks[16], (N_ODD, D_MODEL, ODD_IN), D_MODEL ** -0.5),
        "c_q_g": gain(ks[17], (N_ODD, HEAD_DIM)),
        "c_k_g": gain(ks[18], (N_ODD, HEAD_DIM)),
        "d_q_a_g": gain(ks[19], (N_ODD, D_Q_RANK)),
        "d_kv_a_g": gain(ks[20], (N_ODD, D_KV_RANK)),
        "d_w_q_b": n(ks[21], (N_ODD, D_Q_RANK, D_HEADS * (D_NOPE + D_ROPE)), D_Q_RANK ** -0.5),
        "d_w_kv_b": n(ks[22], (N_ODD, D_KV_RANK, D_HEADS * (D_NOPE + D_V)), D_KV_RANK ** -0.5),
    }


def reference(x, c, ctx, c_ctx, w_mod, b_mod, norm_g, w_o, final_g,
              e_w_in, a_lam_q1, a_lam_k1, a_lam_q2, a_lam_k2, a_sub_g, b_sink,
              o_w_in, c_q_g, c_k_g, d_q_a_g, d_kv_a_g, d_w_q_b, d_w_kv_b):
    s = x.shape[1]
    cs128 = axial_angles(s, HEAD_DIM)
    cs64 = axial_angles(s, A_QK_DIM)
    silu_c = jax.nn.silu(c)
    silu_cc = jax.nn.silu(c_ctx)
    cx = ctx
    for l in range(DEPTH):
        with_ctx = l < DEPTH - 1
        i = l // 2
        mod = silu_c @ w_mod[l] + b_mod[l]
        shift, scale, gate = jnp.split(mod[:, None, :], 3, axis=-1)
        cmod = silu_cc @ w_mod[l] + b_mod[l]
        cshift, cscale, cgate = jnp.split(cmod, 3)
        h = rmsnorm(x, norm_g[l]) * (1.0 + scale) + shift
        hc = rmsnorm(cx, norm_g[l]) * (1.0 + cscale) + cshift
        if l % 2 == 0:
            u = h @ e_w_in[i]
            uc = hc @ e_w_in[i]
            ua, ub, ug = jnp.split(u, [A_IN, A_IN + B_IN], axis=-1)
            uca, ucb, ucg = jnp.split(uc, [A_IN, A_IN + B_IN], axis=-1)
            lam_init = 0.8 - 0.6 * math.exp(-0.3 * l)
            lam = (jnp.exp(jnp.sum(a_lam_q1[i].astype(jnp.float32) * a_lam_k1[i].astype(jnp.float32)))
                   - jnp.exp(jnp.sum(a_lam_q2[i].astype(jnp.float32) * a_lam_k2[i].astype(jnp.float32)))
                   + lam_init)
            y1, yc1 = mixer_a(ua, uca, cs64, lam, lam_init, a_sub_g[i], with_ctx)
            y2, yc2 = mixer_b(ub, ucb, cs128, b_sink[i], with_ctx)
        else:
            u = h @ o_w_in[i]
            uc = hc @ o_w_in[i]
            ucc_, ud, ug = jnp.split(u, [C_IN, C_IN + D_IN], axis=-1)
            ucc2, ucd, ucg = jnp.split(uc, [C_IN, C_IN + D_IN], axis=-1)
            y1, yc1 = mixer_c(ucc_, ucc2, cs128, c_q_g[i], c_k_g[i], with_ctx)
            y2, yc2 = mixer_d(ud, ucd, cs64, d_q_a_g[i], d_kv_a_g[i], d_w_q_b[i], d_w_kv_b[i], with_ctx)
        y = jnp.concatenate([y1, y2], axis=-1) * jax.nn.silu(ug)
        x = x + gate * (y @ w_o[l])
        if with_ctx:
            yc = jnp.concatenate([yc1, yc2], axis=-1) * jax.nn.silu(ucg)
            cx = cx + cgate * (yc @ w_o[l])
    return rmsnorm(x, final_g)
```

```python
import numpy as np
from contextlib import ExitStack
import concourse.bass as bass
import concourse.mybir as mybir
from concourse.bass_utils import run_bass_kernel_spmd

F32 = mybir.dt.float32
BF16 = mybir.dt.bfloat16
AF = mybir.ActivationFunctionType
ALU = mybir.AluOpType
AX = mybir.AxisListType

DM = 2048
NTILE = 9
TOK = 1088
EPS = 1e-6
NKEY = 4352
NKT = 34


class Buf:
    def __init__(self, name):
        self.name = name
        self.w = {}
        self.wf = {}
        self.r = {}


class Prog:
    def __init__(self):
        self.nc = bass.Bass("TRN2", target_bir_lowering=False)
        self.st = ExitStack()
        nc = self.nc
        self.E = {"pe": nc.tensor, "act": nc.scalar, "dve": nc.vector, "pool": nc.gpsimd, "sp": nc.sync}
        self.sem, self.inc, self.seq = {}, {}, {}
        for k in ("pe", "act", "dve", "pool", "cc"):
            self.sem[k] = self.st.enter_context(nc.semaphore(k))
            self.inc[k] = 1
            self.seq[k] = 0
        self.nslots = {"sp": 16, "pool": 8, "act": 4}
        self.slot = {"sp": 0, "pool": 0, "act": 0}
        for q, n in self.nslots.items():
            for i in range(n):
                k = "%s_d%d" % (q, i)
                self.sem[k] = self.st.enter_context(nc.semaphore(k))
                self.inc[k] = 16
                self.seq[k] = 0
        self.known = {k: {} for k in self.E}
        self.nwait = 0

    def _wait(self, eng, deps):
        for e, s in deps.items():
            if s <= 0:
                continue
            if eng == "pe" and e == "pe":
                continue
            if self.known[eng].get(e, 0) >= s:
                continue
            self.E[eng].wait_ge(self.sem[e], s * self.inc[e])
            self.known[eng][e] = s
            self.nwait += 1

    @staticmethod
    def _deps(reads, writes, parts):
        deps = {}

        def add(d):
            for e, s in d.items():
                if deps.get(e, 0) < s:
                    deps[e] = s
        for b in reads:
            add(b.w)
        for b in writes:
            add(b.w)
            add(b.r)
        for b in parts:
            add(b.r)
            add(b.wf)
        return deps

    @staticmethod
    def _commit(e, s, reads, writes, parts):
        for b in reads:
            if b.r.get(e, 0) < s:
                b.r[e] = s
        for b in writes:
            b.w = {e: s}
            b.wf = {e: s}
            b.r = {}
        for b in parts:
            if b.w.get(e, 0) < s:
                b.w[e] = s

    def op(self, eng, fn, reads=(), writes=(), parts=(), sig=True):
        self._wait(eng, self._deps(reads, writes, parts))
        inst = fn(self.E[eng])
        if sig:
            self.seq[eng] += 1
            inst.then_inc(self.sem[eng], 1)
            s = self.seq[eng]
        else:
            s = self.seq[eng] + 1
        self._commit(eng, s, reads, writes, parts)
        return inst

    def dma(self, q, out, in_, reads=(), writes=(), parts=(), **kw):
        k = "%s_d%d" % (q, self.slot[q])
        self.slot[q] = (self.slot[q] + 1) % self.nslots[q]
        deps = self._deps(reads, writes, parts)
        if deps.get(k, 0) < self.seq[k]:
            deps[k] = self.seq[k]
        self._wait(q, deps)
        inst = self.E[q].dma_start(out=out, in_=in_, **kw)
        self.seq[k] += 1
        inst.then_inc(self.sem[k], 16)
        self._commit(k, self.seq[k], reads, writes, parts)
        return inst

    def barrier(self):
        allv = dict(self.seq)
        for eng in self.E:
            self._wait(eng, allv)

    def sb(self, stack, name, shape, dt):
        self.uid = getattr(self, "uid", 0) + 1
        return stack.enter_context(self.nc.sbuf_tensor("%s_%d" % (name, self.uid), list(shape), dt))

    def ps(self, stack, name, shape, dt):
        return stack.enter_context(self.nc.psum_tensor(name, list(shape), dt))


class _Stop(Exception):
    pass


DEBUG_NSTEP = None


def build(n_layers=4):
    P = Prog()

    def chk(k):
        if DEBUG_NSTEP == k:
            raise _Stop()

    nc = P.nc
    st = P.st

    def din(name, shape):
        return nc.dram_tensor(name, list(shape), F32, kind="ExternalInput").ap()

    x_in = din("x", [1024, DM])
    ctx_in = din("ctx", [64, DM])
    cvec = din("cvec", [2, DM])
    wmod = din("wmod", [4, DM, 1536])
    bmod = din("bmod", [4, 1536])
    norm_g = din("norm_g", [4, DM])
    final_g = din("final_g", [1, DM])
    w_o = din("w_o", [4, DM, DM])
    e_w = din("e_w_in", [2, DM, 6656])
    o_w = din("o_w_in", [2, DM, 4416])
    a_lam = din("a_lam", [2, 256])
    a_sub_g = din("a_sub_g", [2, 128])
    b_sink = din("b_sink", [2, 8])
    c_q_g = din("c_q_g", [2, 128])
    c_k_g = din("c_k_g", [2, 128])
    d_q_a_g = din("d_q_a_g", [2, 512])
    d_kv_a_g = din("d_kv_a_g", [2, 256])
    d_w_q_b = din("d_w_q_b", [2, 512, 1536])
    d_w_kv_b = din("d_w_kv_b", [2, 256, 2048])
    rope_in = din("rope", [128, 8 * 288])
    ident_in = din("ident", [128, 128])
    bmask_in = din("bmask", [128, 10 * 128])
    out = nc.dram_tensor("out", [1024, DM], F32, kind="ExternalOutput").ap()

    xs = nc.dram_tensor("xs", [TOK, DM], F32).ap()
    modloc = nc.dram_tensor("modloc", [8, 1536], F32)
    modall = nc.dram_tensor("modall", [32, 1536], F32)
    GROUPS = [[0, 1, 2, 3], [4, 5, 6, 7]]

    ccn = [0]

    def allgather(src_t, dst_t, reads, writes):
        k = "cc%d" % ccn[0]
        ccn[0] += 1
        P.sem[k] = st.enter_context(nc.semaphore(k))
        P.inc[k] = 1
        P.seq[k] = 0
        P._wait("pool", P._deps(reads, (), writes))
        inst = nc.gpsimd.collective_compute("AllGather", ALU.bypass, replica_groups=GROUPS,
                                            ins=[src_t.ap().opt()], outs=[dst_t.ap().opt()])
        inst.then_inc(P.sem[k])
        P.seq[k] = 1
        P._commit(k, 1, reads, (), writes)

    identb = P.sb(st, "identb", [128, 128], BF16)
    maskb = P.sb(st, "maskb", [128, 10, 128], BF16)
    ropet = P.sb(st, "ropet", [128, 8, 288], F32)
    modv = P.sb(st, "modv", [128, 4, 5, 16], F32)
    hT = P.sb(st, "hT", [128, NTILE, 16, 128], BF16)
    wb = [None, None, None]
    nwb = [2]
    lp = P.sb(st, "lp", [128, 1280], F32)
    sm = P.sb(st, "sm", [128, 64], F32)
    b_const = Buf("const")
    b_modv = Buf("modv")
    b_gate = Buf("gate")
    bhT = [Buf("hT%d" % t) for t in range(NTILE)]
    bwb = [Buf("wb0"), Buf("wb1"), Buf("wb2")]
    b_lp = Buf("lp")
    b_sm = Buf("sm")
    wcnt = [0]

    psA = P.ps(st, "psA", [128, 2048], F32)
    psB = P.ps(st, "psB", [128, 1024], F32)
    psT = P.ps(st, "psT", [128, 2, 1024], BF16)
    bA = [Buf("psA%d" % i) for i in range(4)]
    bB = [Buf("psB%d" % i) for i in range(2)]
    bT = [Buf("psT%d" % i) for i in range(2)]
    tcnt = [0]

    bxs = [[Buf("xs%d_%d" % (t, nb)) for nb in range(4)] for t in range(NTILE)]

    P.dma("pool", out=identb[:], in_=ident_in, writes=[b_const])
    P.dma("pool", out=maskb[:], in_=bmask_in.rearrange("p (m k) -> p m k", k=128), parts=[b_const])
    P.dma("sp", out=ropet[:], in_=rope_in.rearrange("p (t k) -> p t k", k=288), parts=[b_const])

    def load_w(src2d, c0, ncols):
        i = wcnt[0] % nwb[0]
        wcnt[0] += 1
        P.dma("pool", out=wb[i][:, :, 0:ncols],
              in_=src2d[:, c0:c0 + ncols].rearrange("(j p) c -> p j c", p=128), writes=[bwb[i]])
        return wb[i], bwb[i]

    with ExitStack() as ph:
        cT = P.sb(ph, "cT", [128, 2, 16], F32)
        scT = P.sb(ph, "scT", [128, 2, 16], F32)
        wm = [P.sb(ph, "wm%d" % i, [128, 8, 512], F32) for i in range(2)]
        rowt = P.sb(ph, "rowt", [64, 128], F32)
        identf = P.sb(ph, "identf", [128, 128], F32)
        b_rowt = Buf("rowt")
        P.dma("sp", out=identf[:], in_=ident_in, parts=[b_const])
        modsb = P.sb(ph, "modsb", [2, 1536], F32)
        bmsb = P.sb(ph, "bmsb", [2, 1536], F32)
        b_cT, b_scT, b_modsb, b_bmsb, b_modloc, b_modall = (Buf(n) for n in ("cT", "scT", "modsb", "bmsb", "modloc", "modall"))
        bwm = [Buf("wm0"), Buf("wm1")]
        P.dma("sp", out=rowt[0:32, :], in_=cvec.rearrange("r (j p) -> (r j) p", p=128), writes=[b_rowt])
        P.op("pe", lambda e: e.transpose(out=psB[:, 0:32], in_=rowt[0:32, :], identity=identf[0:32, 0:32]),
             reads=[b_rowt, b_const], writes=[bB[0]])
        P.op("dve", lambda e: e.tensor_copy(out=cT[:].rearrange("p a b -> p (a b)"), in_=psB[:, 0:32]), reads=[bB[0]], writes=[b_cT])
        P.op("act", lambda e: e.activation(out=scT[:], in_=cT[:], func=AF.Silu), reads=[b_cT], writes=[b_scT])
        k = 0
        for l in range(4):
            P.dma("sp", out=bmsb[:], in_=bmod[l:l + 1, :].broadcast_to([2, 1536]),
                  writes=[b_bmsb])
            for pc in range(3):
                for hf in range(2):
                    i = k % 2
                    k += 1
                    P.dma("sp", out=wm[i][:],
                          in_=wmod[l, hf * 1024:(hf + 1) * 1024, pc * 512:(pc + 1) * 512].rearrange("(j p) c -> p j c", p=128),
                          writes=[bwm[i]])
                    for j in range(8):
                        first = hf == 0 and j == 0
                        last = hf == 1 and j == 7
                        P.op("pe", lambda e: e.matmul(psB[0:2, 0:512], lhsT=scT[:, :, hf * 8 + j], rhs=wm[i][:, j, :],
                                                      start=first, stop=last),
                             reads=[b_scT, bwm[i]], writes=[bB[0]] if first else (), parts=() if first else [bB[0]], sig=last)
                P.op("dve", lambda e: e.tensor_tensor(out=modsb[:, pc * 512:(pc + 1) * 512], in0=psB[0:2, 0:512],
                                                      in1=bmsb[:, pc * 512:(pc + 1) * 512], op=ALU.add),
                     reads=[bB[0], b_bmsb], parts=[b_modsb])
            P.dma("sp", out=modloc.ap()[l * 2:(l + 1) * 2, :], in_=modsb[:], reads=[b_modsb], parts=[b_modloc])
        allgather(modloc, modall, [b_modloc], [b_modall])

        def mod_pieces(col0, n):
            res = []
            off = 0
            while off < n:
                col = col0 + off
                rk, cc = divmod(col, 1536)
                ln = min(n - off, 1536 - cc)
                res.append((rk, cc, ln, off))
                off += ln
            return res

        for l in range(4):
            for row in range(2):
                for rk, cc, ln, off in mod_pieces(0, 4096):
                    P.dma("sp", out=rowt[off // 128:(off + ln) // 128, :],
                          in_=modall.ap()[rk * 8 + l * 2 + row, cc:cc + ln].rearrange("(j p) -> j p", p=128),
                          reads=[b_modall], parts=[b_rowt])
                P.op("pe", lambda e: e.transpose(out=psB[:, 0:32], in_=rowt[0:32, :], identity=identf[0:32, 0:32]),
                     reads=[b_rowt, b_const], writes=[bB[0]])
                P.op("dve", lambda e: e.tensor_copy(out=modv[:, l, row * 2:row * 2 + 2, :].rearrange("p a b -> p (a b)"), in_=psB[:, 0:32]),
                     reads=[bB[0]], parts=[b_modv])
        P.dma("sp", out=rowt[0:64, :], in_=norm_g.rearrange("l (j p) -> (l j) p", p=128), writes=[b_rowt])
        P.op("pe", lambda e: e.transpose(out=psB[:, 0:64], in_=rowt[0:64, :], identity=identf[0:64, 0:64]),
             reads=[b_rowt, b_const], writes=[bB[0]])
        for l in range(4):
            P.op("dve", lambda e: e.tensor_copy(out=modv[:, l, 4, :], in_=psB[:, l * 16:(l + 1) * 16]), reads=[bB[0]], parts=[b_modv])
        for idx in (1, 3):
            P.op("dve", lambda e: e.scalar_tensor_tensor(out=modv[:, :, idx, :], in0=modv[:, :, idx, :], scalar=1.0,
                                                         in1=modv[:, :, 4, :], op0=ALU.add, op1=ALU.mult),
                 reads=[b_modv], writes=[b_modv])

        def load_gate(l, gate_bc):
            for row in range(2):
                for rk, cc, ln, off in mod_pieces(4096, 2048):
                    P.dma("sp", out=gate_bc[:, row, off:off + ln],
                          in_=modall.ap()[rk * 8 + l * 2 + row:rk * 8 + l * 2 + row + 1, cc:cc + ln].broadcast_to([128, ln]),
                          reads=[b_modall], parts=[b_gate])
        P.barrier()

    if DEBUG_STOP == "P":
        dbg = nc.dram_tensor("dbg", [128, 320], F32, kind="ExternalOutput").ap()
        P.dma("sp", out=dbg, in_=modv[:].rearrange("p a b c -> p (a b c)"), reads=[b_modv])
        P.barrier()
        return P
    def mm_group(out_ap, bout, pairs, rd):
        n = len(pairs)
        for i, (lt, rh) in enumerate(pairs):
            P.op("pe", lambda e: e.matmul(out_ap, lhsT=lt, rhs=rh, start=(i == 0), stop=(i == n - 1)),
                 reads=rd, writes=[bout] if i == 0 else (), parts=() if i == 0 else [bout], sig=(i == n - 1))

    def transposes(src_fn, n, M, rd):
        i = tcnt[0] % 2
        tcnt[0] += 1
        for k in range(n):
            P.op("pe", lambda e: e.transpose(out=psT[:, i, k * 128:k * 128 + M], in_=src_fn(k), identity=identb[0:M, 0:M]),
                 reads=rd + [b_const], writes=[bT[i]] if k == 0 else (), parts=() if k == 0 else [bT[i]], sig=(k == n - 1))
        return psT[:, i, 0:n * 128].rearrange("p (h k) -> p h k", k=128)[:, :, 0:M], bT[i]

    epst = P.sb(st, "epst", [128, 1], F32)
    P.op("pool", lambda e: e.memset(epst[:], EPS), writes=[b_const])

    def rstd(out_ap, sum_ap, tmp_ap, n, buf):
        P.op("act", lambda e: e.activation(out=tmp_ap, in_=sum_ap, func=AF.Sqrt, bias=epst[0:out_ap.shape[0], 0:1], scale=1.0 / n),
             reads=[buf, b_const], writes=[buf])
        P.op("dve", lambda e: e.reciprocal(out=out_ap, in_=tmp_ap), reads=[buf], writes=[buf])

    for l in range(n_layers):
      try:
        even = (l % 2 == 0)
        li = l // 2
        with_ctx = l < n_layers - 1
        lam_init = 0.8 - 0.6 * float(np.exp(-0.3 * l))
        w_in = e_w[li] if even else o_w[li]
        NC_IN = 6656 if even else 4416
        GATE0 = NC_IN - 2048
        if even:
            KR, NVC, NQH = 1280, 1280, 24
        else:
            KR, NVC, NQH = 1408, 1280, 20
        R = KR + NVC
        bLB, bG = Buf("LB"), Buf("G")
        nkh_l = 10 if even else 11
        kchunks = [(c * 3, min(3, nkh_l - c * 3)) for c in range((nkh_l + 2) // 3)]
        LBK = [nc.dram_tensor("LBK%d_%d" % (l, c), [nh * 128, TOK], BF16) for c, (h0, nh) in enumerate(kchunks)]
        GK = [nc.dram_tensor("GK%d_%d" % (l, c), [4 * nh * 128, TOK], BF16) for c, (h0, nh) in enumerate(kchunks)]
        VCH = [(0, 256), (256, 256), (512, 256), (768, 320)]
        LBV = [nc.dram_tensor("LBV%d_%d" % (l, c), [n, NVC], BF16) for c, (t0, n) in enumerate(VCH)]
        GV = [nc.dram_tensor("GV%d_%d" % (l, c), [4 * n, NVC], BF16) for c, (t0, n) in enumerate(VCH)]

        def lbk(h):
            return LBK[h // 3].ap()[(h % 3) * 128:(h % 3) * 128 + 128, :]

        def gkr(h):
            return GK[h // 3].ap().rearrange("(s r) c -> r s c", s=4)[(h % 3) * 128:(h % 3) * 128 + 128]

        def vch(t):
            c = min(t // 2, 3)
            return c, t * 128 - VCH[c][0]

        def lbv(t, rows):
            c, r0 = vch(t)
            return LBV[c].ap()[r0:r0 + rows, :]

        def gv(s, t, rows):
            c, r0 = vch(t)
            n = VCH[c][1]
            return GV[c].ap()[s * n + r0:s * n + r0 + rows, :]

        with ExitStack() as ls:
            QT = P.sb(ls, "QT", [128, NQH, TOK], BF16)
            bQT, byT = Buf("QT"), Buf("yT")
            if even:
                P.op("pool", lambda e: e.memset(QT[:], 0.0), writes=[bQT])
            if even:
                P.dma("sp", out=lp[:, 0:256], in_=a_lam[li:li + 1, :].broadcast_to([128, 256]), parts=[b_lp])
                P.dma("sp", out=lp[:, 256:384], in_=a_sub_g[li:li + 1, :].broadcast_to([128, 128]), parts=[b_lp])
                P.dma("sp", out=lp[:, 384:392], in_=b_sink[li:li + 1, :].broadcast_to([128, 8]), parts=[b_lp])
                P.op("dve", lambda e: e.tensor_tensor(out=lp[:, 512:576], in0=lp[:, 0:64], in1=lp[:, 64:128], op=ALU.mult),
                     reads=[b_lp], parts=[b_lp])
                P.op("dve", lambda e: e.tensor_tensor(out=lp[:, 576:640], in0=lp[:, 128:192], in1=lp[:, 192:256], op=ALU.mult),
                     reads=[b_lp], parts=[b_lp])
                P.op("dve", lambda e: e.reduce_sum(out=sm[:, 1:3], in_=lp[:, 512:640].rearrange("p (a b) -> p a b", b=64), axis=AX.X),
                     reads=[b_lp], writes=[b_sm])
                P.op("act", lambda e: e.activation(out=sm[:, 3:5], in_=sm[:, 1:3], func=AF.Exp), reads=[b_sm], writes=[b_sm])
                P.op("act", lambda e: e.activation(out=sm[:, 8:16], in_=lp[:, 384:392], func=AF.Exp), reads=[b_lp, b_sm], writes=[b_sm])
                P.op("dve", lambda e: e.scalar_tensor_tensor(out=sm[:, 0:1], in0=sm[:, 4:5], scalar=-lam_init, in1=sm[:, 3:4],
                                                             op0=ALU.add, op1=ALU.subtract), reads=[b_sm], writes=[b_sm])
                P.op("dve", lambda e: e.tensor_scalar(out=lp[:, 640:768], in0=lp[:, 256:384], scalar1=(1.0 - lam_init), scalar2=None,
                                                      op0=ALU.mult), reads=[b_lp], writes=[b_lp])
            else:
                P.dma("sp", out=lp[:, 0:128], in_=c_q_g[li:li + 1, :].broadcast_to([128, 128]), parts=[b_lp])
                P.dma("sp", out=lp[:, 128:256], in_=c_k_g[li:li + 1, :].broadcast_to([128, 128]), parts=[b_lp])
                P.dma("sp", out=lp[:, 256:768], in_=d_q_a_g[li:li + 1, :].broadcast_to([128, 512]), parts=[b_lp])
                P.dma("sp", out=lp[:, 768:1024], in_=d_kv_a_g[li:li + 1, :].broadcast_to([128, 256]), parts=[b_lp])

            if DEBUG_STOP == "L":
                dbg = nc.dram_tensor("dbg", [128, 64], F32, kind="ExternalOutput").ap()
                P.dma("sp", out=dbg, in_=sm[:], reads=[b_sm, b_lp])
                P.barrier()
                break
            wst = ExitStack()
            nwb[0] = 3 if even else 2
            wcnt[0] = 0
            for i_ in range(nwb[0]):
                wb[i_] = P.sb(wst, "wb%d" % i_, [128, 16, 512], BF16)
            first_c0 = 4096 if even else 1024
            pre_w = load_w(w_in, first_c0, 512)
            with ExitStack() as ph:
                xt = [P.sb(ph, "xt%d" % i, [128, DM], F32) for i in range(2)]
                xn = [P.sb(ph, "xn%d" % i, [128, DM], BF16) for i in range(2)]
                bxt = [Buf("xt0"), Buf("xt1")]
                bxn = [Buf("xn0"), Buf("xn1")]
                ss = P.sb(ph, "ss", [128, 4], F32)
                bss = Buf("ss")
                def nstats(t):
                    i = t % 2
                    so = 4 * i
                    rows = 128 if t < 8 else 64
                    if l == 0:
                        src = x_in[t * 128:(t + 1) * 128, :] if t < 8 else ctx_in[:, :]
                        rd = []
                    else:
                        src = xs[t * 128:t * 128 + rows, :]
                        rd = bxs[t]
                    P.dma("sp", out=xt[i][0:rows, :], in_=src, reads=rd, writes=[bxt[i]])
                    if t == 8:
                        P.op("pool", lambda e: e.memset(xt[i][64:128, :], 0.0), parts=[bxt[i]])
                    P.op("dve", lambda e: e.tensor_tensor(out=xn[i][:], in0=xt[i][:], in1=xt[i][:], op=ALU.mult),
                         reads=[bxt[i]], writes=[bxn[i]])
                    P.op("dve", lambda e: e.reduce_sum(out=ss8[:, so:so + 1], in_=xn[i][:], axis=AX.X), reads=[bxn[i]], writes=[bss8[i]])
                    rstd(ss8[:, so + 2:so + 3], ss8[:, so:so + 1], ss8[:, so + 1:so + 2], DM, bss8[i])
                    P.op("dve", lambda e: e.tensor_scalar(out=xn[i][:], in0=xt[i][:], scalar1=ss8[:, so + 2:so + 3], scalar2=None,
                                                          op0=ALU.mult), reads=[bxt[i], bss8[i]], writes=[bxn[i]])

                def nevac(t):
                    i = t % 2
                    mrow = 0 if t < 8 else 2
                    for half in range(2):
                        pst, bps = transposes(lambda k: xn[i][:, (half * 8 + k) * 128:(half * 8 + k + 1) * 128], 8, 128, [bxn[i]])
                        for k in range(8):
                            j = half * 8 + k
                            P.op("act", lambda e: e.activation(out=hT[:, t, j, :], in_=pst[:, k, :], func=AF.Identity,
                                                               bias=modv[:, l, mrow, j:j + 1], scale=modv[:, l, mrow + 1, j:j + 1]),
                                 reads=[bps, b_modv], parts=[bhT[t]])

                ss8 = P.sb(ph, "ss8", [128, 8], F32)
                bss8 = [Buf("ss8a"), Buf("ss8b")]
                nstats(0)
                for t in range(NTILE):
                    if t + 1 < NTILE:
                        nstats(t + 1)
                    nevac(t)
            P.barrier()
            if DEBUG_STOP == "N":
                dbg = nc.dram_tensor("dbg", [128, 64], F32, kind="ExternalOutput").ap()
                P.dma("sp", out=dbg, in_=sm[:], reads=[b_sm, b_lp])
                P.barrier()
                break

            with ExitStack() as pa:
                KTst = P.sb(pa, "KTst", [128, 11, TOK], BF16)
                bKT = [Buf("KT%d" % h) for h in range(11)]
                u32 = P.sb(pa, "u32", [128, 2048], F32)
                ta = P.sb(pa, "ta", [128, 512], F32)
                tb = P.sb(pa, "tb", [128, 512], F32)
                tn = P.sb(pa, "tn", [128, 512], F32)
                o16 = P.sb(pa, "o16", [128, 1024], BF16)
                v16 = P.sb(pa, "v16", [128, 1024], BF16)
                st8 = P.sb(pa, "st8", [128, 16], F32)
                bu, bta, btb, btn, bo16, bv16, bst8 = (Buf(n) for n in ("u32", "ta", "tb", "tn", "o16", "v16", "st8"))
                if not even:
                    wqb = P.sb(pa, "wqb", [128, 4, 1536], BF16)
                    wkvb = P.sb(pa, "wkvb", [128, 2, 2048], BF16)
                    cT16 = P.sb(pa, "cT16", [128, 4, 128], BF16)
                    bwq, bwkv, bcT16 = Buf("wqb"), Buf("wkvb"), Buf("cT16")
                    P.dma("pool", out=wqb[:], in_=d_w_q_b[li].rearrange("(j p) c -> p j c", p=128), writes=[bwq])
                    P.dma("pool", out=wkvb[:], in_=d_w_kv_b[li].rearrange("(j p) c -> p j c", p=128), writes=[bwkv])

                def rope(src, ncols, G, Dh, t, dst, bsrc, bdst):
                    h = Dh // 2
                    c0, s0, n0 = (0, 96, 192) if Dh == 64 else (32, 128, 224)
                    pat = "p (g two h) -> p g two h"
                    s4 = src[:, 0:ncols].rearrange(pat, two=2, h=h)
                    a4 = ta[:, 0:ncols].rearrange(pat, two=2, h=h)
                    b4 = tb[:, 0:ncols].rearrange(pat, two=2, h=h)
                    cosb = ropet[:, t, c0:c0 + h].unsqueeze(1).unsqueeze(1).broadcast_to([128, G, 2, h])
                    sinb = ropet[:, t, s0:s0 + h].unsqueeze(1).broadcast_to([128, G, h])
                    nsinb = ropet[:, t, n0:n0 + h].unsqueeze(1).broadcast_to([128, G, h])
                    P.op("dve" if even else "pool", lambda e: e.tensor_tensor(out=a4, in0=s4, in1=cosb, op=ALU.mult), reads=[bsrc, b_const], writes=[bta])
                    P.op("dve", lambda e: e.tensor_tensor(out=b4[:, :, 0, :], in0=s4[:, :, 1, :], in1=nsinb, op=ALU.mult),
                         reads=[bsrc, b_const], writes=[btb])
                    P.op("dve", lambda e: e.tensor_tensor(out=b4[:, :, 1, :], in0=s4[:, :, 0, :], in1=sinb, op=ALU.mult),
                         reads=[bsrc, b_const], parts=[btb])
                    P.op("dve", lambda e: e.tensor_tensor(out=dst[:, 0:ncols], in0=ta[:, 0:ncols], in1=tb[:, 0:ncols], op=ALU.add),
                         reads=[bta, btb], writes=[bdst])

                def rmsn(src, ncols, G, Dh, gain, bsrc):
                    s3 = src[:, 0:ncols].rearrange("p (g d) -> p g d", d=Dh)
                    t3 = tn[:, 0:ncols].rearrange("p (g d) -> p g d", d=Dh)
                    P.op("pool", lambda e: e.tensor_tensor(out=ta[:, 0:ncols], in0=src[:, 0:ncols], in1=src[:, 0:ncols], op=ALU.mult),
                         reads=[bsrc], writes=[bta])
                    P.op("dve", lambda e: e.reduce_sum(out=st8[:, 0:G], in_=ta[:, 0:ncols].rearrange("p (g d) -> p g d", d=Dh), axis=AX.X),
                         reads=[bta], writes=[bst8])
                    P.op("act", lambda e: e.activation(out=st8[:, 8:8 + G], in_=st8[:, 0:G], func=AF.Sqrt, bias=epst[:, 0:1], scale=1.0 / Dh),
                         reads=[bst8, b_const], writes=[bst8])
                    P.op("dve", lambda e: e.reciprocal(out=st8[:, 0:G], in_=st8[:, 8:8 + G]), reads=[bst8], writes=[bst8])
                    P.op("dve", lambda e: e.tensor_tensor(out=t3, in0=s3, in1=st8[:, 0:G].unsqueeze(2).broadcast_to([128, G, Dh]), op=ALU.mult),
                         reads=[bsrc, bst8], writes=[btn])
                    P.op("pool", lambda e: e.tensor_tensor(out=t3, in0=t3, in1=gain.unsqueeze(1).broadcast_to([128, G, Dh]), op=ALU.mult),
                         reads=[btn, b_lp], writes=[btn])

                def to_fm(src16, n, t, dstT, h0, bsrc, bdst_list, split=False):
                    ntk = 128 if t < 8 else 64
                    pst, bps = transposes(lambda k: src16[:, k * 128:(k + 1) * 128], n, 128, [bsrc])
                    if split:
                        P.op("act", lambda e: e.copy(out=dstT[0:64, h0:h0 + n, t * 128:t * 128 + ntk], in_=pst[0:64, :, 0:ntk]),
                             reads=[bps], parts=bdst_list)
                        P.op("act", lambda e: e.copy(out=dstT[64:128, 16 + h0:16 + h0 + n, t * 128:t * 128 + ntk], in_=pst[64:128, :, 0:ntk]),
                             reads=[bps], parts=bdst_list)
                    else:
                        P.op("act", lambda e: e.copy(out=dstT[:, h0:h0 + n, t * 128:t * 128 + ntk], in_=pst[:, :, 0:ntk]),
                             reads=[bps], parts=bdst_list)

                def qk_post(ps_ap, bps_, ncols, G, Dh, t, gain, dstT, h0, bdst_list, do_rope=True, split=False):
                    P.op("act", lambda e: e.copy(out=u32[:, 0:ncols], in_=ps_ap), reads=[bps_], writes=[bu])
                    src, bsrc = u32, bu
                    if gain is not None:
                        rmsn(u32, ncols, G, Dh, gain, bu)
                        src, bsrc = tn, btn
                    if do_rope and t < 8:
                        rope(src, ncols, G, Dh, t, o16, bsrc, bo16)
                    else:
                        P.op("dve", lambda e: e.tensor_copy(out=o16[:, 0:ncols], in_=src[:, 0:ncols]), reads=[bsrc], writes=[bo16])
                    to_fm(o16, ncols // 128, t, dstT, h0, bo16, bdst_list, split=split)

                def v_post(ps_ap, bps_, ncols, t, vcol0):
                    rows = 128 if t < 8 else 64
                    P.op("act", lambda e: e.copy(out=v16[:, 0:ncols], in_=ps_ap), reads=[bps_], writes=[bv16])
                    P.dma("sp", out=lbv(t, rows)[:, vcol0:vcol0 + ncols], in_=v16[0:rows, 0:ncols], reads=[bv16], parts=[bLB])

                acc = [0]

                jobs = []

                def inproj(c0, ncols, consume):
                    jobs.append((c0, ncols, consume))

                def flush():
                    wj = [k for k, jb in enumerate(jobs) if jb[0] != "call"]
                    assert jobs[wj[0]][0] == first_c0 and jobs[wj[0]][1] == 512
                    loaded = {wj[0]: pre_w}
                    depth = nwb[0] - 1
                    for k, jb in enumerate(jobs):
                        if jb[0] == "call":
                            jb[1]()
                            continue
                        c0, ncols, consume = jb
                        pos = wj.index(k)
                        for ahead in range(1, depth + 1):
                            if pos + ahead < len(wj):
                                kn = wj[pos + ahead]
                                if kn not in loaded:
                                    loaded[kn] = load_w(w_in, jobs[kn][0], jobs[kn][1])
                        wt, bw = loaded[k]
                        prev = None
                        for t in range(NTILE):
                            i = acc[0] % 2
                            acc[0] += 1
                            bank = psB[:, i * 512:i * 512 + ncols]
                            mm_group(bank, bB[i], [(hT[:, t, j, :], wt[:, j, 0:ncols]) for j in range(16)], [bhT[t], bw])
                            if prev is not None:
                                consume(*prev)
                            prev = (t, bank, bB[i])
                        consume(*prev)
                    del jobs[:]

                def xch(heads, kch, vch):
                    for h in heads:
                        P.dma("sp", out=lbk(h), in_=KTst[:, h, :], reads=[bKT[h]], parts=[bLB])
                    for c in kch:
                        allgather(LBK[c], GK[c], [bLB], [bG])
                    for c in vch:
                        allgather(LBV[c], GV[c], [bLB], [bG])

                if even:
                    def bkv(t, bk, bb):
                        qk_post(bk[:, 0:256], bb, 256, 2, 128, t, None, KTst, 8, bKT[8:10])
                        v_post(bk[:, 256:512], bb, 256, t, 1024)
                    inproj(4096, 512, bkv)
                    for hb in range(2):
                        inproj(2048 + hb * 512, 512, lambda t, bk, bb, hb=hb: v_post(bk, bb, 512, t, hb * 512))
                    jobs.append(("call", lambda: xch([9], [3], [0, 1])))
                    inproj(1024, 512, lambda t, bk, bb: qk_post(bk, bb, 512, 8, 64, t, None, KTst, 0, bKT[0:4]))
                    jobs.append(("call", lambda: xch([0, 1, 2], [0], [2, 3])))
                    inproj(1536, 512, lambda t, bk, bb: qk_post(bk, bb, 512, 8, 64, t, None, KTst, 4, bKT[4:8]))
                    jobs.append(("call", lambda: xch([3, 4, 5, 6, 7, 8], [1, 2], [])))
                    nkh = 10
                else:
                    def ckv(t, bk, bb):
                        qk_post(bk[:, 0:256], bb, 256, 2, 128, t, lp[:, 128:256], KTst, 0, bKT[0:2])
                        v_post(bk[:, 256:512], bb, 256, t, 0)
                    inproj(1024, 512, ckv)

                    def dkv(t, bk, bb):
                        P.op("act", lambda e: e.copy(out=u32[:, 0:320], in_=bk[:, 0:320]), reads=[bb], writes=[bu])
                        rmsn(u32, 256, 1, 256, lp[:, 768:1024], bu)
                        P.op("dve", lambda e: e.tensor_copy(out=o16[:, 0:256], in_=tn[:, 0:256]), reads=[btn], writes=[bo16])
                        pst, bps = transposes(lambda k: o16[:, k * 128:(k + 1) * 128], 2, 128, [bo16])
                        P.op("act", lambda e: e.copy(out=cT16[:, 0:2, :], in_=pst), reads=[bps], writes=[bcT16])
                        for nb in range(4):
                            mm_group(psA[:, nb * 512:(nb + 1) * 512], bA[nb],
                                     [(cT16[:, kc, :], wkvb[:, kc, nb * 512:(nb + 1) * 512]) for kc in range(2)], [bcT16, bwkv])
                        if t < 8:
                            rope(u32[:, 256:320], 64, 1, 64, t, o16[:, 0:64], bu, bo16)
                        else:
                            P.op("dve", lambda e: e.tensor_copy(out=o16[:, 0:64], in_=u32[:, 256:320]), reads=[bu], writes=[bo16])
                        P.op("dve", lambda e: e.tensor_copy(out=o16[:, 64:128], in_=o16[:, 0:64]), reads=[bo16], parts=[bo16])
                        to_fm(o16, 1, t, KTst, 10, bo16, [bKT[10]])
                        for nb in range(4):
                            kv4 = psA[:, nb * 512:(nb + 1) * 512].rearrange("p (h two d) -> p h two d", two=2, d=128)
                            P.op("act", lambda e: e.copy(out=o16[:, nb * 256:(nb + 1) * 256].rearrange("p (h d) -> p h d", d=128), in_=kv4[:, :, 0, :]),
                                 reads=[bA[nb]], writes=[bo16] if nb == 0 else (), parts=() if nb == 0 else [bo16])
                            P.op("dve", lambda e: e.tensor_copy(out=v16[:, nb * 256:(nb + 1) * 256].rearrange("p (h d) -> p h d", d=128), in_=kv4[:, :, 1, :]),
                                 reads=[bA[nb]], writes=[bv16] if nb == 0 else (), parts=() if nb == 0 else [bv16])
                        to_fm(o16, 8, t, KTst, 2, bo16, bKT[2:10])
                        rows = 128 if t < 8 else 64
                        P.dma("sp", out=lbv(t, rows)[:, 256:1280], in_=v16[0:rows, 0:1024], reads=[bv16], parts=[bLB])
                    inproj(2048, 320, dkv)
                    nkh = 11
                if not even:
                    jobs.append(("call", lambda: xch(list(range(11)), [0, 1, 2], [])))

                if even:
                    for hb in range(2):
                        inproj(hb * 512, 512, lambda t, bk, bb, hb=hb: qk_post(bk, bb, 512, 8, 64, t, None, QT, hb * 4, [bQT], split=True))
                    for hb in range(2):
                        inproj(3072 + hb * 512, 512, lambda t, bk, bb, hb=hb: qk_post(bk, bb, 512, 4, 128, t, None, QT, 8 + hb * 4, [bQT]))
                else:
                    for hb in range(2):
                        inproj(hb * 512, 512, lambda t, bk, bb, hb=hb: qk_post(bk, bb, 512, 4, 128, t, lp[:, 0:128], QT, hb * 4, [bQT]))
                        jobs.append(("call", (lambda: xch([], [3], [0, 1])) if hb == 0 else (lambda: xch([], [], [2, 3]))))

                    def dq(t, bk, bb):
                        P.op("act", lambda e: e.copy(out=u32[:, 0:512], in_=bk), reads=[bb], writes=[bu])
                        rmsn(u32, 512, 1, 512, lp[:, 256:768], bu)
                        P.op("dve", lambda e: e.tensor_copy(out=o16[:, 0:512], in_=tn[:, 0:512]), reads=[btn], writes=[bo16])
                        pst, bps = transposes(lambda k: o16[:, k * 128:(k + 1) * 128], 4, 128, [bo16])
                        P.op("act", lambda e: e.copy(out=cT16[:], in_=pst), reads=[bps], writes=[bcT16])
                        for nb in range(3):
                            mm_group(psA[:, nb * 512:(nb + 1) * 512], bA[nb],
                                     [(cT16[:, kc, :], wqb[:, kc, nb * 512:(nb + 1) * 512]) for kc in range(4)], [bcT16, bwq])
                        for nb in range(3):
                            P.op("act", lambda e: e.copy(out=u32[:, nb * 512:(nb + 1) * 512], in_=psA[:, nb * 512:(nb + 1) * 512]),
                                 reads=[bA[nb]], writes=[bu] if nb == 0 else (), parts=() if nb == 0 else [bu])
                        q3 = u32[:, 0:1536].rearrange("p (h d) -> p h d", d=192)
                        P.op("dve", lambda e: e.tensor_copy(out=o16[:, 0:1024].rearrange("p (h d) -> p h d", d=128), in_=q3[:, :, 0:128]),
                             reads=[bu], writes=[bo16])
                        P.op("pool", lambda e: e.tensor_copy(out=tn[:, 0:512].rearrange("p (h d) -> p h d", d=64), in_=q3[:, :, 128:192]),
                             reads=[bu], writes=[btn])
                        to_fm(o16, 8, t, QT, 8, bo16, [bQT])
                        if t < 8:
                            rope(tn, 512, 8, 64, t, v16, btn, bv16)
                        else:
                            P.op("dve", lambda e: e.tensor_copy(out=v16[:, 0:512], in_=tn[:, 0:512]), reads=[btn], writes=[bv16])
                        to_fm(v16, 4, t, QT, 16, bv16, [bQT])
                    inproj(1536, 512, dq)
                flush()
            wst.close()
            P.barrier()
            if DEBUG_STOP == "A":
                dbg = nc.dram_tensor("dbg", [128, 64], F32, kind="ExternalOutput").ap()
                P.dma("sp", out=dbg, in_=sm[:], reads=[b_sm, b_lp])
                P.barrier()
                break

            pbc = ExitStack()
            yT = P.sb(pbc, "yT", [128, 16, TOK], BF16)
            with ExitStack() as pb:
                KT = [P.sb(pb, "KT", [128, NKEY], BF16) for _ in range(2)]
                Vt = [P.sb(pb, "Vt", [128, NKT, 130], BF16) for _ in range(2)]
                PT = [P.sb(pb, "PT", [128, 2, 512], BF16) for _ in range(2)]
                osb = P.sb(pb, "osb", [128, 1, 4, 128], F32)
                y16 = P.sb(pb, "y16", [128, 4, 128], BF16)
                Pacc = P.sb(pb, "Pacc", [128, 2, 512], F32)
                rcs = P.sb(pb, "rcs", [128, 512], F32)
                onesf = P.sb(pb, "onesf", [128, 128], F32)
                bPacc, brcs, bones = [Buf("Pacc0"), Buf("Pacc1")], Buf("rcs"), Buf("onesf")
                PT4 = [P.sb(pb, "PT4", [128, 512], BF16) for _ in range(4)]
                pair16 = [P.sb(pb, "pair16", [128, 512], BF16) for _ in range(2)]
                bpair = [Buf("pair0"), Buf("pair1")]
                bPT4 = [Buf("PT4_%d" % i_) for i_ in range(4)]
                P.op("pool", lambda e: e.memset(onesf[:], 1.0), writes=[bones])
                if even:
                    oT = P.sb(pb, "oT", [128, 2, 512], F32)
                    dT = P.sb(pb, "dT", [128, 512], F32)
                    sqT = P.sb(pb, "sqT", [128, 512], F32)
                    subgc = P.sb(pb, "subgc", [128, 1], F32)
                    boT, bdT, bsqT, bsubg = Buf("oT"), Buf("dT"), Buf("sqT"), Buf("subgc")
                    P.dma("sp", out=subgc[:], in_=a_sub_g[li, :].rearrange("(p o) -> p o", o=1), writes=[bsubg])
                    P.op("dve", lambda e: e.tensor_scalar(out=subgc[:], in0=subgc[:], scalar1=(1.0 - lam_init), scalar2=None, op0=ALU.mult),
                         reads=[bsubg], writes=[bsubg])
                rc = P.sb(pb, "rc", [128, 16], F32)
                bK, bV, bPT = [Buf("K0"), Buf("K1")], [Buf("V0"), Buf("V1")], [Buf("PT0"), Buf("PT1")]
                bo, by16, brc, bkpe = (Buf(n) for n in ("osb", "y16", "rc", "kpe"))
                by16b = [Buf("y16b0"), Buf("y16b1")]
                for i in range(2):
                    P.op("pool", lambda e: e.memset(Vt[i][:, :, 128:130], 1.0), writes=[bV[i]])
                kvn = [0]
                pcn = [0]
                ocnt = [0]

                def load_k(dst, krow0, bdst):
                    P.dma("sp", out=dst[:, 0:4096].rearrange("p (s c) -> p s c", s=4), in_=gkr(krow0 // 128)[:, :, 0:1024],
                          reads=[bG], parts=[bdst])
                    P.dma("sp", out=dst[:, 4096:4352].rearrange("p (s c) -> p s c", s=4), in_=gkr(krow0 // 128)[:, :, 1024:1088],
                          reads=[bG], parts=[bdst])

                def load_kv(krow0, vcol0):
                    i = kvn[0] % 2
                    kvn[0] += 1
                    load_k(KT[i], krow0, bK[i])
                    for s_ in range(4):
                        for c in range(4):
                            P.dma("sp", out=Vt[i][:, s_ * 8 + 2 * c:s_ * 8 + 2 * c + 2, 0:128],
                                  in_=gv(s_, 2 * c, 256)[:, vcol0:vcol0 + 128].rearrange("(k p) c -> p k c", p=128), reads=[bG], parts=[bV[i]])
                        P.dma("sp", out=Vt[i][(s_ % 2) * 64:(s_ % 2) * 64 + 64, 32 + s_ // 2, 0:128],
                              in_=gv(s_, 8, 64)[:, vcol0:vcol0 + 128], reads=[bG], parts=[bV[i]])
                    return i

                def attend(pieces, rdK, Vi, kts, nq, sc, oi, sink_ap=None, masks=None, tilew=512):
                    nqs = max(1, nq // 128)
                    M = min(nq, 128)
                    gsz = 2 if tilew == 512 else 8
                    groups = [kts[a:a + gsz] for a in range(0, len(kts), gsz)]
                    ng = len(groups)
                    pbase = pcn[0]
                    pcn[0] += ng

                    def pv(gi, pview):
                        grp = groups[gi]
                        n = len(grp)
                        pk = (pbase + gi) % 2
                        for ki, kt in enumerate(grp):
                            for qs in range(nqs):
                                first = gi == 0 and ki == 0
                                last = gi == ng - 1 and ki == n - 1
                                ob = qs // 2
                                col = ob * 512 + (qs % 2) * 129
                                st_ = first and qs % 2 == 0
                                P.op("pe", lambda e: e.matmul(psB[0:M, col:col + 129], lhsT=pview(pk, ki, qs), rhs=Vt[Vi][:, kt, 0:129],
                                                              start=st_, stop=last, skip_group_check=True),
                                     reads=[bPT[pk], bV[Vi]], writes=[bB[ob]] if st_ else (), parts=() if st_ else [bB[ob]], sig=last)

                    if tilew == 512:
                        def qk(gi):
                            sg = gi % 2
                            for ki, kt in enumerate(groups[gi]):
                                bi = 2 * sg + ki
                                mm_group(psA[0:128, bi * 512:bi * 512 + nq], bA[bi], [(kf(kt), qa) for kf, qa in pieces], rdK)

                        def ex(gi):
                            sg = gi % 2
                            n = len(groups[gi])
                            pk = (pbase + gi) % 2
                            P.op("act", lambda e: e.activation(out=PT[pk][:, 0:n, 0:nq],
                                                               in_=psA[:, 2 * sg * 512:(2 * sg + n) * 512].rearrange("p (k c) -> p k c", c=512)[:, :, 0:nq],
                                                               func=AF.Exp, scale=sc),
                                 reads=[bA[2 * sg + a] for a in range(n)], writes=[bPT[pk]])

                        pview5 = lambda pk, ki, qs: PT[pk][:, ki, qs * 128:qs * 128 + M]
                        qk(0)
                        for gi in range(ng):
                            if gi + 1 < ng:
                                qk(gi + 1)
                            ex(gi)
                            pv(gi, pview5)
                    else:
                        for gi, grp in enumerate(groups):
                            n = len(grp)
                            pk = (pbase + gi) % 2
                            for ki, kt in enumerate(grp):
                                bi = ki // 4
                                mm_group(psA[0:128, ki * 128:ki * 128 + nq], bA[bi], [(kf(kt), qa) for kf, qa in pieces], rdK)
                            pflat = PT[pk][:].rearrange("p a b -> p (a b)")
                            P.op("act", lambda e: e.activation(out=pflat[:, 0:n * 128].rearrange("p (k c) -> p k c", c=128)[:, :, 0:nq],
                                                               in_=psA[:, 0:n * 128].rearrange("p (k c) -> p k c", c=128)[:, :, 0:nq],
                                                               func=AF.Exp, scale=sc),
                                 reads=[bA[0], bA[1]], writes=[bPT[pk]])
                            if masks is not None:
                                for ki, kt in enumerate(grp):
                                    if masks[ki] is not None:
                                        P.op("dve", lambda e: e.tensor_tensor(out=pflat[:, ki * 128:ki * 128 + nq], in0=pflat[:, ki * 128:ki * 128 + nq],
                                                                              in1=maskb[:, masks[ki], 0:nq], op=ALU.mult),
                                             reads=[bPT[pk], b_const], writes=[bPT[pk]])
                            pv(gi, lambda pk_, ki, qs: PT[pk_][:].rearrange("p a b -> p (a b)")[:, ki * 128:ki * 128 + M])
                    for qs in range(nqs):
                        ob = qs // 2
                        col = ob * 512 + (qs % 2) * 129
                        if sink_ap is None:
                            P.op("dve", lambda e: e.reciprocal(out=rc[0:M, qs:qs + 1], in_=psB[0:M, col + 128:col + 129]), reads=[bB[ob]], writes=[brc])
                        else:
                            P.op("dve", lambda e: e.tensor_tensor(out=rc[0:M, 8 + qs:9 + qs], in0=psB[0:M, col + 128:col + 129], in1=sink_ap, op=ALU.add),
                                 reads=[bB[ob], b_sm], writes=[brc])
                            P.op("dve", lambda e: e.reciprocal(out=rc[0:M, qs:qs + 1], in_=rc[0:M, 8 + qs:9 + qs]), reads=[brc], writes=[brc])
                        P.op("dve", lambda e: e.tensor_scalar(out=osb[0:M, oi, qs, :], in0=psB[0:M, col:col + 128], scalar1=rc[0:M, qs:qs + 1],
                                                              scalar2=None, op0=ALU.mult), reads=[bB[ob], brc], parts=[bo])

                def emit_y(src, bsrc, chunk, q0, nq):
                    nqs = max(1, nq // 128)
                    M = min(nq, 128)
                    P.op("pool", lambda e: e.tensor_copy(out=y16[0:M, 0:nqs, :], in_=src[0:M, 0:nqs, :]), reads=[bsrc], writes=[by16])
                    pst, bps = transposes(lambda k: y16[0:M, k, :], nqs, M, [by16])
                    P.op("act", lambda e: e.copy(out=yT[:, chunk, q0:q0 + nq].rearrange("p (k m) -> p k m", m=M), in_=pst), reads=[bps], parts=[byT])

                def attendT(pieces, rdK, Vi, kts, nq, sc, fin):
                    nk = len(kts)
                    pbase = pcn[0]
                    pcn[0] += nk

                    ob = ocnt[0] % 2
                    ocnt[0] += 1
                    o_ap, o_buf = psB[:, ob * 512:ob * 512 + nq], bB[ob]

                    def qk(k):
                        bi = k % 3
                        mm_group(psA[0:128, bi * 512:bi * 512 + nq], bA[bi], [(kf(kts[k]), qa) for kf, qa in pieces], rdK)

                    for k in range(min(2, nk)):
                        qk(k)
                    for k in range(nk):
                        if k + 2 < nk:
                            qk(k + 2)
                        bi = k % 3
                        pk = (pbase + k) % 4
                        P.op("act", lambda e: e.activation(out=PT4[pk][:, 0:nq], in_=psA[:, bi * 512:bi * 512 + nq], func=AF.Exp, scale=sc),
                             reads=[bA[bi]], writes=[bPT4[pk]])
                        first = k == 0
                        last = k == nk - 1
                        P.op("pe", lambda e: e.matmul(o_ap, lhsT=Vt[Vi][:, kts[k], 0:128], rhs=PT4[pk][:, 0:nq], start=first, stop=last),
                             reads=[bPT4[pk], bV[Vi]], writes=[o_buf] if first else (), parts=() if first else [o_buf], sig=last)
                        if k % 2 == 1:
                            pkp = (pbase + k - 1) % 4
                            j2 = (k // 2) % 2
                            P.op("dve", lambda e: e.tensor_tensor(out=pair16[j2][:, 0:nq], in0=PT4[pkp][:, 0:nq], in1=PT4[pk][:, 0:nq], op=ALU.add),
                                 reads=[bPT4[pkp], bPT4[pk]], writes=[bpair[j2]])
                            if k == 1:
                                P.op("dve", lambda e: e.tensor_copy(out=Pacc[:, 0, 0:nq], in_=pair16[j2][:, 0:nq]), reads=[bpair[j2]], writes=[bPacc[0]])
                            else:
                                P.op("dve", lambda e: e.tensor_tensor(out=Pacc[:, 0, 0:nq], in0=Pacc[:, 0, 0:nq], in1=pair16[j2][:, 0:nq], op=ALU.add),
                                     reads=[bpair[j2], bPacc[0]], writes=[bPacc[0]])
                        elif k == nk - 1:
                            if k == 0:
                                P.op("dve", lambda e: e.tensor_copy(out=Pacc[:, 0, 0:nq], in_=PT4[pk][:, 0:nq]), reads=[bPT4[pk]], writes=[bPacc[0]])
                            else:
                                P.op("dve", lambda e: e.tensor_tensor(out=Pacc[:, 0, 0:nq], in0=Pacc[:, 0, 0:nq], in1=PT4[pk][:, 0:nq], op=ALU.add),
                                     reads=[bPT4[pk], bPacc[0]], writes=[bPacc[0]])
                    mm_group(psA[:, 1536:1536 + nq], bA[3], [(onesf[:], Pacc[:, 0, 0:nq])], [bones] + bPacc)
                    P.op("dve", lambda e: e.reciprocal(out=rcs[:, 0:nq], in_=psA[:, 1536:1536 + nq]), reads=[bA[3]], writes=[brcs])
                    fin(o_ap, o_buf)

                ALLK = list(range(NKT))
                qblocks = [(0, 512, ALLK), (512, 512, ALLK)] + ([(1024, 64, [32, 33])] if with_ctx else [])
                if even:
                    sc = 64 ** -0.5
                    for h in range(8):
                        Ki = load_kv(h * 128, h * 128)
                        for q0, nq, kts in qblocks:
                            nqs = max(1, nq // 128)
                            M = min(nq, 128)
                            for pr in range(2):
                                attendT([(lambda kt: KT[Ki][:, kt * 128:(kt + 1) * 128], QT[:, h + 16 * pr, q0:q0 + nq])],
                                        [bK[Ki], bQT], Ki, kts, nq, sc,
                                        lambda o_ap, o_buf, pr=pr: P.op("dve", lambda e: e.tensor_tensor(out=oT[:, pr, 0:nq], in0=o_ap, in1=rcs[:, 0:nq], op=ALU.mult),
                                                                           reads=[o_buf, brcs], parts=[boT]))
                            P.op("dve", lambda e: e.scalar_tensor_tensor(out=dT[:, 0:nq], in0=oT[:, 1, 0:nq], scalar=sm[:, 0:1], in1=oT[:, 0, 0:nq],
                                                                         op0=ALU.mult, op1=ALU.add), reads=[boT, b_sm], writes=[bdT])
                            P.op("pool", lambda e: e.tensor_tensor(out=sqT[:, 0:nq], in0=dT[:, 0:nq], in1=dT[:, 0:nq], op=ALU.mult), reads=[bdT], writes=[bsqT])
                            mm_group(psA[:, 1536:1536 + nq], bA[3], [(onesf[:], sqT[:, 0:nq])], [bones, bsqT])
                            P.op("act", lambda e: e.activation(out=sqT[:, 0:nq], in_=psA[:, 1536:1536 + nq], func=AF.Ln, bias=epst[:, 0:1], scale=1.0 / 128),
                                 reads=[bA[3], b_const], writes=[bsqT])
                            P.op("act", lambda e: e.activation(out=sqT[:, 0:nq], in_=sqT[:, 0:nq], func=AF.Exp, scale=-0.5), reads=[bsqT], writes=[bsqT])
                            P.op("dve", lambda e: e.scalar_tensor_tensor(out=yT[:, h, q0:q0 + nq], in0=dT[:, 0:nq], scalar=subgc[:, 0:1], in1=sqT[:, 0:nq],
                                                                         op0=ALU.mult, op1=ALU.mult), reads=[bdT, bsqT, bsubg], parts=[byT])
                    sc = 128 ** -0.5
                    for hk in range(2):
                        Ki = kvn[0] % 2
                        kvn[0] += 1
                        kr = (8 + hk) * 128
                        vc = 1024 + hk * 128
                        P.dma("sp", out=KT[Ki][:, 0:1024], in_=lbk(8 + hk)[:, 0:1024], reads=[bLB], parts=[bK[Ki]])
                        P.dma("sp", out=KT[Ki][:, 1024:1536].rearrange("p (s c) -> p s c", s=4), in_=gkr(8 + hk)[:, :, 896:1024], reads=[bG], parts=[bK[Ki]])
                        P.dma("sp", out=KT[Ki][:, 1536:2048].rearrange("p (s c) -> p s c", s=4), in_=gkr(8 + hk)[:, :, 0:128], reads=[bG], parts=[bK[Ki]])
                        P.dma("sp", out=KT[Ki][:, 2048:2304].rearrange("p (s c) -> p s c", s=4), in_=gkr(8 + hk)[:, :, 1024:1088], reads=[bG], parts=[bK[Ki]])
                        for c in range(4):
                            P.dma("sp", out=Vt[Ki][:, 2 * c:2 * c + 2, 0:128], in_=lbv(2 * c, 256)[:, vc:vc + 128].rearrange("(k p) c -> p k c", p=128),
                                  reads=[bLB], parts=[bV[Ki]])
                        for s_ in range(4):
                            P.dma("sp", out=Vt[Ki][:, 8 + s_, 0:128], in_=gv(s_, 7, 128)[:, vc:vc + 128], reads=[bG], parts=[bV[Ki]])
                            P.dma("sp", out=Vt[Ki][:, 12 + s_, 0:128], in_=gv(s_, 0, 128)[:, vc:vc + 128], reads=[bG], parts=[bV[Ki]])
                            P.dma("sp", out=Vt[Ki][(s_ % 2) * 64:(s_ % 2) * 64 + 64, 16 + s_ // 2, 0:128], in_=gv(s_, 8, 64)[:, vc:vc + 128],
                                  reads=[bG], parts=[bV[Ki]])
                        units = []
                        for g in range(4):
                            h = hk * 4 + g
                            for t in range(8 + (1 if with_ctx else 0)):
                                if t == 8:
                                    kts, mk, nq = [16, 17], [None, None], 64
                                else:
                                    kts, mk, nq = [16, 17, t], [None, None, None], 128
                                    if t > 0:
                                        kts.append(t - 1); mk.append(0)
                                    else:
                                        kts += [8, 9, 10, 11]; mk += [2, 3, 4, 5]
                                    if t < 7:
                                        kts.append(t + 1); mk.append(1)
                                    else:
                                        kts += [12, 13, 14, 15]; mk += [6, 7, 8, 9]
                                units.append((h, t, kts, mk, nq, len(units)))

                        def b_stage1(u):
                            h, t, kts, mk, nq, idx = u
                            so = (idx % 2) * 2
                            pk = idx % 2
                            n = len(kts)
                            for ki, kt in enumerate(kts):
                                col = so * 512 + ki * 128
                                mm_group(psA[0:128, col:col + nq], bA[so + ki // 4],
                                         [(KT[Ki][:, kt * 128:(kt + 1) * 128], QT[:, 8 + h, t * 128:t * 128 + nq])], [bK[Ki], bQT])
                            pflat = PT[pk][:].rearrange("p a b -> p (a b)")
                            P.op("act", lambda e: e.activation(out=pflat[:, 0:n * 128].rearrange("p (k c) -> p k c", c=128)[:, :, 0:nq],
                                                               in_=psA[:, so * 512:so * 512 + n * 128].rearrange("p (k c) -> p k c", c=128)[:, :, 0:nq],
                                                               func=AF.Exp, scale=sc),
                                 reads=[bA[so], bA[so + 1]], writes=[bPT[pk]])
                            for ki, kt in enumerate(kts):
                                if mk[ki] is not None:
                                    P.op("pool" if ki % 2 else "dve",
                                         lambda e: e.tensor_tensor(out=pflat[:, ki * 128:ki * 128 + nq], in0=pflat[:, ki * 128:ki * 128 + nq],
                                                                   in1=maskb[:, mk[ki], 0:nq], op=ALU.mult),
                                         reads=[bPT[pk], b_const], writes=[bPT[pk]])

                        def b_stage2(u):
                            h, t, kts, mk, nq, idx = u
                            pk = idx % 2
                            ob = idx % 2
                            M = min(nq, 128)
                            n = len(kts)
                            pflat = PT[pk][:].rearrange("p a b -> p (a b)")
                            acc_ = psB[0:M, ob * 512:ob * 512 + 129]
                            mm_group(acc_, bB[ob], [(pflat[:, ki * 128:ki * 128 + M], Vt[Ki][:, kt, 0:129]) for ki, kt in enumerate(kts)], [bPT[pk], bV[Ki]])
                            P.op("dve", lambda e: e.tensor_tensor(out=rc[0:M, 8 + ob:9 + ob], in0=acc_[:, 128:129], in1=sm[0:M, 8 + h:9 + h], op=ALU.add),
                                 reads=[bB[ob], b_sm], writes=[brc])
                            P.op("dve", lambda e: e.reciprocal(out=rc[0:M, ob:ob + 1], in_=rc[0:M, 8 + ob:9 + ob]), reads=[brc], writes=[brc])
                            P.op("dve", lambda e: e.tensor_scalar(out=y16[0:M, ob, :], in0=acc_[:, 0:128], scalar1=rc[0:M, ob:ob + 1], scalar2=None, op0=ALU.mult),
                                 reads=[bB[ob], brc], writes=[by16b[ob]])
                            pst, bps = transposes(lambda k_: y16[0:M, ob, :], 1, M, [by16b[ob]])
                            P.op("act", lambda e: e.copy(out=yT[:, 8 + h, t * 128:t * 128 + nq].rearrange("p (k m) -> p k m", m=M), in_=pst), reads=[bps], parts=[byT])

                        b_stage1(units[0])
                        for ui in range(len(units)):
                            if ui + 1 < len(units):
                                b_stage1(units[ui + 1])
                            b_stage2(units[ui])
                else:
                    sc = 128 ** -0.5
                    for hk in range(2):
                        Ki = load_kv(hk * 128, hk * 128)
                        for g in range(4):
                            h = hk * 4 + g
                            for q0, nq, kts in qblocks:
                                attendT([(lambda kt: KT[Ki][:, kt * 128:(kt + 1) * 128], QT[:, h, q0:q0 + nq])], [bK[Ki], bQT], Ki, kts, nq, sc,
                                        lambda o_ap, o_buf, h=h, q0=q0, nq=nq: P.op("dve", lambda e: e.tensor_tensor(out=yT[:, h, q0:q0 + nq], in0=o_ap, in1=rcs[:, 0:nq], op=ALU.mult),
                                                                                     reads=[o_buf, brcs], parts=[byT]))
                    kpeL = P.sb(pb, "kpeL", [128, NKEY], BF16)
                    kpeH = P.sb(pb, "kpeH", [128, NKEY], BF16)
                    P.op("pool", lambda e: e.memset(kpeL[64:128, :], 0.0), parts=[bkpe])
                    P.op("pool", lambda e: e.memset(kpeH[0:64, :], 0.0), parts=[bkpe])
                    for kx, r0 in ((kpeL, 0), (kpeH, 64)):
                        P.dma("sp", out=kx[r0:r0 + 64, 0:4096].rearrange("p (s c) -> p s c", s=4), in_=gkr(10)[r0:r0 + 64, :, 0:1024],
                              reads=[bG], parts=[bkpe])
                        P.dma("sp", out=kx[r0:r0 + 64, 4096:4352].rearrange("p (s c) -> p s c", s=4), in_=gkr(10)[r0:r0 + 64, :, 1024:1088],
                              reads=[bG], parts=[bkpe])
                    sc = 192 ** -0.5
                    for h in range(8):
                        Ki = load_kv(256 + h * 128, 256 + h * 128)
                        kpx = kpeL if h % 2 == 0 else kpeH
                        for q0, nq, kts in qblocks:
                            attendT([(lambda kt: KT[Ki][:, kt * 128:(kt + 1) * 128], QT[:, 8 + h, q0:q0 + nq]),
                                     (lambda kt: kpx[:, kt * 128:(kt + 1) * 128], QT[:, 16 + h // 2, q0:q0 + nq])],
                                    [bK[Ki], bkpe, bQT], Ki, kts, nq, sc,
                                    lambda o_ap, o_buf, h=h, q0=q0, nq=nq: P.op("dve", lambda e: e.tensor_tensor(out=yT[:, 8 + h, q0:q0 + nq], in0=o_ap, in1=rcs[:, 0:nq], op=ALU.mult),
                                                                                 reads=[o_buf, brcs], parts=[byT]))
            P.barrier()
            if DEBUG_STOP == "B":
                pbc.close()
            if DEBUG_STOP == "B":
                dbg = nc.dram_tensor("dbg", [128, 64], F32, kind="ExternalOutput").ap()
                P.dma("sp", out=dbg, in_=sm[:], reads=[b_sm, b_lp])
                P.barrier()
                break

            with ExitStack() as pc_:
                nwb[0] = 2
                wcnt[0] = 0
                wb[0] = P.sb(pc_, "wb0", [128, 16, 512], BF16)
                wb[1] = P.sb(pc_, "wb1", [128, 16, 512], BF16)
                gate_bc = P.sb(pc_, "gate_bc", [128, 2, DM], F32)
                gs = P.sb(pc_, "gs", [128, 512], BF16)
                xp = [P.sb(pc_, "xp", [128, 512], F32) for _ in range(2)]
                xg = [P.sb(pc_, "xg", [128, 512], F32) for _ in range(2)]
                bgs, bxp, bxg = Buf("gs"), [Buf("xp0"), Buf("xp1")], [Buf("xg0"), Buf("xg1")]
                b_gate.w, b_gate.wf, b_gate.r = {}, {}, {}
                load_gate(l, gate_bc)
                ntl = NTILE if with_ctx else 8
                tgroups = [(0, 4, 512), (4, 4, 512)] + ([(8, 1, 64)] if with_ctx else [])
                ai = 0
                cjobs = [("g", k_) for k_ in range(4)] + [("o", k_) for k_ in range(4)]

                def cload(j):
                    kind, k_ = cjobs[j]
                    return load_w(w_in, GATE0 + k_ * 512, 512) if kind == "g" else load_w(w_o[l], k_ * 512, 512)
                cw = {0: cload(0)}
                for blk in range(4):
                    cw[blk + 1] = cload(blk + 1)
                    wt, bw = cw[blk]
                    for mm in range(4):
                        m = blk * 4 + mm
                        for t0, nt, ntok in tgroups:
                            i = ai % 2
                            ai += 1
                            if nt == 4:
                                rhs_f = lambda j: hT[:, t0:t0 + 4, j, :]
                            else:
                                rhs_f = lambda j: hT[:, 8, j, 0:64]
                            mm_group(psB[:, i * 512:i * 512 + ntok], bB[i], [(wt[:, j, mm * 128:(mm + 1) * 128], rhs_f(j)) for j in range(16)],
                                     [bw] + bhT)
                            P.op("act", lambda e: e.activation(out=gs[:, 0:ntok], in_=psB[:, i * 512:i * 512 + ntok], func=AF.Silu),
                                 reads=[bB[i]], writes=[bgs])
                            P.op("dve", lambda e: e.tensor_tensor(out=yT[:, m, t0 * 128:t0 * 128 + ntok], in0=yT[:, m, t0 * 128:t0 * 128 + ntok],
                                                                  in1=gs[:, 0:ntok], op=ALU.mult), reads=[bgs, byT], parts=[byT])
                for nb in range(4):
                    if nb + 1 < 4:
                        cw[4 + nb + 1] = cload(4 + nb + 1)
                    wt, bw = cw[4 + nb]
                    for t in range(ntl):
                        rows = 128 if t < 8 else 64
                        i = ai % 2
                        ai += 1
                        if l == 0:
                            src = x_in[t * 128:(t + 1) * 128, nb * 512:(nb + 1) * 512] if t < 8 else ctx_in[:, nb * 512:(nb + 1) * 512]
                            rd = []
                        else:
                            src = xs[t * 128:t * 128 + rows, nb * 512:(nb + 1) * 512]
                            rd = [bxs[t][nb]]
                        P.dma("sp", out=xp[i][0:rows, :], in_=src, reads=rd, writes=[bxp[i]])
                        mm_group(psB[0:rows, i * 512:(i + 1) * 512], bB[i],
                                 [(yT[:, m, t * 128:t * 128 + rows], wt[:, m, :]) for m in range(16)], [bw, byT])
                        grow = 0 if t < 8 else 1
                        P.op("dve", lambda e: e.tensor_tensor(out=xg[i][0:rows, :], in0=psB[0:rows, i * 512:(i + 1) * 512],
                                                              in1=gate_bc[0:rows, grow, nb * 512:(nb + 1) * 512], op=ALU.mult),
                             reads=[b_gate, bB[i]], writes=[bxg[i]])
                        P.op("pool", lambda e: e.tensor_tensor(out=xp[i][0:rows, :], in0=xp[i][0:rows, :], in1=xg[i][0:rows, :], op=ALU.add),
                             reads=[bxg[i]], writes=[bxp[i]])
                        P.dma("sp", out=xs[t * 128:t * 128 + rows, nb * 512:(nb + 1) * 512], in_=xp[i][0:rows, :], reads=[bxp[i]], writes=[bxs[t][nb]])
            pbc.close()
            P.barrier()
      except _Stop:
        P.barrier()
        dbg = nc.dram_tensor("dbg", [128, 64], F32, kind="ExternalOutput").ap()
        P.dma("sp", out=dbg, in_=sm[:], reads=[b_sm, b_lp])
        P.barrier()
        return P

    with ExitStack() as pf:
        fg = P.sb(pf, "fg", [128, DM], F32)
        xt = [P.sb(pf, "fxt", [128, DM], F32) for _ in range(2)]
        xq = [P.sb(pf, "fxq", [128, DM], F32) for _ in range(2)]
        ss = P.sb(pf, "fss", [128, 4], F32)
        bfg, bss, bxt, bxq = Buf("fg"), Buf("fss"), [Buf("a"), Buf("b")], [Buf("c"), Buf("d")]
        bout = Buf("out")
        P.dma("sp", out=fg[:], in_=final_g.broadcast_to([128, DM]), writes=[bfg])
        for t in range(8):
            i = t % 2
            P.dma("sp", out=xt[i][:], in_=xs[t * 128:(t + 1) * 128, :], reads=bxs[t], writes=[bxt[i]])
            P.op("pool", lambda e: e.tensor_tensor(out=xq[i][:], in0=xt[i][:], in1=xt[i][:], op=ALU.mult), reads=[bxt[i]], writes=[bxq[i]])
            P.op("dve", lambda e: e.reduce_sum(out=ss[:, 0:1], in_=xq[i][:], axis=AX.X), reads=[bxq[i]], writes=[bss])
            rstd(ss[:, 2:3], ss[:, 0:1], ss[:, 1:2], DM, bss)
            P.op("dve", lambda e: e.scalar_tensor_tensor(out=xq[i][:], in0=xt[i][:], scalar=ss[:, 2:3], in1=fg[:], op0=ALU.mult, op1=ALU.mult),
                 reads=[bxt[i], bss, bfg], writes=[bxq[i]])
            P.dma("sp", out=out[t * 128:(t + 1) * 128, :], in_=xq[i][:], reads=[bxq[i]], parts=[bout])
    P.barrier()
    return P


DEBUG_STOP = None


def _rope_tables(r):
    tok = r * 1024 + np.arange(1024)
    row = (tok // 64).astype(np.float32)
    col = (tok % 64).astype(np.float32)
    tabs = {}
    for dim in (64, 128):
        nf = dim // 4
        inv = (np.float32(10000.0) ** (-np.arange(nf, dtype=np.float32) / np.float32(nf))).astype(np.float32)
        ang = np.concatenate([row[:, None] * inv[None], col[:, None] * inv[None]], axis=-1).astype(np.float32)
        tabs[dim] = (np.cos(ang).astype(np.float32), np.sin(ang).astype(np.float32))
    t = np.zeros((1024, 288), np.float32)
    t[:, 0:32] = tabs[64][0]
    t[:, 32:96] = tabs[128][0]
    t[:, 96:128] = tabs[64][1]
    t[:, 128:192] = tabs[128][1]
    t[:, 192:224] = -tabs[64][1]
    t[:, 224:288] = -tabs[128][1]
    return np.ascontiguousarray(t.reshape(8, 128, 288).transpose(1, 0, 2).reshape(128, 8 * 288))


def _masks(r):
    i = np.arange(128)[:, None]
    j = np.arange(128)[None, :]
    triP = (i >= j).astype(np.float32)
    triN = (i <= j).astype(np.float32)
    m = np.zeros((128, 10, 128), np.float32)
    m[:, 0] = triP
    m[:, 1] = triN
    for s in range(4):
        if s == r - 1:
            m[:, 2 + s] = triP
        if s == r + 1:
            m[:, 6 + s] = triN
    return np.ascontiguousarray(m.reshape(128, 1280))


def make_in_maps(inp):
    f = lambda a: np.ascontiguousarray(np.asarray(a, dtype=np.float32))
    x, c, ctx, c_ctx = f(inp["x"]), f(inp["c"]), f(inp["ctx"]), f(inp["c_ctx"])
    w_mod, b_mod = f(inp["w_mod"]), f(inp["b_mod"])
    shared = {
        "norm_g": f(inp["norm_g"]), "final_g": f(inp["final_g"]).reshape(1, DM), "w_o": f(inp["w_o"]),
        "e_w_in": f(inp["e_w_in"]), "o_w_in": f(inp["o_w_in"]),
        "a_lam": np.ascontiguousarray(np.concatenate([f(inp["a_lam_q1"]), f(inp["a_lam_k1"]), f(inp["a_lam_q2"]), f(inp["a_lam_k2"])], axis=1)),
        "a_sub_g": f(inp["a_sub_g"]), "b_sink": f(inp["b_sink"]), "c_q_g": f(inp["c_q_g"]), "c_k_g": f(inp["c_k_g"]),
        "d_q_a_g": f(inp["d_q_a_g"]), "d_kv_a_g": f(inp["d_kv_a_g"]), "d_w_q_b": f(inp["d_w_q_b"]), "d_w_kv_b": f(inp["d_w_kv_b"]),
        "ident": np.eye(128, dtype=np.float32),
    }
    wm_sl = [np.ascontiguousarray(w_mod[:, :, r * 1536:(r + 1) * 1536]) for r in range(4)]
    bm_sl = [np.ascontiguousarray(b_mod[:, r * 1536:(r + 1) * 1536]) for r in range(4)]
    maps = []
    for core in range(8):
        b, r = divmod(core, 4)
        m = dict(shared)
        m["x"] = np.ascontiguousarray(x[b, r * 1024:(r + 1) * 1024])
        m["ctx"] = np.ascontiguousarray(ctx[b, r * 64:(r + 1) * 64])
        m["cvec"] = np.ascontiguousarray(np.stack([c[b], c_ctx]))
        m["wmod"] = wm_sl[r]
        m["bmod"] = bm_sl[r]
        m["rope"] = _rope_tables(r)
        m["bmask"] = _masks(r)
        maps.append(m)
    return maps


_NC_CACHE = {}


def kernel(**inputs):
    if "nc" not in _NC_CACHE:
        _NC_CACHE["nc"] = build(4).nc
    maps = make_in_maps(inputs)
    res = run_bass_kernel_spmd(_NC_CACHE["nc"], maps, core_ids=list(range(8)))
    out = np.zeros((2, 4096, DM), np.float32)
    for core in range(8):
        b, r = divmod(core, 4)
        out[b, r * 1024:(r + 1) * 1024] = res.results[core]["out"]
    return out
```

```python
import numpy as np
from contextlib import ExitStack
import concourse.bass as bass
import concourse.mybir as mybir
from concourse.bass_utils import run_bass_kernel_spmd

F32 = mybir.dt.float32
BF16 = mybir.dt.bfloat16
AF = mybir.ActivationFunctionType
ALU = mybir.AluOpType
AX = mybir.AxisListType

DM = 2048
NTILE = 9
TOK = 1088
EPS = 1e-6
NKEY = 4352
NKT = 34


class Buf:
    def __init__(self, name):
        self.name = name
        self.w = {}
        self.wf = {}
        self.r = {}


class Prog:
    def __init__(self):
        self.nc = bass.Bass("TRN2", target_bir_lowering=False)
        self.st = ExitStack()
        nc = self.nc
        self.E = {"pe": nc.tensor, "act": nc.scalar, "dve": nc.vector, "pool": nc.gpsimd, "sp": nc.sync}
        self.sem, self.inc, self.seq = {}, {}, {}
        for k in ("pe", "act", "dve", "pool", "cc"):
            self.sem[k] = self.st.enter_context(nc.semaphore(k))
            self.inc[k] = 1
            self.seq[k] = 0
        self.nslots = {"sp": 16, "pool": 8, "act": 4}
        self.slot = {"sp": 0, "pool": 0, "act": 0}
        for q, n in self.nslots.items():
            for i in range(n):
                k = "%s_d%d" % (q, i)
                self.sem[k] = self.st.enter_context(nc.semaphore(k))
                self.inc[k] = 16
                self.seq[k] = 0
        self.known = {k: {} for k in self.E}
        self.nwait = 0

    def _wait(self, eng, deps):
        for e, s in deps.items():
            if s <= 0:
                continue
            if eng == "pe" and e == "pe":
                continue
            if self.known[eng].get(e, 0) >= s:
                continue
            self.E[eng].wait_ge(self.sem[e], s * self.inc[e])
            self.known[eng][e] = s
            self.nwait += 1

    @staticmethod
    def _deps(reads, writes, parts):
        deps = {}

        def add(d):
            for e, s in d.items():
                if deps.get(e, 0) < s:
                    deps[e] = s
        for b in reads:
            add(b.w)
        for b in writes:
            add(b.w)
            add(b.r)
        for b in parts:
            add(b.r)
            add(b.wf)
        return deps

    @staticmethod
    def _commit(e, s, reads, writes, parts):
        for b in reads:
            if b.r.get(e, 0) < s:
                b.r[e] = s
        for b in writes:
            b.w = {e: s}
            b.wf = {e: s}
            b.r = {}
        for b in parts:
            if b.w.get(e, 0) < s:
                b.w[e] = s

    def op(self, eng, fn, reads=(), writes=(), parts=(), sig=True):
        self._wait(eng, self._deps(reads, writes, parts))
        inst = fn(self.E[eng])
        if sig:
            self.seq[eng] += 1
            inst.then_inc(self.sem[eng], 1)
            s = self.seq[eng]
        else:
            s = self.seq[eng] + 1
        self._commit(eng, s, reads, writes, parts)
        return inst

    def dma(self, q, out, in_, reads=(), writes=(), parts=(), **kw):
        k = "%s_d%d" % (q, self.slot[q])
        self.slot[q] = (self.slot[q] + 1) % self.nslots[q]
        deps = self._deps(reads, writes, parts)
        if deps.get(k, 0) < self.seq[k]:
            deps[k] = self.seq[k]
        self._wait(q, deps)
        inst = self.E[q].dma_start(out=out, in_=in_, **kw)
        self.seq[k] += 1
        inst.then_inc(self.sem[k], 16)
        self._commit(k, self.seq[k], reads, writes, parts)
        return inst

    def barrier(self):
        allv = dict(self.seq)
        for eng in self.E:
            self._wait(eng, allv)

    def sb(self, stack, name, shape, dt):
        self.uid = getattr(self, "uid", 0) + 1
        return stack.enter_context(self.nc.sbuf_tensor("%s_%d" % (name, self.uid), list(shape), dt))

    def ps(self, stack, name, shape, dt):
        return stack.enter_context(self.nc.psum_tensor(name, list(shape), dt))


class _Stop(Exception):
    pass


DEBUG_NSTEP = None


def build(n_layers=4):
    P = Prog()

    def chk(k):
        if DEBUG_NSTEP == k:
            raise _Stop()

    nc = P.nc
    st = P.st

    def din(name, shape):
        return nc.dram_tensor(name, list(shape), F32, kind="ExternalInput").ap()

    x_in = din("x", [1024, DM])
    ctx_in = din("ctx", [64, DM])
    cvec = din("cvec", [2, DM])
    wmod = din("wmod", [4, DM, 1536])
    bmod = din("bmod", [4, 1536])
    norm_g = din("norm_g", [4, DM])
    final_g = din("final_g", [1, DM])
    w_o = din("w_o", [4, DM, DM])
    e_w = din("e_w_in", [2, DM, 6656])
    o_w = din("o_w_in", [2, DM, 4416])
    a_lam = din("a_lam", [2, 256])
    a_sub_g = din("a_sub_g", [2, 128])
    b_sink = din("b_sink", [2, 8])
    c_q_g = din("c_q_g", [2, 128])
    c_k_g = din("c_k_g", [2, 128])
    d_q_a_g = din("d_q_a_g", [2, 512])
    d_kv_a_g = din("d_kv_a_g", [2, 256])
    d_w_q_b = din("d_w_q_b", [2, 512, 1536])
    d_w_kv_b = din("d_w_kv_b", [2, 256, 2048])
    rope_in = din("rope", [128, 8 * 288])
    ident_in = din("ident", [128, 128])
    bmask_in = din("bmask", [128, 10 * 128])
    out = nc.dram_tensor("out", [1024, DM], F32, kind="ExternalOutput").ap()

    xs = nc.dram_tensor("xs", [TOK, DM], F32).ap()
    modloc = nc.dram_tensor("modloc", [8, 1536], F32)
    modall = nc.dram_tensor("modall", [32, 1536], F32)
    GROUPS = [[0, 1, 2, 3], [4, 5, 6, 7]]

    ccn = [0]

    def allgather(src_t, dst_t, reads, writes):
        k = "cc%d" % ccn[0]
        ccn[0] += 1
        P.sem[k] = st.enter_context(nc.semaphore(k))
        P.inc[k] = 1
        P.seq[k] = 0
        P._wait("pool", P._deps(reads, (), writes))
        inst = nc.gpsimd.collective_compute("AllGather", ALU.bypass, replica_groups=GROUPS,
                                            ins=[src_t.ap().opt()], outs=[dst_t.ap().opt()])
        inst.then_inc(P.sem[k])
        P.seq[k] = 1
        P._commit(k, 1, reads, (), writes)

    identb = P.sb(st, "identb", [128, 128], BF16)
    maskb = P.sb(st, "maskb", [128, 10, 128], BF16)
    ropet = P.sb(st, "ropet", [128, 8, 288], F32)
    modv = P.sb(st, "modv", [128, 4, 5, 16], F32)
    hT = P.sb(st, "hT", [128, NTILE, 16, 128], BF16)
    wb = [None, None, None]
    nwb = [2]
    lp = P.sb(st, "lp", [128, 1280], F32)
    sm = P.sb(st, "sm", [128, 64], F32)
    b_const = Buf("const")
    b_modv = Buf("modv")
    b_gate = Buf("gate")
    bhT = [Buf("hT%d" % t) for t in range(NTILE)]
    bwb = [Buf("wb0"), Buf("wb1"), Buf("wb2")]
    b_lp = Buf("lp")
    b_sm = Buf("sm")
    wcnt = [0]

    psA = P.ps(st, "psA", [128, 2048], F32)
    psB = P.ps(st, "psB", [128, 1024], F32)
    psT = P.ps(st, "psT", [128, 2, 1024], BF16)
    bA = [Buf("psA%d" % i) for i in range(4)]
    bB = [Buf("psB%d" % i) for i in range(2)]
    bT = [Buf("psT%d" % i) for i in range(2)]
    tcnt = [0]

    bxs = [[Buf("xs%d_%d" % (t, nb)) for nb in range(4)] for t in range(NTILE)]

    P.dma("pool", out=identb[:], in_=ident_in, writes=[b_const])
    P.dma("pool", out=maskb[:], in_=bmask_in.rearrange("p (m k) -> p m k", k=128), parts=[b_const])
    P.dma("sp", out=ropet[:], in_=rope_in.rearrange("p (t k) -> p t k", k=288), parts=[b_const])

    def load_w(src2d, c0, ncols):
        i = wcnt[0] % nwb[0]
        wcnt[0] += 1
        P.dma("pool", out=wb[i][:, :, 0:ncols],
              in_=src2d[:, c0:c0 + ncols].rearrange("(j p) c -> p j c", p=128), writes=[bwb[i]])
        return wb[i], bwb[i]

    with ExitStack() as ph:
        cT = P.sb(ph, "cT", [128, 2, 16], F32)
        scT = P.sb(ph, "scT", [128, 2, 16], F32)
        wm = [P.sb(ph, "wm%d" % i, [128, 8, 512], F32) for i in range(2)]
        rowt = P.sb(ph, "rowt", [64, 128], F32)
        identf = P.sb(ph, "identf", [128, 128], F32)
        b_rowt = Buf("rowt")
        P.dma("sp", out=identf[:], in_=ident_in, parts=[b_const])
        modsb = P.sb(ph, "modsb", [2, 1536], F32)
        bmsb = P.sb(ph, "bmsb", [2, 1536], F32)
        b_cT, b_scT, b_modsb, b_bmsb, b_modloc, b_modall = (Buf(n) for n in ("cT", "scT", "modsb", "bmsb", "modloc", "modall"))
        bwm = [Buf("wm0"), Buf("wm1")]
        P.dma("sp", out=rowt[0:32, :], in_=cvec.rearrange("r (j p) -> (r j) p", p=128), writes=[b_rowt])
        P.op("pe", lambda e: e.transpose(out=psB[:, 0:32], in_=rowt[0:32, :], identity=identf[0:32, 0:32]),
             reads=[b_rowt, b_const], writes=[bB[0]])
        P.op("dve", lambda e: e.tensor_copy(out=cT[:].rearrange("p a b -> p (a b)"), in_=psB[:, 0:32]), reads=[bB[0]], writes=[b_cT])
        P.op("act", lambda e: e.activation(out=scT[:], in_=cT[:], func=AF.Silu), reads=[b_cT], writes=[b_scT])
        k = 0
        for l in range(4):
            P.dma("sp", out=bmsb[:], in_=bmod[l:l + 1, :].broadcast_to([2, 1536]),
                  writes=[b_bmsb])
            for pc in range(3):
                for hf in range(2):
                    i = k % 2
                    k += 1
                    P.dma("sp", out=wm[i][:],
                          in_=wmod[l, hf * 1024:(hf + 1) * 1024, pc * 512:(pc + 1) * 512].rearrange("(j p) c -> p j c", p=128),
                          writes=[bwm[i]])
                    for j in range(8):
                        first = hf == 0 and j == 0
                        last = hf == 1 and j == 7
                        P.op("pe", lambda e: e.matmul(psB[0:2, 0:512], lhsT=scT[:, :, hf * 8 + j], rhs=wm[i][:, j, :],
                                                      start=first, stop=last),
                             reads=[b_scT, bwm[i]], writes=[bB[0]] if first else (), parts=() if first else [bB[0]], sig=last)
                P.op("dve", lambda e: e.tensor_tensor(out=modsb[:, pc * 512:(pc + 1) * 512], in0=psB[0:2, 0:512],
                                                      in1=bmsb[:, pc * 512:(pc + 1) * 512], op=ALU.add),
                     reads=[bB[0], b_bmsb], parts=[b_modsb])
            P.dma("sp", out=modloc.ap()[l * 2:(l + 1) * 2, :], in_=modsb[:], reads=[b_modsb], parts=[b_modloc])
        allgather(modloc, modall, [b_modloc], [b_modall])

        def mod_pieces(col0, n):
            res = []
            off = 0
            while off < n:
                col = col0 + off
                rk, cc = divmod(col, 1536)
                ln = min(n - off, 1536 - cc)
                res.append((rk, cc, ln, off))
                off += ln
            return res

        for l in range(4):
            for row in range(2):
                for rk, cc, ln, off in mod_pieces(0, 4096):
                    P.dma("sp", out=rowt[off // 128:(off + ln) // 128, :],
                          in_=modall.ap()[rk * 8 + l * 2 + row, cc:cc + ln].rearrange("(j p) -> j p", p=128),
                          reads=[b_modall], parts=[b_rowt])
                P.op("pe", lambda e: e.transpose(out=psB[:, 0:32], in_=rowt[0:32, :], identity=identf[0:32, 0:32]),
                     reads=[b_rowt, b_const], writes=[bB[0]])
                P.op("dve", lambda e: e.tensor_copy(out=modv[:, l, row * 2:row * 2 + 2, :].rearrange("p a b -> p (a b)"), in_=psB[:, 0:32]),
                     reads=[bB[0]], parts=[b_modv])
        P.dma("sp", out=rowt[0:64, :], in_=norm_g.rearrange("l (j p) -> (l j) p", p=128), writes=[b_rowt])
        P.op("pe", lambda e: e.transpose(out=psB[:, 0:64], in_=rowt[0:64, :], identity=identf[0:64, 0:64]),
             reads=[b_rowt, b_const], writes=[bB[0]])
        for l in range(4):
            P.op("dve", lambda e: e.tensor_copy(out=modv[:, l, 4, :], in_=psB[:, l * 16:(l + 1) * 16]), reads=[bB[0]], parts=[b_modv])
        for idx in (1, 3):
            P.op("dve", lambda e: e.scalar_tensor_tensor(out=modv[:, :, idx, :], in0=modv[:, :, idx, :], scalar=1.0,
                                                         in1=modv[:, :, 4, :], op0=ALU.add, op1=ALU.mult),
                 reads=[b_modv], writes=[b_modv])

        def load_gate(l, gate_bc):
            for row in range(2):
                for rk, cc, ln, off in mod_pieces(4096, 2048):
                    P.dma("sp", out=gate_bc[:, row, off:off + ln],
                          in_=modall.ap()[rk * 8 + l * 2 + row:rk * 8 + l * 2 + row + 1, cc:cc + ln].broadcast_to([128, ln]),
                          reads=[b_modall], parts=[b_gate])
        P.barrier()

    if DEBUG_STOP == "P":
        dbg = nc.dram_tensor("dbg", [128, 320], F32, kind="ExternalOutput").ap()
        P.dma("sp", out=dbg, in_=modv[:].rearrange("p a b c -> p (a b c)"), reads=[b_modv])
        P.barrier()
        return P
    def mm_group(out_ap, bout, pairs, rd):
        n = len(pairs)
        for i, (lt, rh) in enumerate(pairs):
            P.op("pe", lambda e: e.matmul(out_ap, lhsT=lt, rhs=rh, start=(i == 0), stop=(i == n - 1)),
                 reads=rd, writes=[bout] if i == 0 else (), parts=() if i == 0 else [bout], sig=(i == n - 1))

    def transposes(src_fn, n, M, rd):
        i = tcnt[0] % 2
        tcnt[0] += 1
        for k in range(n):
            P.op("pe", lambda e: e.transpose(out=psT[:, i, k * 128:k * 128 + M], in_=src_fn(k), identity=identb[0:M, 0:M]),
                 reads=rd + [b_const], writes=[bT[i]] if k == 0 else (), parts=() if k == 0 else [bT[i]], sig=(k == n - 1))
        return psT[:, i, 0:n * 128].rearrange("p (h k) -> p h k", k=128)[:, :, 0:M], bT[i]

    epst = P.sb(st, "epst", [128, 1], F32)
    P.op("pool", lambda e: e.memset(epst[:], EPS), writes=[b_const])

    def rstd(out_ap, sum_ap, tmp_ap, n, buf):
        P.op("act", lambda e: e.activation(out=tmp_ap, in_=sum_ap, func=AF.Sqrt, bias=epst[0:out_ap.shape[0], 0:1], scale=1.0 / n),
             reads=[buf, b_const], writes=[buf])
        P.op("dve", lambda e: e.reciprocal(out=out_ap, in_=tmp_ap), reads=[buf], writes=[buf])

    for l in range(n_layers):
      try:
        even = (l % 2 == 0)
        li = l // 2
        with_ctx = l < n_layers - 1
        lam_init = 0.8 - 0.6 * float(np.exp(-0.3 * l))
        w_in = e_w[li] if even else o_w[li]
        NC_IN = 6656 if even else 4416
        GATE0 = NC_IN - 2048
        if even:
            KR, NVC, NQH = 1280, 1280, 24
        else:
            KR, NVC, NQH = 1408, 1280, 20
        R = KR + NVC
        bLB, bG = Buf("LB"), Buf("G")
        nkh_l = 10 if even else 11
        kchunks = [(c * 3, min(3, nkh_l - c * 3)) for c in range((nkh_l + 2) // 3)]
        LBK = [nc.dram_tensor("LBK%d_%d" % (l, c), [nh * 128, TOK], BF16) for c, (h0, nh) in enumerate(kchunks)]
        GK = [nc.dram_tensor("GK%d_%d" % (l, c), [4 * nh * 128, TOK], BF16) for c, (h0, nh) in enumerate(kchunks)]
        VCH = [(0, 256), (256, 256), (512, 256), (768, 320)]
        LBV = [nc.dram_tensor("LBV%d_%d" % (l, c), [n, NVC], BF16) for c, (t0, n) in enumerate(VCH)]
        GV = [nc.dram_tensor("GV%d_%d" % (l, c), [4 * n, NVC], BF16) for c, (t0, n) in enumerate(VCH)]

        def lbk(h):
            return LBK[h // 3].ap()[(h % 3) * 128:(h % 3) * 128 + 128, :]

        def gkr(h):
            return GK[h // 3].ap().rearrange("(s r) c -> r s c", s=4)[(h % 3) * 128:(h % 3) * 128 + 128]

        def vch(t):
            c = min(t // 2, 3)
            return c, t * 128 - VCH[c][0]

        def lbv(t, rows):
            c, r0 = vch(t)
            return LBV[c].ap()[r0:r0 + rows, :]

        def gv(s, t, rows):
            c, r0 = vch(t)
            n = VCH[c][1]
            return GV[c].ap()[s * n + r0:s * n + r0 + rows, :]

        with ExitStack() as ls:
            QT = P.sb(ls, "QT", [128, NQH, TOK], BF16)
            bQT, byT = Buf("QT"), Buf("yT")
            if even:
                P.op("pool", lambda e: e.memset(QT[:], 0.0), writes=[bQT])
            if even:
                P.dma("sp", out=lp[:, 0:256], in_=a_lam[li:li + 1, :].broadcast_to([128, 256]), parts=[b_lp])
                P.dma("sp", out=lp[:, 256:384], in_=a_sub_g[li:li + 1, :].broadcast_to([128, 128]), parts=[b_lp])
                P.dma("sp", out=lp[:, 384:392], in_=b_sink[li:li + 1, :].broadcast_to([128, 8]), parts=[b_lp])
                P.op("dve", lambda e: e.tensor_tensor(out=lp[:, 512:576], in0=lp[:, 0:64], in1=lp[:, 64:128], op=ALU.mult),
                     reads=[b_lp], parts=[b_lp])
                P.op("dve", lambda e: e.tensor_tensor(out=lp[:, 576:640], in0=lp[:, 128:192], in1=lp[:, 192:256], op=ALU.mult),
                     reads=[b_lp], parts=[b_lp])
                P.op("dve", lambda e: e.reduce_sum(out=sm[:, 1:3], in_=lp[:, 512:640].rearrange("p (a b) -> p a b", b=64), axis=AX.X),
                     reads=[b_lp], writes=[b_sm])
                P.op("act", lambda e: e.activation(out=sm[:, 3:5], in_=sm[:, 1:3], func=AF.Exp), reads=[b_sm], writes=[b_sm])
                P.op("act", lambda e: e.activation(out=sm[:, 8:16], in_=lp[:, 384:392], func=AF.Exp), reads=[b_lp, b_sm], writes=[b_sm])
                P.op("dve", lambda e: e.scalar_tensor_tensor(out=sm[:, 0:1], in0=sm[:, 4:5], scalar=-lam_init, in1=sm[:, 3:4],
                                                             op0=ALU.add, op1=ALU.subtract), reads=[b_sm], writes=[b_sm])
                P.op("dve", lambda e: e.tensor_scalar(out=lp[:, 640:768], in0=lp[:, 256:384], scalar1=(1.0 - lam_init), scalar2=None,
                                                      op0=ALU.mult), reads=[b_lp], writes=[b_lp])
            else:
                P.dma("sp", out=lp[:, 0:128], in_=c_q_g[li:li + 1, :].broadcast_to([128, 128]), parts=[b_lp])
                P.dma("sp", out=lp[:, 128:256], in_=c_k_g[li:li + 1, :].broadcast_to([128, 128]), parts=[b_lp])
                P.dma("sp", out=lp[:, 256:768], in_=d_q_a_g[li:li + 1, :].broadcast_to([128, 512]), parts=[b_lp])
                P.dma("sp", out=lp[:, 768:1024], in_=d_kv_a_g[li:li + 1, :].broadcast_to([128, 256]), parts=[b_lp])

            if DEBUG_STOP == "L":
                dbg = nc.dram_tensor("dbg", [128, 64], F32, kind="ExternalOutput").ap()
                P.dma("sp", out=dbg, in_=sm[:], reads=[b_sm, b_lp])
                P.barrier()
                break
            wst = ExitStack()
            nwb[0] = 3 if even else 2
            wcnt[0] = 0
            for i_ in range(nwb[0]):
                wb[i_] = P.sb(wst, "wb%d" % i_, [128, 16, 512], BF16)
            first_c0 = 4096 if even else 1024
            pre_w = load_w(w_in, first_c0, 512)
            with ExitStack() as ph:
                xt = [P.sb(ph, "xt%d" % i, [128, DM], F32) for i in range(2)]
                xn = [P.sb(ph, "xn%d" % i, [128, DM], BF16) for i in range(2)]
                bxt = [Buf("xt0"), Buf("xt1")]
                bxn = [Buf("xn0"), Buf("xn1")]
                ss = P.sb(ph, "ss", [128, 4], F32)
                bss = Buf("ss")
                def nstats(t):
                    i = t % 2
                    so = 4 * i
                    rows = 128 if t < 8 else 64
                    if l == 0:
                        src = x_in[t * 128:(t + 1) * 128, :] if t < 8 else ctx_in[:, :]
                        rd = []
                    else:
                        src = xs[t * 128:t * 128 + rows, :]
                        rd = bxs[t]
                    P.dma("sp", out=xt[i][0:rows, :], in_=src, reads=rd, writes=[bxt[i]])
                    if t == 8:
                        P.op("pool", lambda e: e.memset(xt[i][64:128, :], 0.0), parts=[bxt[i]])
                    P.op("dve", lambda e: e.tensor_tensor(out=xn[i][:], in0=xt[i][:], in1=xt[i][:], op=ALU.mult),
                         reads=[bxt[i]], writes=[bxn[i]])
                    P.op("dve", lambda e: e.reduce_sum(out=ss8[:, so:so + 1], in_=xn[i][:], axis=AX.X), reads=[bxn[i]], writes=[bss8[i]])
                    rstd(ss8[:, so + 2:so + 3], ss8[:, so:so + 1], ss8[:, so + 1:so + 2], DM, bss8[i])
                    P.op("dve", lambda e: e.tensor_scalar(out=xn[i][:], in0=xt[i][:], scalar1=ss8[:, so + 2:so + 3], scalar2=None,
                                                          op0=ALU.mult), reads=[bxt[i], bss8[i]], writes=[bxn[i]])

                def nevac(t):
                    i = t % 2
                    mrow = 0 if t < 8 else 2
                    for half in range(2):
                        pst, bps = transposes(lambda k: xn[i][:, (half * 8 + k) * 128:(half * 8 + k + 1) * 128], 8, 128, [bxn[i]])
                        for k in range(8):
                            j = half * 8 + k
                            P.op("act", lambda e: e.activation(out=hT[:, t, j, :], in_=pst[:, k, :], func=AF.Identity,
                                                               bias=modv[:, l, mrow, j:j + 1], scale=modv[:, l, mrow + 1, j:j + 1]),
                                 reads=[bps, b_modv], parts=[bhT[t]])

                ss8 = P.sb(ph, "ss8", [128, 8], F32)
                bss8 = [Buf("ss8a"), Buf("ss8b")]
                nstats(0)
                for t in range(NTILE):
                    if t + 1 < NTILE:
                        nstats(t + 1)
                    nevac(t)
            P.barrier()
            if DEBUG_STOP == "N":
                dbg = nc.dram_tensor("dbg", [128, 64], F32, kind="ExternalOutput").ap()
                P.dma("sp", out=dbg, in_=sm[:], reads=[b_sm, b_lp])
                P.barrier()
                break

            with ExitStack() as pa:
                KTst = P.sb(pa, "KTst", [128, 11, TOK], BF16)
                bKT = [Buf("KT%d" % h) for h in range(11)]
                u32 = P.sb(pa, "u32", [128, 2048], F32)
                ta = P.sb(pa, "ta", [128, 512], F32)
                tb = P.sb(pa, "tb", [128, 512], F32)
                tn = P.sb(pa, "tn", [128, 512], F32)
                o16 = P.sb(pa, "o16", [128, 1024], BF16)
                v16 = P.sb(pa, "v16", [128, 1024], BF16)
                st8 = P.sb(pa, "st8", [128, 16], F32)
                bu, bta, btb, btn, bo16, bv16, bst8 = (Buf(n) for n in ("u32", "ta", "tb", "tn", "o16", "v16", "st8"))
                if not even:
                    wqb = P.sb(pa, "wqb", [128, 4, 1536], BF16)
                    wkvb = P.sb(pa, "wkvb", [128, 2, 2048], BF16)
                    cT16 = P.sb(pa, "cT16", [128, 4, 128], BF16)
                    bwq, bwkv, bcT16 = Buf("wqb"), Buf("wkvb"), Buf("cT16")
                    P.dma("pool", out=wqb[:], in_=d_w_q_b[li].rearrange("(j p) c -> p j c", p=128), writes=[bwq])
                    P.dma("pool", out=wkvb[:], in_=d_w_kv_b[li].rearrange("(j p) c -> p j c", p=128), writes=[bwkv])

                def rope(src, ncols, G, Dh, t, dst, bsrc, bdst):
                    h = Dh // 2
                    c0, s0, n0 = (0, 96, 192) if Dh == 64 else (32, 128, 224)
                    pat = "p (g two h) -> p g two h"
                    s4 = src[:, 0:ncols].rearrange(pat, two=2, h=h)
                    a4 = ta[:, 0:ncols].rearrange(pat, two=2, h=h)
                    b4 = tb[:, 0:ncols].rearrange(pat, two=2, h=h)
                    cosb = ropet[:, t, c0:c0 + h].unsqueeze(1).unsqueeze(1).broadcast_to([128, G, 2, h])
                    sinb = ropet[:, t, s0:s0 + h].unsqueeze(1).broadcast_to([128, G, h])
                    nsinb = ropet[:, t, n0:n0 + h].unsqueeze(1).broadcast_to([128, G, h])
                    P.op("dve" if even else "pool", lambda e: e.tensor_tensor(out=a4, in0=s4, in1=cosb, op=ALU.mult), reads=[bsrc, b_const], writes=[bta])
                    P.op("dve", lambda e: e.tensor_tensor(out=b4[:, :, 0, :], in0=s4[:, :, 1, :], in1=nsinb, op=ALU.mult),
                         reads=[bsrc, b_const], writes=[btb])
                    P.op("dve", lambda e: e.tensor_tensor(out=b4[:, :, 1, :], in0=s4[:, :, 0, :], in1=sinb, op=ALU.mult),
                         reads=[bsrc, b_const], parts=[btb])
                    P.op("dve", lambda e: e.tensor_tensor(out=dst[:, 0:ncols], in0=ta[:, 0:ncols], in1=tb[:, 0:ncols], op=ALU.add),
                         reads=[bta, btb], writes=[bdst])

                def rmsn(src, ncols, G, Dh, gain, bsrc):
                    s3 = src[:, 0:ncols].rearrange("p (g d) -> p g d", d=Dh)
                    t3 = tn[:, 0:ncols].rearrange("p (g d) -> p g d", d=Dh)
                    P.op("pool", lambda e: e.tensor_tensor(out=ta[:, 0:ncols], in0=src[:, 0:ncols], in1=src[:, 0:ncols], op=ALU.mult),
                         reads=[bsrc], writes=[bta])
                    P.op("dve", lambda e: e.reduce_sum(out=st8[:, 0:G], in_=ta[:, 0:ncols].rearrange("p (g d) -> p g d", d=Dh), axis=AX.X),
                         reads=[bta], writes=[bst8])
                    P.op("act", lambda e: e.activation(out=st8[:, 8:8 + G], in_=st8[:, 0:G], func=AF.Sqrt, bias=epst[:, 0:1], scale=1.0 / Dh),
                         reads=[bst8, b_const], writes=[bst8])
                    P.op("dve", lambda e: e.reciprocal(out=st8[:, 0:G], in_=st8[:, 8:8 + G]), reads=[bst8], writes=[bst8])
                    P.op("dve", lambda e: e.tensor_tensor(out=t3, in0=s3, in1=st8[:, 0:G].unsqueeze(2).broadcast_to([128, G, Dh]), op=ALU.mult),
                         reads=[bsrc, bst8], writes=[btn])
                    P.op("pool", lambda e: e.tensor_tensor(out=t3, in0=t3, in1=gain.unsqueeze(1).broadcast_to([128, G, Dh]), op=ALU.mult),
                         reads=[btn, b_lp], writes=[btn])

                def to_fm(src16, n, t, dstT, h0, bsrc, bdst_list, split=False):
                    ntk = 128 if t < 8 else 64
                    pst, bps = transposes(lambda k: src16[:, k * 128:(k + 1) * 128], n, 128, [bsrc])
                    if split:
                        P.op("act", lambda e: e.copy(out=dstT[0:64, h0:h0 + n, t * 128:t * 128 + ntk], in_=pst[0:64, :, 0:ntk]),
                             reads=[bps], parts=bdst_list)
                        P.op("act", lambda e: e.copy(out=dstT[64:128, 16 + h0:16 + h0 + n, t * 128:t * 128 + ntk], in_=pst[64:128, :, 0:ntk]),
                             reads=[bps], parts=bdst_list)
                    else:
                        P.op("act", lambda e: e.copy(out=dstT[:, h0:h0 + n, t * 128:t * 128 + ntk], in_=pst[:, :, 0:ntk]),
                             reads=[bps], parts=bdst_list)

                def qk_post(ps_ap, bps_, ncols, G, Dh, t, gain, dstT, h0, bdst_list, do_rope=True, split=False):
                    P.op("act", lambda e: e.copy(out=u32[:, 0:ncols], in_=ps_ap), reads=[bps_], writes=[bu])
                    src, bsrc = u32, bu
                    if gain is not None:
                        rmsn(u32, ncols, G, Dh, gain, bu)
                        src, bsrc = tn, btn
                    if do_rope and t < 8:
                        rope(src, ncols, G, Dh, t, o16, bsrc, bo16)
                    else:
                        P.op("dve", lambda e: e.tensor_copy(out=o16[:, 0:ncols], in_=src[:, 0:ncols]), reads=[bsrc], writes=[bo16])
                    to_fm(o16, ncols // 128, t, dstT, h0, bo16, bdst_list, split=split)

                def v_post(ps_ap, bps_, ncols, t, vcol0):
                    rows = 128 if t < 8 else 64
                    P.op("act", lambda e: e.copy(out=v16[:, 0:ncols], in_=ps_ap), reads=[bps_], writes=[bv16])
                    P.dma("sp", out=lbv(t, rows)[:, vcol0:vcol0 + ncols], in_=v16[0:rows, 0:ncols], reads=[bv16], parts=[bLB])

                acc = [0]

                jobs = []

                def inproj(c0, ncols, consume):
                    jobs.append((c0, ncols, consume))

                def flush():
                    wj = [k for k, jb in enumerate(jobs) if jb[0] != "call"]
                    assert jobs[wj[0]][0] == first_c0 and jobs[wj[0]][1] == 512
                    loaded = {wj[0]: pre_w}
                    depth = nwb[0] - 1
                    for k, jb in enumerate(jobs):
                        if jb[0] == "call":
                            jb[1]()
                            continue
                        c0, ncols, consume = jb
                        pos = wj.index(k)
                        for ahead in range(1, depth + 1):
                            if pos + ahead < len(wj):
                                kn = wj[pos + ahead]
                                if kn not in loaded:
                                    loaded[kn] = load_w(w_in, jobs[kn][0], jobs[kn][1])
                        wt, bw = loaded[k]
                        prev = None
                        for t in range(NTILE):
                            i = acc[0] % 2
                            acc[0] += 1
                            bank = psB[:, i * 512:i * 512 + ncols]
                            mm_group(bank, bB[i], [(hT[:, t, j, :], wt[:, j, 0:ncols]) for j in range(16)], [bhT[t], bw])
                            if prev is not None:
                                consume(*prev)
                            prev = (t, bank, bB[i])
                        consume(*prev)
                    del jobs[:]

                def xch(heads, kch, vch):
                    for h in heads:
                        P.dma("sp", out=lbk(h), in_=KTst[:, h, :], reads=[bKT[h]], parts=[bLB])
                    for c in kch:
                        allgather(LBK[c], GK[c], [bLB], [bG])
                    for c in vch:
                        allgather(LBV[c], GV[c], [bLB], [bG])

                if even:
                    def bkv(t, bk, bb):
                        qk_post(bk[:, 0:256], bb, 256, 2, 128, t, None, KTst, 8, bKT[8:10])
                        v_post(bk[:, 256:512], bb, 256, t, 1024)
                    inproj(4096, 512, bkv)
                    for hb in range(2):
                        inproj(2048 + hb * 512, 512, lambda t, bk, bb, hb=hb: v_post(bk, bb, 512, t, hb * 512))
                    jobs.append(("call", lambda: xch([9], [3], [0, 1])))
                    inproj(1024, 512, lambda t, bk, bb: qk_post(bk, bb, 512, 8, 64, t, None, KTst, 0, bKT[0:4]))
                    jobs.append(("call", lambda: xch([0, 1, 2], [0], [2, 3])))
                    inproj(1536, 512, lambda t, bk, bb: qk_post(bk, bb, 512, 8, 64, t, None, KTst, 4, bKT[4:8]))
                    jobs.append(("call", lambda: xch([3, 4, 5, 6, 7, 8], [1, 2], [])))
                    nkh = 10
                else:
                    def ckv(t, bk, bb):
                        qk_post(bk[:, 0:256], bb, 256, 2, 128, t, lp[:, 128:256], KTst, 0, bKT[0:2])
                        v_post(bk[:, 256:512], bb, 256, t, 0)
                    inproj(1024, 512, ckv)

                    def dkv(t, bk, bb):
                        P.op("act", lambda e: e.copy(out=u32[:, 0:320], in_=bk[:, 0:320]), reads=[bb], writes=[bu])
                        rmsn(u32, 256, 1, 256, lp[:, 768:1024], bu)
                        P.op("dve", lambda e: e.tensor_copy(out=o16[:, 0:256], in_=tn[:, 0:256]), reads=[btn], writes=[bo16])
                        pst, bps = transposes(lambda k: o16[:, k * 128:(k + 1) * 128], 2, 128, [bo16])
                        P.op("act", lambda e: e.copy(out=cT16[:, 0:2, :], in_=pst), reads=[bps], writes=[bcT16])
                        for nb in range(4):
                            mm_group(psA[:, nb * 512:(nb + 1) * 512], bA[nb],
                                     [(cT16[:, kc, :], wkvb[:, kc, nb * 512:(nb + 1) * 512]) for kc in range(2)], [bcT16, bwkv])
                        if t < 8:
                            rope(u32[:, 256:320], 64, 1, 64, t, o16[:, 0:64], bu, bo16)
                        else:
                            P.op("dve", lambda e: e.tensor_copy(out=o16[:, 0:64], in_=u32[:, 256:320]), reads=[bu], writes=[bo16])
                        P.op("dve", lambda e: e.tensor_copy(out=o16[:, 64:128], in_=o16[:, 0:64]), reads=[bo16], parts=[bo16])
                        to_fm(o16, 1, t, KTst, 10, bo16, [bKT[10]])
                        for nb in range(4):
                            kv4 = psA[:, nb * 512:(nb + 1) * 512].rearrange("p (h two d) -> p h two d", two=2, d=128)
                            P.op("act", lambda e: e.copy(out=o16[:, nb * 256:(nb + 1) * 256].rearrange("p (h d) -> p h d", d=128), in_=kv4[:, :, 0, :]),
                                 reads=[bA[nb]], writes=[bo16] if nb == 0 else (), parts=() if nb == 0 else [bo16])
                            P.op("dve", lambda e: e.tensor_copy(out=v16[:, nb * 256:(nb + 1) * 256].rearrange("p (h d) -> p h d", d=128), in_=kv4[:, :, 1, :]),
                                 reads=[bA[nb]], writes=[bv16] if nb == 0 else (), parts=() if nb == 0 else [bv16])
                        to_fm(o16, 8, t, KTst, 2, bo16, bKT[2:10])
                        rows = 128 if t < 8 else 64
                        P.dma("sp", out=lbv(t, rows)[:, 256:1280], in_=v16[0:rows, 0:1024], reads=[bv16], parts=[bLB])
                    inproj(2048, 320, dkv)
                    nkh = 11
                if not even:
                    jobs.append(("call", lambda: xch(list(range(11)), [0, 1, 2], [])))

                if even:
                    for hb in range(2):
                        inproj(hb * 512, 512, lambda t, bk, bb, hb=hb: qk_post(bk, bb, 512, 8, 64, t, None, QT, hb * 4, [bQT], split=True))
                    for hb in range(2):
                        inproj(3072 + hb * 512, 512, lambda t, bk, bb, hb=hb: qk_post(bk, bb, 512, 4, 128, t, None, QT, 8 + hb * 4, [bQT]))
                else:
                    for hb in range(2):
                        inproj(hb * 512, 512, lambda t, bk, bb, hb=hb: qk_post(bk, bb, 512, 4, 128, t, lp[:, 0:128], QT, hb * 4, [bQT]))
                        jobs.append(("call", (lambda: xch([], [3], [0, 1])) if hb == 0 else (lambda: xch([], [], [2, 3]))))

                    def dq(t, bk, bb):
                        P.op("act", lambda e: e.copy(out=u32[:, 0:512], in_=bk), reads=[bb], writes=[bu])
                        rmsn(u32, 512, 1, 512, lp[:, 256:768], bu)
                        P.op("dve", lambda e: e.tensor_copy(out=o16[:, 0:512], in_=tn[:, 0:512]), reads=[btn], writes=[bo16])
                        pst, bps = transposes(lambda k: o16[:, k * 128:(k + 1) * 128], 4, 128, [bo16])
                        P.op("act", lambda e: e.copy(out=cT16[:], in_=pst), reads=[bps], writes=[bcT16])
                        for nb in range(3):
                            mm_group(psA[:, nb * 512:(nb + 1) * 512], bA[nb],
                                     [(cT16[:, kc, :], wqb[:, kc, nb * 512:(nb + 1) * 512]) for kc in range(4)], [bcT16, bwq])
                        for nb in range(3):
                            P.op("act", lambda e: e.copy(out=u32[:, nb * 512:(nb + 1) * 512], in_=psA[:, nb * 512:(nb + 1) * 512]),
                                 reads=[bA[nb]], writes=[bu] if nb == 0 else (), parts=() if nb == 0 else [bu])
                        q3 = u32[:, 0:1536].rearrange("p (h d) -> p h d", d=192)
                        P.op("dve", lambda e: e.tensor_copy(out=o16[:, 0:1024].rearrange("p (h d) -> p h d", d=128), in_=q3[:, :, 0:128]),
                             reads=[bu], writes=[bo16])
                        P.op("pool", lambda e: e.tensor_copy(out=tn[:, 0:512].rearrange("p (h d) -> p h d", d=64), in_=q3[:, :, 128:192]),
                             reads=[bu], writes=[btn])
                        to_fm(o16, 8, t, QT, 8, bo16, [bQT])
                        if t < 8:
                            rope(tn, 512, 8, 64, t, v16, btn, bv16)
                        else:
                            P.op("dve", lambda e: e.tensor_copy(out=v16[:, 0:512], in_=tn[:, 0:512]), reads=[btn], writes=[bv16])
                        to_fm(v16, 4, t, QT, 16, bv16, [bQT])
                    inproj(1536, 512, dq)
                flush()
            wst.close()
            P.barrier()
            if DEBUG_STOP == "A":
                dbg = nc.dram_tensor("dbg", [128, 64], F32, kind="ExternalOutput").ap()
                P.dma("sp", out=dbg, in_=sm[:], reads=[b_sm, b_lp])
                P.barrier()
                break

            pbc = ExitStack()
            yT = P.sb(pbc, "yT", [128, 16, TOK], BF16)
            with ExitStack() as pb:
                KT = [P.sb(pb, "KT", [128, NKEY], BF16) for _ in range(2)]
                Vt = [P.sb(pb, "Vt", [128, NKT, 130], BF16) for _ in range(2)]
                PT = [P.sb(pb, "PT", [128, 2, 512], BF16) for _ in range(2)]
                osb = P.sb(pb, "osb", [128, 1, 4, 128], F32)
                y16 = P.sb(pb, "y16", [128, 4, 128], BF16)
                Pacc = P.sb(pb, "Pacc", [128, 2, 512], F32)
                rcs = P.sb(pb, "rcs", [128, 512], F32)
                onesf = P.sb(pb, "onesf", [128, 128], F32)
                bPacc, brcs, bones = [Buf("Pacc0"), Buf("Pacc1")], Buf("rcs"), Buf("onesf")
                PT4 = [P.sb(pb, "PT4", [128, 512], BF16) for _ in range(4)]
                pair16 = [P.sb(pb, "pair16", [128, 512], BF16) for _ in range(2)]
                bpair = [Buf("pair0"), Buf("pair1")]
                bPT4 = [Buf("PT4_%d" % i_) for i_ in range(4)]
                P.op("pool", lambda e: e.memset(onesf[:], 1.0), writes=[bones])
                if even:
                    oT = P.sb(pb, "oT", [128, 2, 512], F32)
                    dT = P.sb(pb, "dT", [128, 512], F32)
                    sqT = P.sb(pb, "sqT", [128, 512], F32)
                    subgc = P.sb(pb, "subgc", [128, 1], F32)
                    boT, bdT, bsqT, bsubg = Buf("oT"), Buf("dT"), Buf("sqT"), Buf("subgc")
                    P.dma("sp", out=subgc[:], in_=a_sub_g[li, :].rearrange("(p o) -> p o", o=1), writes=[bsubg])
                    P.op("dve", lambda e: e.tensor_scalar(out=subgc[:], in0=subgc[:], scalar1=(1.0 - lam_init), scalar2=None, op0=ALU.mult),
                         reads=[bsubg], writes=[bsubg])
                rc = P.sb(pb, "rc", [128, 16], F32)
                bK, bV, bPT = [Buf("K0"), Buf("K1")], [Buf("V0"), Buf("V1")], [Buf("PT0"), Buf("PT1")]
                bo, by16, brc, bkpe = (Buf(n) for n in ("osb", "y16", "rc", "kpe"))
                by16b = [Buf("y16b0"), Buf("y16b1")]
                for i in range(2):
                    P.op("pool", lambda e: e.memset(Vt[i][:, :, 128:130], 1.0), writes=[bV[i]])
                kvn = [0]
                pcn = [0]
                ocnt = [0]
                pend = [None]

                def load_k(dst, krow0, bdst):
                    P.dma("sp", out=dst[:, 0:4096].rearrange("p (s c) -> p s c", s=4), in_=gkr(krow0 // 128)[:, :, 0:1024],
                          reads=[bG], parts=[bdst])
                    P.dma("sp", out=dst[:, 4096:4352].rearrange("p (s c) -> p s c", s=4), in_=gkr(krow0 // 128)[:, :, 1024:1088],
                          reads=[bG], parts=[bdst])

                def load_kv(krow0, vcol0):
                    i = kvn[0] % 2
                    kvn[0] += 1
                    load_k(KT[i], krow0, bK[i])
                    for s_ in range(4):
                        for c in range(4):
                            P.dma("sp", out=Vt[i][:, s_ * 8 + 2 * c:s_ * 8 + 2 * c + 2, 0:128],
                                  in_=gv(s_, 2 * c, 256)[:, vcol0:vcol0 + 128].rearrange("(k p) c -> p k c", p=128), reads=[bG], parts=[bV[i]])
                        P.dma("sp", out=Vt[i][(s_ % 2) * 64:(s_ % 2) * 64 + 64, 32 + s_ // 2, 0:128],
                              in_=gv(s_, 8, 64)[:, vcol0:vcol0 + 128], reads=[bG], parts=[bV[i]])
                    return i

                def attend(pieces, rdK, Vi, kts, nq, sc, oi, sink_ap=None, masks=None, tilew=512):
                    nqs = max(1, nq // 128)
                    M = min(nq, 128)
                    gsz = 2 if tilew == 512 else 8
                    groups = [kts[a:a + gsz] for a in range(0, len(kts), gsz)]
                    ng = len(groups)
                    pbase = pcn[0]
                    pcn[0] += ng

                    def pv(gi, pview):
                        grp = groups[gi]
                        n = len(grp)
                        pk = (pbase + gi) % 2
                        for ki, kt in enumerate(grp):
                            for qs in range(nqs):
                                first = gi == 0 and ki == 0
                                last = gi == ng - 1 and ki == n - 1
                                ob = qs // 2
                                col = ob * 512 + (qs % 2) * 129
                                st_ = first and qs % 2 == 0
                                P.op("pe", lambda e: e.matmul(psB[0:M, col:col + 129], lhsT=pview(pk, ki, qs), rhs=Vt[Vi][:, kt, 0:129],
                                                              start=st_, stop=last, skip_group_check=True),
                                     reads=[bPT[pk], bV[Vi]], writes=[bB[ob]] if st_ else (), parts=() if st_ else [bB[ob]], sig=last)

                    if tilew == 512:
                        def qk(gi):
                            sg = gi % 2
                            for ki, kt in enumerate(groups[gi]):
                                bi = 2 * sg + ki
                                mm_group(psA[0:128, bi * 512:bi * 512 + nq], bA[bi], [(kf(kt), qa) for kf, qa in pieces], rdK)

                        def ex(gi):
                            sg = gi % 2
                            n = len(groups[gi])
                            pk = (pbase + gi) % 2
                            P.op("act", lambda e: e.activation(out=PT[pk][:, 0:n, 0:nq],
                                                               in_=psA[:, 2 * sg * 512:(2 * sg + n) * 512].rearrange("p (k c) -> p k c", c=512)[:, :, 0:nq],
                                                               func=AF.Exp, scale=sc),
                                 reads=[bA[2 * sg + a] for a in range(n)], writes=[bPT[pk]])

                        pview5 = lambda pk, ki, qs: PT[pk][:, ki, qs * 128:qs * 128 + M]
                        qk(0)
                        for gi in range(ng):
                            if gi + 1 < ng:
                                qk(gi + 1)
                            ex(gi)
                            pv(gi, pview5)
                    else:
                        for gi, grp in enumerate(groups):
                            n = len(grp)
                            pk = (pbase + gi) % 2
                            for ki, kt in enumerate(grp):
                                bi = ki // 4
                                mm_group(psA[0:128, ki * 128:ki * 128 + nq], bA[bi], [(kf(kt), qa) for kf, qa in pieces], rdK)
                            pflat = PT[pk][:].rearrange("p a b -> p (a b)")
                            P.op("act", lambda e: e.activation(out=pflat[:, 0:n * 128].rearrange("p (k c) -> p k c", c=128)[:, :, 0:nq],
                                                               in_=psA[:, 0:n * 128].rearrange("p (k c) -> p k c", c=128)[:, :, 0:nq],
                                                               func=AF.Exp, scale=sc),
                                 reads=[bA[0], bA[1]], writes=[bPT[pk]])
                            if masks is not None:
                                for ki, kt in enumerate(grp):
                                    if masks[ki] is not None:
                                        P.op("dve", lambda e: e.tensor_tensor(out=pflat[:, ki * 128:ki * 128 + nq], in0=pflat[:, ki * 128:ki * 128 + nq],
                                                                              in1=maskb[:, masks[ki], 0:nq], op=ALU.mult),
                                             reads=[bPT[pk], b_const], writes=[bPT[pk]])
                            pv(gi, lambda pk_, ki, qs: PT[pk_][:].rearrange("p a b -> p (a b)")[:, ki * 128:ki * 128 + M])
                    for qs in range(nqs):
                        ob = qs // 2
                        col = ob * 512 + (qs % 2) * 129
                        if sink_ap is None:
                            P.op("dve", lambda e: e.reciprocal(out=rc[0:M, qs:qs + 1], in_=psB[0:M, col + 128:col + 129]), reads=[bB[ob]], writes=[brc])
                        else:
                            P.op("dve", lambda e: e.tensor_tensor(out=rc[0:M, 8 + qs:9 + qs], in0=psB[0:M, col + 128:col + 129], in1=sink_ap, op=ALU.add),
                                 reads=[bB[ob], b_sm], writes=[brc])
                            P.op("dve", lambda e: e.reciprocal(out=rc[0:M, qs:qs + 1], in_=rc[0:M, 8 + qs:9 + qs]), reads=[brc], writes=[brc])
                        P.op("dve", lambda e: e.tensor_scalar(out=osb[0:M, oi, qs, :], in0=psB[0:M, col:col + 128], scalar1=rc[0:M, qs:qs + 1],
                                                              scalar2=None, op0=ALU.mult), reads=[bB[ob], brc], parts=[bo])

                def emit_y(src, bsrc, chunk, q0, nq):
                    nqs = max(1, nq // 128)
                    M = min(nq, 128)
                    P.op("pool", lambda e: e.tensor_copy(out=y16[0:M, 0:nqs, :], in_=src[0:M, 0:nqs, :]), reads=[bsrc], writes=[by16])
                    pst, bps = transposes(lambda k: y16[0:M, k, :], nqs, M, [by16])
                    P.op("act", lambda e: e.copy(out=yT[:, chunk, q0:q0 + nq].rearrange("p (k m) -> p k m", m=M), in_=pst), reads=[bps], parts=[byT])

                def attendT(pieces, rdK, Vi, kts, nq, sc, fin):
                    nk = len(kts)
                    pbase = pcn[0]
                    pcn[0] += nk

                    ob = ocnt[0] % 2
                    ocnt[0] += 1
                    o_ap, o_buf = psB[:, ob * 512:ob * 512 + nq], bB[ob]
                    pa_ = Pacc[:, ob, 0:nq]
                    bpa_ = bPacc[ob]

                    def qk(k):
                        bi = k % 3
                        mm_group(psA[0:128, bi * 512:bi * 512 + nq], bA[bi], [(kf(kts[k]), qa) for kf, qa in pieces], rdK)

                    for k in range(min(2, nk)):
                        qk(k)
                    if pend[0] is not None:
                        pend[0]()
                        pend[0] = None
                    for k in range(nk):
                        if k + 2 < nk:
                            qk(k + 2)
                        bi = k % 3
                        pk = (pbase + k) % 4
                        P.op("act", lambda e: e.activation(out=PT4[pk][:, 0:nq], in_=psA[:, bi * 512:bi * 512 + nq], func=AF.Exp, scale=sc),
                             reads=[bA[bi]], writes=[bPT4[pk]])
                        first = k == 0
                        last = k == nk - 1
                        P.op("pe", lambda e: e.matmul(o_ap, lhsT=Vt[Vi][:, kts[k], 0:128], rhs=PT4[pk][:, 0:nq], start=first, stop=last),
                             reads=[bPT4[pk], bV[Vi]], writes=[o_buf] if first else (), parts=() if first else [o_buf], sig=last)
                        if k % 2 == 1:
                            pkp = (pbase + k - 1) % 4
                            j2 = (k // 2) % 2
                            P.op("dve", lambda e: e.tensor_tensor(out=pair16[j2][:, 0:nq], in0=PT4[pkp][:, 0:nq], in1=PT4[pk][:, 0:nq], op=ALU.add),
                                 reads=[bPT4[pkp], bPT4[pk]], writes=[bpair[j2]])
                            if k == 1:
                                P.op("dve", lambda e: e.tensor_copy(out=pa_, in_=pair16[j2][:, 0:nq]), reads=[bpair[j2]], writes=[bpa_])
                            else:
                                P.op("dve", lambda e: e.tensor_tensor(out=pa_, in0=pa_, in1=pair16[j2][:, 0:nq], op=ALU.add),
                                     reads=[bpair[j2], bpa_], writes=[bpa_])
                        elif k == nk - 1:
                            if k == 0:
                                P.op("dve", lambda e: e.tensor_copy(out=pa_, in_=PT4[pk][:, 0:nq]), reads=[bPT4[pk]], writes=[bpa_])
                            else:
                                P.op("dve", lambda e: e.tensor_tensor(out=pa_, in0=pa_, in1=PT4[pk][:, 0:nq], op=ALU.add),
                                     reads=[bPT4[pk], bpa_], writes=[bpa_])
                    def epi():
                        mm_group(psA[:, 1536:1536 + nq], bA[3], [(onesf[:], pa_)], [bones, bpa_])
                        P.op("dve", lambda e: e.reciprocal(out=rcs[:, 0:nq], in_=psA[:, 1536:1536 + nq]), reads=[bA[3]], writes=[brcs])
                        fin(o_ap, o_buf)
                    pend[0] = epi

                def flush_pend():
                    if pend[0] is not None:
                        pend[0]()
                        pend[0] = None

                ALLK = list(range(NKT))
                qblocks = [(0, 512, ALLK), (512, 512, ALLK)] + ([(1024, 64, [32, 33])] if with_ctx else [])
                if even:
                    sc = 64 ** -0.5
                    for h in range(8):
                        Ki = load_kv(h * 128, h * 128)
                        for q0, nq, kts in qblocks:
                            nqs = max(1, nq // 128)
                            M = min(nq, 128)
                            for pr in range(2):
                                attendT([(lambda kt: KT[Ki][:, kt * 128:(kt + 1) * 128], QT[:, h + 16 * pr, q0:q0 + nq])],
                                        [bK[Ki], bQT], Ki, kts, nq, sc,
                                        lambda o_ap, o_buf, pr=pr, nq=nq: P.op("dve", lambda e: e.tensor_tensor(out=oT[:, pr, 0:nq], in0=o_ap, in1=rcs[:, 0:nq], op=ALU.mult),
                                                                           reads=[o_buf, brcs], parts=[boT]))
                            def subnorm(h=h, q0=q0, nq=nq):
                                P.op("dve", lambda e: e.scalar_tensor_tensor(out=dT[:, 0:nq], in0=oT[:, 1, 0:nq], scalar=sm[:, 0:1], in1=oT[:, 0, 0:nq],
                                                                             op0=ALU.mult, op1=ALU.add), reads=[boT, b_sm], writes=[bdT])
                                P.op("pool", lambda e: e.tensor_tensor(out=sqT[:, 0:nq], in0=dT[:, 0:nq], in1=dT[:, 0:nq], op=ALU.mult), reads=[bdT], writes=[bsqT])
                                mm_group(psA[:, 1536:1536 + nq], bA[3], [(onesf[:], sqT[:, 0:nq])], [bones, bsqT])
                                P.op("act", lambda e: e.activation(out=sqT[:, 0:nq], in_=psA[:, 1536:1536 + nq], func=AF.Ln, bias=epst[:, 0:1], scale=1.0 / 128),
                                     reads=[bA[3], b_const], writes=[bsqT])
                                P.op("act", lambda e: e.activation(out=sqT[:, 0:nq], in_=sqT[:, 0:nq], func=AF.Exp, scale=-0.5), reads=[bsqT], writes=[bsqT])
                                P.op("dve", lambda e: e.scalar_tensor_tensor(out=yT[:, h, q0:q0 + nq], in0=dT[:, 0:nq], scalar=subgc[:, 0:1], in1=sqT[:, 0:nq],
                                                                             op0=ALU.mult, op1=ALU.mult), reads=[bdT, bsqT, bsubg], parts=[byT])
                            prev_ = pend[0]
                            pend[0] = (lambda prev_=prev_, sn_=subnorm: (prev_(), sn_()))
                    flush_pend()
                    sc = 128 ** -0.5
                    for hk in range(2):
                        Ki = kvn[0] % 2
                        kvn[0] += 1
                        kr = (8 + hk) * 128
                        vc = 1024 + hk * 128
                        P.dma("sp", out=KT[Ki][:, 0:1024], in_=lbk(8 + hk)[:, 0:1024], reads=[bLB], parts=[bK[Ki]])
                        P.dma("sp", out=KT[Ki][:, 1024:1536].rearrange("p (s c) -> p s c", s=4), in_=gkr(8 + hk)[:, :, 896:1024], reads=[bG], parts=[bK[Ki]])
                        P.dma("sp", out=KT[Ki][:, 1536:2048].rearrange("p (s c) -> p s c", s=4), in_=gkr(8 + hk)[:, :, 0:128], reads=[bG], parts=[bK[Ki]])
                        P.dma("sp", out=KT[Ki][:, 2048:2304].rearrange("p (s c) -> p s c", s=4), in_=gkr(8 + hk)[:, :, 1024:1088], reads=[bG], parts=[bK[Ki]])
                        for c in range(4):
                            P.dma("sp", out=Vt[Ki][:, 2 * c:2 * c + 2, 0:128], in_=lbv(2 * c, 256)[:, vc:vc + 128].rearrange("(k p) c -> p k c", p=128),
                                  reads=[bLB], parts=[bV[Ki]])
                        for s_ in range(4):
                            P.dma("sp", out=Vt[Ki][:, 8 + s_, 0:128], in_=gv(s_, 7, 128)[:, vc:vc + 128], reads=[bG], parts=[bV[Ki]])
                            P.dma("sp", out=Vt[Ki][:, 12 + s_, 0:128], in_=gv(s_, 0, 128)[:, vc:vc + 128], reads=[bG], parts=[bV[Ki]])
                            P.dma("sp", out=Vt[Ki][(s_ % 2) * 64:(s_ % 2) * 64 + 64, 16 + s_ // 2, 0:128], in_=gv(s_, 8, 64)[:, vc:vc + 128],
                                  reads=[bG], parts=[bV[Ki]])
                        units = []
                        for g in range(4):
                            h = hk * 4 + g
                            for t in range(8 + (1 if with_ctx else 0)):
                                if t == 8:
                                    kts, mk, nq = [16, 17], [None, None], 64
                                else:
                                    kts, mk, nq = [16, 17, t], [None, None, None], 128
                                    if t > 0:
                                        kts.append(t - 1); mk.append(0)
                                    else:
                                        kts += [8, 9, 10, 11]; mk += [2, 3, 4, 5]
                                    if t < 7:
                                        kts.append(t + 1); mk.append(1)
                                    else:
                                        kts += [12, 13, 14, 15]; mk += [6, 7, 8, 9]
                                units.append((h, t, kts, mk, nq, len(units)))

                        def b_stage1(u):
                            h, t, kts, mk, nq, idx = u
                            so = (idx % 2) * 2
                            pk = idx % 2
                            n = len(kts)
                            for ki, kt in enumerate(kts):
                                col = so * 512 + ki * 128
                                mm_group(psA[0:128, col:col + nq], bA[so + ki // 4],
                                         [(KT[Ki][:, kt * 128:(kt + 1) * 128], QT[:, 8 + h, t * 128:t * 128 + nq])], [bK[Ki], bQT])
                            pflat = PT[pk][:].rearrange("p a b -> p (a b)")
                            P.op("act", lambda e: e.activation(out=pflat[:, 0:n * 128].rearrange("p (k c) -> p k c", c=128)[:, :, 0:nq],
                                                               in_=psA[:, so * 512:so * 512 + n * 128].rearrange("p (k c) -> p k c", c=128)[:, :, 0:nq],
                                                               func=AF.Exp, scale=sc),
                                 reads=[bA[so], bA[so + 1]], writes=[bPT[pk]])
                            for ki, kt in enumerate(kts):
                                if mk[ki] is not None:
                                    P.op("pool" if ki % 2 else "dve",
                                         lambda e: e.tensor_tensor(out=pflat[:, ki * 128:ki * 128 + nq], in0=pflat[:, ki * 128:ki * 128 + nq],
                                                                   in1=maskb[:, mk[ki], 0:nq], op=ALU.mult),
                                         reads=[bPT[pk], b_const], writes=[bPT[pk]])

                        def b_stage2(u):
                            h, t, kts, mk, nq, idx = u
                            pk = idx % 2
                            ob = idx % 2
                            M = min(nq, 128)
                            n = len(kts)
                            pflat = PT[pk][:].rearrange("p a b -> p (a b)")
                            acc_ = psB[0:M, ob * 512:ob * 512 + 129]
                            mm_group(acc_, bB[ob], [(pflat[:, ki * 128:ki * 128 + M], Vt[Ki][:, kt, 0:129]) for ki, kt in enumerate(kts)], [bPT[pk], bV[Ki]])
                            P.op("dve", lambda e: e.tensor_tensor(out=rc[0:M, 8 + ob:9 + ob], in0=acc_[:, 128:129], in1=sm[0:M, 8 + h:9 + h], op=ALU.add),
                                 reads=[bB[ob], b_sm], writes=[brc])
                            P.op("dve", lambda e: e.reciprocal(out=rc[0:M, ob:ob + 1], in_=rc[0:M, 8 + ob:9 + ob]), reads=[brc], writes=[brc])
                            P.op("dve", lambda e: e.tensor_scalar(out=y16[0:M, ob, :], in0=acc_[:, 0:128], scalar1=rc[0:M, ob:ob + 1], scalar2=None, op0=ALU.mult),
                                 reads=[bB[ob], brc], writes=[by16b[ob]])
                            pst, bps = transposes(lambda k_: y16[0:M, ob, :], 1, M, [by16b[ob]])
                            P.op("act", lambda e: e.copy(out=yT[:, 8 + h, t * 128:t * 128 + nq].rearrange("p (k m) -> p k m", m=M), in_=pst), reads=[bps], parts=[byT])

                        b_stage1(units[0])
                        for ui in range(len(units)):
                            if ui + 1 < len(units):
                                b_stage1(units[ui + 1])
                            b_stage2(units[ui])
                else:
                    sc = 128 ** -0.5
                    for hk in range(2):
                        Ki = load_kv(hk * 128, hk * 128)
                        for g in range(4):
                            h = hk * 4 + g
                            for q0, nq, kts in qblocks:
                                attendT([(lambda kt: KT[Ki][:, kt * 128:(kt + 1) * 128], QT[:, h, q0:q0 + nq])], [bK[Ki], bQT], Ki, kts, nq, sc,
                                        lambda o_ap, o_buf, h=h, q0=q0, nq=nq: P.op("dve", lambda e: e.tensor_tensor(out=yT[:, h, q0:q0 + nq], in0=o_ap, in1=rcs[:, 0:nq], op=ALU.mult),
                                                                                     reads=[o_buf, brcs], parts=[byT]))
                    kpeL = P.sb(pb, "kpeL", [128, NKEY], BF16)
                    kpeH = P.sb(pb, "kpeH", [128, NKEY], BF16)
                    P.op("pool", lambda e: e.memset(kpeL[64:128, :], 0.0), parts=[bkpe])
                    P.op("pool", lambda e: e.memset(kpeH[0:64, :], 0.0), parts=[bkpe])
                    for kx, r0 in ((kpeL, 0), (kpeH, 64)):
                        P.dma("sp", out=kx[r0:r0 + 64, 0:4096].rearrange("p (s c) -> p s c", s=4), in_=gkr(10)[r0:r0 + 64, :, 0:1024],
                              reads=[bG], parts=[bkpe])
                        P.dma("sp", out=kx[r0:r0 + 64, 4096:4352].rearrange("p (s c) -> p s c", s=4), in_=gkr(10)[r0:r0 + 64, :, 1024:1088],
                              reads=[bG], parts=[bkpe])
                    sc = 192 ** -0.5
                    for h in range(8):
                        Ki = load_kv(256 + h * 128, 256 + h * 128)
                        kpx = kpeL if h % 2 == 0 else kpeH
                        for q0, nq, kts in qblocks:
                            attendT([(lambda kt: KT[Ki][:, kt * 128:(kt + 1) * 128], QT[:, 8 + h, q0:q0 + nq]),
                                     (lambda kt: kpx[:, kt * 128:(kt + 1) * 128], QT[:, 16 + h // 2, q0:q0 + nq])],
                                    [bK[Ki], bkpe, bQT], Ki, kts, nq, sc,
                                    lambda o_ap, o_buf, h=h, q0=q0, nq=nq: P.op("dve", lambda e: e.tensor_tensor(out=yT[:, 8 + h, q0:q0 + nq], in0=o_ap, in1=rcs[:, 0:nq], op=ALU.mult),
                                                                                 reads=[o_buf, brcs], parts=[byT]))
                    flush_pend()
            P.barrier()
            if DEBUG_STOP == "B":
                pbc.close()
            if DEBUG_STOP == "B":
                dbg = nc.dram_tensor("dbg", [128, 64], F32, kind="ExternalOutput").ap()
                P.dma("sp", out=dbg, in_=sm[:], reads=[b_sm, b_lp])
                P.barrier()
                break

            with ExitStack() as pc_:
                nwb[0] = 2
                wcnt[0] = 0
                wb[0] = P.sb(pc_, "wb0", [128, 16, 512], BF16)
                wb[1] = P.sb(pc_, "wb1", [128, 16, 512], BF16)
                gate_bc = P.sb(pc_, "gate_bc", [128, 2, DM], F32)
                gs = P.sb(pc_, "gs", [128, 512], BF16)
                xp = [P.sb(pc_, "xp", [128, 512], F32) for _ in range(2)]
                xg = [P.sb(pc_, "xg", [128, 512], F32) for _ in range(2)]
                bgs, bxp, bxg = Buf("gs"), [Buf("xp0"), Buf("xp1")], [Buf("xg0"), Buf("xg1")]
                b_gate.w, b_gate.wf, b_gate.r = {}, {}, {}
                load_gate(l, gate_bc)
                ntl = NTILE if with_ctx else 8
                tgroups = [(0, 4, 512), (4, 4, 512)] + ([(8, 1, 64)] if with_ctx else [])
                ai = 0
                cjobs = [("g", k_) for k_ in range(4)] + [("o", k_) for k_ in range(4)]

                def cload(j):
                    kind, k_ = cjobs[j]
                    return load_w(w_in, GATE0 + k_ * 512, 512) if kind == "g" else load_w(w_o[l], k_ * 512, 512)
                cw = {0: cload(0)}
                for blk in range(4):
                    cw[blk + 1] = cload(blk + 1)
                    wt, bw = cw[blk]
                    for mm in range(4):
                        m = blk * 4 + mm
                        for t0, nt, ntok in tgroups:
                            i = ai % 2
                            ai += 1
                            if nt == 4:
                                rhs_f = lambda j: hT[:, t0:t0 + 4, j, :]
                            else:
                                rhs_f = lambda j: hT[:, 8, j, 0:64]
                            mm_group(psB[:, i * 512:i * 512 + ntok], bB[i], [(wt[:, j, mm * 128:(mm + 1) * 128], rhs_f(j)) for j in range(16)],
                                     [bw] + bhT)
                            P.op("act", lambda e: e.activation(out=gs[:, 0:ntok], in_=psB[:, i * 512:i * 512 + ntok], func=AF.Silu),
                                 reads=[bB[i]], writes=[bgs])
                            P.op("dve", lambda e: e.tensor_tensor(out=yT[:, m, t0 * 128:t0 * 128 + ntok], in0=yT[:, m, t0 * 128:t0 * 128 + ntok],
                                                                  in1=gs[:, 0:ntok], op=ALU.mult), reads=[bgs, byT], parts=[byT])
                for nb in range(4):
                    if nb + 1 < 4:
                        cw[4 + nb + 1] = cload(4 + nb + 1)
                    wt, bw = cw[4 + nb]
                    for t in range(ntl):
                        rows = 128 if t < 8 else 64
                        i = ai % 2
                        ai += 1
                        if l == 0:
                            src = x_in[t * 128:(t + 1) * 128, nb * 512:(nb + 1) * 512] if t < 8 else ctx_in[:, nb * 512:(nb + 1) * 512]
                            rd = []
                        else:
                            src = xs[t * 128:t * 128 + rows, nb * 512:(nb + 1) * 512]
                            rd = [bxs[t][nb]]
                        P.dma("sp", out=xp[i][0:rows, :], in_=src, reads=rd, writes=[bxp[i]])
                        mm_group(psB[0:rows, i * 512:(i + 1) * 512], bB[i],
                                 [(yT[:, m, t * 128:t * 128 + rows], wt[:, m, :]) for m in range(16)], [bw, byT])
                        grow = 0 if t < 8 else 1
                        P.op("dve", lambda e: e.tensor_tensor(out=xg[i][0:rows, :], in0=psB[0:rows, i * 512:(i + 1) * 512],
                                                              in1=gate_bc[0:rows, grow, nb * 512:(nb + 1) * 512], op=ALU.mult),
                             reads=[b_gate, bB[i]], writes=[bxg[i]])
                        P.op("pool", lambda e: e.tensor_tensor(out=xp[i][0:rows, :], in0=xp[i][0:rows, :], in1=xg[i][0:rows, :], op=ALU.add),
                             reads=[bxg[i]], writes=[bxp[i]])
                        P.dma("sp", out=xs[t * 128:t * 128 + rows, nb * 512:(nb + 1) * 512], in_=xp[i][0:rows, :], reads=[bxp[i]], writes=[bxs[t][nb]])
            pbc.close()
            P.barrier()
      except _Stop:
        P.barrier()
        dbg = nc.dram_tensor("dbg", [128, 64], F32, kind="ExternalOutput").ap()
        P.dma("sp", out=dbg, in_=sm[:], reads=[b_sm, b_lp])
        P.barrier()
        return P

    with ExitStack() as pf:
        fg = P.sb(pf, "fg", [128, DM], F32)
        xt = [P.sb(pf, "fxt", [128, DM], F32) for _ in range(2)]
        xq = [P.sb(pf, "fxq", [128, DM], F32) for _ in range(2)]
        ss = P.sb(pf, "fss", [128, 4], F32)
        bfg, bss, bxt, bxq = Buf("fg"), Buf("fss"), [Buf("a"), Buf("b")], [Buf("c"), Buf("d")]
        bout = Buf("out")
        P.dma("sp", out=fg[:], in_=final_g.broadcast_to([128, DM]), writes=[bfg])
        for t in range(8):
            i = t % 2
            P.dma("sp", out=xt[i][:], in_=xs[t * 128:(t + 1) * 128, :], reads=bxs[t], writes=[bxt[i]])
            P.op("pool", lambda e: e.tensor_tensor(out=xq[i][:], in0=xt[i][:], in1=xt[i][:], op=ALU.mult), reads=[bxt[i]], writes=[bxq[i]])
            P.op("dve", lambda e: e.reduce_sum(out=ss[:, 0:1], in_=xq[i][:], axis=AX.X), reads=[bxq[i]], writes=[bss])
            rstd(ss[:, 2:3], ss[:, 0:1], ss[:, 1:2], DM, bss)
            P.op("dve", lambda e: e.scalar_tensor_tensor(out=xq[i][:], in0=xt[i][:], scalar=ss[:, 2:3], in1=fg[:], op0=ALU.mult, op1=ALU.mult),
                 reads=[bxt[i], bss, bfg], writes=[bxq[i]])
            P.dma("sp", out=out[t * 128:(t + 1) * 128, :], in_=xq[i][:], reads=[bxq[i]], parts=[bout])
    P.barrier()
    return P


DEBUG_STOP = None


def _rope_tables(r):
    tok = r * 1024 + np.arange(1024)
    row = (tok // 64).astype(np.float32)
    col = (tok % 64).astype(np.float32)
    tabs = {}
    for dim in (64, 128):
        nf = dim // 4
        inv = (np.float32(10000.0) ** (-np.arange(nf, dtype=np.float32) / np.float32(nf))).astype(np.float32)
        ang = np.concatenate([row[:, None] * inv[None], col[:, None] * inv[None]], axis=-1).astype(np.float32)
        tabs[dim] = (np.cos(ang).astype(np.float32), np.sin(ang).astype(np.float32))
    t = np.zeros((1024, 288), np.float32)
    t[:, 0:32] = tabs[64][0]
    t[:, 32:96] = tabs[128][0]
    t[:, 96:128] = tabs[64][1]
    t[:, 128:192] = tabs[128][1]
    t[:, 192:224] = -tabs[64][1]
    t[:, 224:288] = -tabs[128][1]
    return np.ascontiguousarray(t.reshape(8, 128, 288).transpose(1, 0, 2).reshape(128, 8 * 288))


def _masks(r):
    i = np.arange(128)[:, None]
    j = np.arange(128)[None, :]
    triP = (i >= j).astype(np.float32)
    triN = (i <= j).astype(np.float32)
    m = np.zeros((128, 10, 128), np.float32)
    m[:, 0] = triP
    m[:, 1] = triN
    for s in range(4):
        if s == r - 1:
            m[:, 2 + s] = triP
        if s == r + 1:
            m[:, 6 + s] = triN
    return np.ascontiguousarray(m.reshape(128, 1280))


def make_in_maps(inp):
    f = lambda a: np.ascontiguousarray(np.asarray(a, dtype=np.float32))
    x, c, ctx, c_ctx = f(inp["x"]), f(inp["c"]), f(inp["ctx"]), f(inp["c_ctx"])
    w_mod, b_mod = f(inp["w_mod"]), f(inp["b_mod"])
    shared = {
        "norm_g": f(inp["norm_g"]), "final_g": f(inp["final_g"]).reshape(1, DM), "w_o": f(inp["w_o"]),
        "e_w_in": f(inp["e_w_in"]), "o_w_in": f(inp["o_w_in"]),
        "a_lam": np.ascontiguousarray(np.concatenate([f(inp["a_lam_q1"]), f(inp["a_lam_k1"]), f(inp["a_lam_q2"]), f(inp["a_lam_k2"])], axis=1)),
        "a_sub_g": f(inp["a_sub_g"]), "b_sink": f(inp["b_sink"]), "c_q_g": f(inp["c_q_g"]), "c_k_g": f(inp["c_k_g"]),
        "d_q_a_g": f(inp["d_q_a_g"]), "d_kv_a_g": f(inp["d_kv_a_g"]), "d_w_q_b": f(inp["d_w_q_b"]), "d_w_kv_b": f(inp["d_w_kv_b"]),
        "ident": np.eye(128, dtype=np.float32),
    }
    wm_sl = [np.ascontiguousarray(w_mod[:, :, r * 1536:(r + 1) * 1536]) for r in range(4)]
    bm_sl = [np.ascontiguousarray(b_mod[:, r * 1536:(r + 1) * 1536]) for r in range(4)]
    maps = []
    for core in range(8):
        b, r = divmod(core, 4)
        m = dict(shared)
        m["x"] = np.ascontiguousarray(x[b, r * 1024:(r + 1) * 1024])
        m["ctx"] = np.ascontiguousarray(ctx[b, r * 64:(r + 1) * 64])
        m["cvec"] = np.ascontiguousarray(np.stack([c[b], c_ctx]))
        m["wmod"] = wm_sl[r]
        m["bmod"] = bm_sl[r]
        m["rope"] = _rope_tables(r)
        m["bmask"] = _masks(r)
        maps.append(m)
    return maps


_NC_CACHE = {}


def kernel(**inputs):
    if "nc" not in _NC_CACHE:
        _NC_CACHE["nc"] = build(4).nc
    maps = make_in_maps(inputs)
    res = run_bass_kernel_spmd(_NC_CACHE["nc"], maps, core_ids=list(range(8)))
    out = np.zeros((2, 4096, DM), np.float32)
    for core in range(8):
        b, r = divmod(core, 4)
        out[b, r * 1024:(r + 1) * 1024] = res.results[core]["out"]
    return out
```

```python
import numpy as np
from contextlib import ExitStack
import concourse.bass as bass
import concourse.mybir as mybir
from concourse.bass_utils import run_bass_kernel_spmd

F32 = mybir.dt.float32
BF16 = mybir.dt.bfloat16
AF = mybir.ActivationFunctionType
ALU = mybir.AluOpType
AX = mybir.AxisListType

DM = 2048
NTILE = 9
TOK = 1088
EPS = 1e-6
NKEY = 4352
NKT = 34


class Buf:
    def __init__(self, name):
        self.name = name
        self.w = {}
        self.wf = {}
        self.r = {}


class Prog:
    def __init__(self):
        self.nc = bass.Bass("TRN2", target_bir_lowering=False)
        self.st = ExitStack()
        nc = self.nc
        self.E = {"pe": nc.tensor, "act": nc.scalar, "dve": nc.vector, "pool": nc.gpsimd, "sp": nc.sync}
        self.sem, self.inc, self.seq = {}, {}, {}
        for k in ("pe", "act", "dve", "pool", "cc"):
            self.sem[k] = self.st.enter_context(nc.semaphore(k))
            self.inc[k] = 1
            self.seq[k] = 0
        self.nslots = {"sp": 16, "pool": 8, "act": 4}
        self.slot = {"sp": 0, "pool": 0, "act": 0}
        for q, n in self.nslots.items():
            for i in range(n):
                k = "%s_d%d" % (q, i)
                self.sem[k] = self.st.enter_context(nc.semaphore(k))
                self.inc[k] = 16
                self.seq[k] = 0
        self.known = {k: {} for k in self.E}
        self.nwait = 0

    def _wait(self, eng, deps):
        for e, s in deps.items():
            if s <= 0:
                continue
            if eng == "pe" and e == "pe":
                continue
            if self.known[eng].get(e, 0) >= s:
                continue
            self.E[eng].wait_ge(self.sem[e], s * self.inc[e])
            self.known[eng][e] = s
            self.nwait += 1

    @staticmethod
    def _deps(reads, writes, parts):
        deps = {}

        def add(d):
            for e, s in d.items():
                if deps.get(e, 0) < s:
                    deps[e] = s
        for b in reads:
            add(b.w)
        for b in writes:
            add(b.w)
            add(b.r)
        for b in parts:
            add(b.r)
            add(b.wf)
        return deps

    @staticmethod
    def _commit(e, s, reads, writes, parts):
        for b in reads:
            if b.r.get(e, 0) < s:
                b.r[e] = s
        for b in writes:
            b.w = {e: s}
            b.wf = {e: s}
            b.r = {}
        for b in parts:
            if b.w.get(e, 0) < s:
                b.w[e] = s

    def op(self, eng, fn, reads=(), writes=(), parts=(), sig=True):
        self._wait(eng, self._deps(reads, writes, parts))
        inst = fn(self.E[eng])
        if sig:
            self.seq[eng] += 1
            inst.then_inc(self.sem[eng], 1)
            s = self.seq[eng]
        else:
            s = self.seq[eng] + 1
        self._commit(eng, s, reads, writes, parts)
        return inst

    def dma(self, q, out, in_, reads=(), writes=(), parts=(), **kw):
        k = "%s_d%d" % (q, self.slot[q])
        self.slot[q] = (self.slot[q] + 1) % self.nslots[q]
        deps = self._deps(reads, writes, parts)
        if deps.get(k, 0) < self.seq[k]:
            deps[k] = self.seq[k]
        self._wait(q, deps)
        inst = self.E[q].dma_start(out=out, in_=in_, **kw)
        self.seq[k] += 1
        inst.then_inc(self.sem[k], 16)
        self._commit(k, self.seq[k], reads, writes, parts)
        return inst

    def barrier(self):
        allv = dict(self.seq)
        for eng in self.E:
            self._wait(eng, allv)

    def sb(self, stack, name, shape, dt):
        self.uid = getattr(self, "uid", 0) + 1
        return stack.enter_context(self.nc.sbuf_tensor("%s_%d" % (name, self.uid), list(shape), dt))

    def ps(self, stack, name, shape, dt):
        return stack.enter_context(self.nc.psum_tensor(name, list(shape), dt))


class _Stop(Exception):
    pass


DEBUG_NSTEP = None


def build(n_layers=4):
    P = Prog()

    def chk(k):
        if DEBUG_NSTEP == k:
            raise _Stop()

    nc = P.nc
    st = P.st

    def din(name, shape):
        return nc.dram_tensor(name, list(shape), F32, kind="ExternalInput").ap()

    x_in = din("x", [1024, DM])
    ctx_in = din("ctx", [64, DM])
    cvec = din("cvec", [2, DM])
    wmod = din("wmod", [4, DM, 1536])
    bmod = din("bmod", [4, 1536])
    norm_g = din("norm_g", [4, DM])
    final_g = din("final_g", [1, DM])
    w_o = din("w_o", [4, DM, DM])
    e_w = din("e_w_in", [2, DM, 6656])
    o_w = din("o_w_in", [2, DM, 4416])
    a_lam = din("a_lam", [2, 256])
    a_sub_g = din("a_sub_g", [2, 128])
    b_sink = din("b_sink", [2, 8])
    c_q_g = din("c_q_g", [2, 128])
    c_k_g = din("c_k_g", [2, 128])
    d_q_a_g = din("d_q_a_g", [2, 512])
    d_kv_a_g = din("d_kv_a_g", [2, 256])
    d_w_q_b = din("d_w_q_b", [2, 512, 1536])
    d_w_kv_b = din("d_w_kv_b", [2, 256, 2048])
    rope_in = din("rope", [128, 8 * 288])
    ident_in = din("ident", [128, 128])
    bmask_in = din("bmask", [128, 10 * 128])
    out = nc.dram_tensor("out", [1024, DM], F32, kind="ExternalOutput").ap()

    xs = nc.dram_tensor("xs", [TOK, DM], F32).ap()
    modloc = nc.dram_tensor("modloc", [8, 1536], F32)
    modall = nc.dram_tensor("modall", [32, 1536], F32)
    GROUPS = [[0, 1, 2, 3], [4, 5, 6, 7]]

    ccn = [0]

    def allgather(src_t, dst_t, reads, writes):
        k = "cc%d" % ccn[0]
        ccn[0] += 1
        P.sem[k] = st.enter_context(nc.semaphore(k))
        P.inc[k] = 1
        P.seq[k] = 0
        P._wait("pool", P._deps(reads, (), writes))
        inst = nc.gpsimd.collective_compute("AllGather", ALU.bypass, replica_groups=GROUPS,
                                            ins=[src_t.ap().opt()], outs=[dst_t.ap().opt()])
        inst.then_inc(P.sem[k])
        P.seq[k] = 1
        P._commit(k, 1, reads, (), writes)

    identb = P.sb(st, "identb", [128, 128], BF16)
    maskb = P.sb(st, "maskb", [128, 10, 128], BF16)
    ropet = P.sb(st, "ropet", [128, 8, 288], F32)
    modv = P.sb(st, "modv", [128, 4, 5, 16], F32)
    hT = P.sb(st, "hT", [128, NTILE, 16, 128], BF16)
    wb = [None, None, None]
    nwb = [2]
    lp = P.sb(st, "lp", [128, 1280], F32)
    sm = P.sb(st, "sm", [128, 64], F32)
    b_const = Buf("const")
    b_modv = Buf("modv")
    b_gate = Buf("gate")
    bhT = [Buf("hT%d" % t) for t in range(NTILE)]
    bwb = [Buf("wb0"), Buf("wb1"), Buf("wb2")]
    b_lp = Buf("lp")
    b_sm = Buf("sm")
    wcnt = [0]

    psA = P.ps(st, "psA", [128, 2048], F32)
    psB = P.ps(st, "psB", [128, 1024], F32)
    psT = P.ps(st, "psT", [128, 2, 1024], BF16)
    bA = [Buf("psA%d" % i) for i in range(4)]
    bB = [Buf("psB%d" % i) for i in range(2)]
    bT = [Buf("psT%d" % i) for i in range(2)]
    tcnt = [0]

    bxs = [[Buf("xs%d_%d" % (t, nb)) for nb in range(4)] for t in range(NTILE)]

    P.dma("pool", out=identb[:], in_=ident_in, writes=[b_const])
    P.dma("pool", out=maskb[:], in_=bmask_in.rearrange("p (m k) -> p m k", k=128), parts=[b_const])
    P.dma("sp", out=ropet[:], in_=rope_in.rearrange("p (t k) -> p t k", k=288), parts=[b_const])

    def load_w(src2d, c0, ncols):
        i = wcnt[0] % nwb[0]
        wcnt[0] += 1
        P.dma("pool", out=wb[i][:, :, 0:ncols],
              in_=src2d[:, c0:c0 + ncols].rearrange("(j p) c -> p j c", p=128), writes=[bwb[i]])
        return wb[i], bwb[i]

    with ExitStack() as ph:
        cT = P.sb(ph, "cT", [128, 2, 16], F32)
        scT = P.sb(ph, "scT", [128, 2, 16], F32)
        wm = [P.sb(ph, "wm%d" % i, [128, 8, 512], F32) for i in range(2)]
        rowt = P.sb(ph, "rowt", [64, 128], F32)
        identf = P.sb(ph, "identf", [128, 128], F32)
        b_rowt = Buf("rowt")
        P.dma("sp", out=identf[:], in_=ident_in, parts=[b_const])
        modsb = P.sb(ph, "modsb", [2, 1536], F32)
        bmsb = P.sb(ph, "bmsb", [2, 1536], F32)
        b_cT, b_scT, b_modsb, b_bmsb, b_modloc, b_modall = (Buf(n) for n in ("cT", "scT", "modsb", "bmsb", "modloc", "modall"))
        bwm = [Buf("wm0"), Buf("wm1")]
        P.dma("sp", out=rowt[0:32, :], in_=cvec.rearrange("r (j p) -> (r j) p", p=128), writes=[b_rowt])
        P.op("pe", lambda e: e.transpose(out=psB[:, 0:32], in_=rowt[0:32, :], identity=identf[0:32, 0:32]),
             reads=[b_rowt, b_const], writes=[bB[0]])
        P.op("dve", lambda e: e.tensor_copy(out=cT[:].rearrange("p a b -> p (a b)"), in_=psB[:, 0:32]), reads=[bB[0]], writes=[b_cT])
        P.op("act", lambda e: e.activation(out=scT[:], in_=cT[:], func=AF.Silu), reads=[b_cT], writes=[b_scT])
        k = 0
        for l in range(4):
            P.dma("sp", out=bmsb[:], in_=bmod[l:l + 1, :].broadcast_to([2, 1536]),
                  writes=[b_bmsb])
            for pc in range(3):
                for hf in range(2):
                    i = k % 2
                    k += 1
                    P.dma("sp", out=wm[i][:],
                          in_=wmod[l, hf * 1024:(hf + 1) * 1024, pc * 512:(pc + 1) * 512].rearrange("(j p) c -> p j c", p=128),
                          writes=[bwm[i]])
                    for j in range(8):
                        first = hf == 0 and j == 0
                        last = hf == 1 and j == 7
                        P.op("pe", lambda e: e.matmul(psB[0:2, 0:512], lhsT=scT[:, :, hf * 8 + j], rhs=wm[i][:, j, :],
                                                      start=first, stop=last),
                             reads=[b_scT, bwm[i]], writes=[bB[0]] if first else (), parts=() if first else [bB[0]], sig=last)
                P.op("dve", lambda e: e.tensor_tensor(out=modsb[:, pc * 512:(pc + 1) * 512], in0=psB[0:2, 0:512],
                                                      in1=bmsb[:, pc * 512:(pc + 1) * 512], op=ALU.add),
                     reads=[bB[0], b_bmsb], parts=[b_modsb])
            P.dma("sp", out=modloc.ap()[l * 2:(l + 1) * 2, :], in_=modsb[:], reads=[b_modsb], parts=[b_modloc])
        allgather(modloc, modall, [b_modloc], [b_modall])

        def mod_pieces(col0, n):
            res = []
            off = 0
            while off < n:
                col = col0 + off
                rk, cc = divmod(col, 1536)
                ln = min(n - off, 1536 - cc)
                res.append((rk, cc, ln, off))
                off += ln
            return res

        for l in range(4):
            for row in range(2):
                for rk, cc, ln, off in mod_pieces(0, 4096):
                    P.dma("sp", out=rowt[off // 128:(off + ln) // 128, :],
                          in_=modall.ap()[rk * 8 + l * 2 + row, cc:cc + ln].rearrange("(j p) -> j p", p=128),
                          reads=[b_modall], parts=[b_rowt])
                P.op("pe", lambda e: e.transpose(out=psB[:, 0:32], in_=rowt[0:32, :], identity=identf[0:32, 0:32]),
                     reads=[b_rowt, b_const], writes=[bB[0]])
                P.op("dve", lambda e: e.tensor_copy(out=modv[:, l, row * 2:row * 2 + 2, :].rearrange("p a b -> p (a b)"), in_=psB[:, 0:32]),
                     reads=[bB[0]], parts=[b_modv])
        P.dma("sp", out=rowt[0:64, :], in_=norm_g.rearrange("l (j p) -> (l j) p", p=128), writes=[b_rowt])
        P.op("pe", lambda e: e.transpose(out=psB[:, 0:64], in_=rowt[0:64, :], identity=identf[0:64, 0:64]),
             reads=[b_rowt, b_const], writes=[bB[0]])
        for l in range(4):
            P.op("dve", lambda e: e.tensor_copy(out=modv[:, l, 4, :], in_=psB[:, l * 16:(l + 1) * 16]), reads=[bB[0]], parts=[b_modv])
        for idx in (1, 3):
            P.op("dve", lambda e: e.scalar_tensor_tensor(out=modv[:, :, idx, :], in0=modv[:, :, idx, :], scalar=1.0,
                                                         in1=modv[:, :, 4, :], op0=ALU.add, op1=ALU.mult),
                 reads=[b_modv], writes=[b_modv])

        def load_gate(l, gate_bc):
            for row in range(2):
                for rk, cc, ln, off in mod_pieces(4096, 2048):
                    P.dma("sp", out=gate_bc[:, row, off:off + ln],
                          in_=modall.ap()[rk * 8 + l * 2 + row:rk * 8 + l * 2 + row + 1, cc:cc + ln].broadcast_to([128, ln]),
                          reads=[b_modall], parts=[b_gate])
        P.barrier()

    if DEBUG_STOP == "P":
        dbg = nc.dram_tensor("dbg", [128, 320], F32, kind="ExternalOutput").ap()
        P.dma("sp", out=dbg, in_=modv[:].rearrange("p a b c -> p (a b c)"), reads=[b_modv])
        P.barrier()
        return P
    def mm_group(out_ap, bout, pairs, rd):
        n = len(pairs)
        for i, (lt, rh) in enumerate(pairs):
            P.op("pe", lambda e: e.matmul(out_ap, lhsT=lt, rhs=rh, start=(i == 0), stop=(i == n - 1)),
                 reads=rd, writes=[bout] if i == 0 else (), parts=() if i == 0 else [bout], sig=(i == n - 1))

    def transposes(src_fn, n, M, rd):
        i = tcnt[0] % 2
        tcnt[0] += 1
        for k in range(n):
            P.op("pe", lambda e: e.transpose(out=psT[:, i, k * 128:k * 128 + M], in_=src_fn(k), identity=identb[0:M, 0:M]),
                 reads=rd + [b_const], writes=[bT[i]] if k == 0 else (), parts=() if k == 0 else [bT[i]], sig=(k == n - 1))
        return psT[:, i, 0:n * 128].rearrange("p (h k) -> p h k", k=128)[:, :, 0:M], bT[i]

    epst = P.sb(st, "epst", [128, 1], F32)
    P.op("pool", lambda e: e.memset(epst[:], EPS), writes=[b_const])

    def rstd(out_ap, sum_ap, tmp_ap, n, buf):
        P.op("act", lambda e: e.activation(out=tmp_ap, in_=sum_ap, func=AF.Sqrt, bias=epst[0:out_ap.shape[0], 0:1], scale=1.0 / n),
             reads=[buf, b_const], writes=[buf])
        P.op("dve", lambda e: e.reciprocal(out=out_ap, in_=tmp_ap), reads=[buf], writes=[buf])

    for l in range(n_layers):
      try:
        even = (l % 2 == 0)
        li = l // 2
        with_ctx = l < n_layers - 1
        lam_init = 0.8 - 0.6 * float(np.exp(-0.3 * l))
        w_in = e_w[li] if even else o_w[li]
        NC_IN = 6656 if even else 4416
        GATE0 = NC_IN - 2048
        if even:
            KR, NVC, NQH = 1280, 1280, 24
        else:
            KR, NVC, NQH = 1408, 1280, 20
        R = KR + NVC
        bLB, bG = Buf("LB"), Buf("G")
        nkh_l = 10 if even else 11
        kchunks = [(c * 3, min(3, nkh_l - c * 3)) for c in range((nkh_l + 2) // 3)]
        LBK = [nc.dram_tensor("LBK%d_%d" % (l, c), [nh * 128, TOK], BF16) for c, (h0, nh) in enumerate(kchunks)]
        GK = [nc.dram_tensor("GK%d_%d" % (l, c), [4 * nh * 128, TOK], BF16) for c, (h0, nh) in enumerate(kchunks)]
        VCH = [(0, 256), (256, 256), (512, 256), (768, 320)]
        LBV = [nc.dram_tensor("LBV%d_%d" % (l, c), [n, NVC], BF16) for c, (t0, n) in enumerate(VCH)]
        GV = [nc.dram_tensor("GV%d_%d" % (l, c), [4 * n, NVC], BF16) for c, (t0, n) in enumerate(VCH)]

        def lbk(h):
            return LBK[h // 3].ap()[(h % 3) * 128:(h % 3) * 128 + 128, :]

        def gkr(h):
            return GK[h // 3].ap().rearrange("(s r) c -> r s c", s=4)[(h % 3) * 128:(h % 3) * 128 + 128]

        def vch(t):
            c = min(t // 2, 3)
            return c, t * 128 - VCH[c][0]

        def lbv(t, rows):
            c, r0 = vch(t)
            return LBV[c].ap()[r0:r0 + rows, :]

        def gv(s, t, rows):
            c, r0 = vch(t)
            n = VCH[c][1]
            return GV[c].ap()[s * n + r0:s * n + r0 + rows, :]

        with ExitStack() as ls:
            QT = P.sb(ls, "QT", [128, NQH, TOK], BF16)
            bQT, byT = Buf("QT"), Buf("yT")
            if even:
                P.op("pool", lambda e: e.memset(QT[:], 0.0), writes=[bQT])
            if even:
                P.dma("sp", out=lp[:, 0:256], in_=a_lam[li:li + 1, :].broadcast_to([128, 256]), parts=[b_lp])
                P.dma("sp", out=lp[:, 256:384], in_=a_sub_g[li:li + 1, :].broadcast_to([128, 128]), parts=[b_lp])
                P.dma("sp", out=lp[:, 384:392], in_=b_sink[li:li + 1, :].broadcast_to([128, 8]), parts=[b_lp])
                P.op("dve", lambda e: e.tensor_tensor(out=lp[:, 512:576], in0=lp[:, 0:64], in1=lp[:, 64:128], op=ALU.mult),
                     reads=[b_lp], parts=[b_lp])
                P.op("dve", lambda e: e.tensor_tensor(out=lp[:, 576:640], in0=lp[:, 128:192], in1=lp[:, 192:256], op=ALU.mult),
                     reads=[b_lp], parts=[b_lp])
                P.op("dve", lambda e: e.reduce_sum(out=sm[:, 1:3], in_=lp[:, 512:640].rearrange("p (a b) -> p a b", b=64), axis=AX.X),
                     reads=[b_lp], writes=[b_sm])
                P.op("act", lambda e: e.activation(out=sm[:, 3:5], in_=sm[:, 1:3], func=AF.Exp), reads=[b_sm], writes=[b_sm])
                P.op("act", lambda e: e.activation(out=sm[:, 8:16], in_=lp[:, 384:392], func=AF.Exp), reads=[b_lp, b_sm], writes=[b_sm])
                P.op("dve", lambda e: e.scalar_tensor_tensor(out=sm[:, 0:1], in0=sm[:, 4:5], scalar=-lam_init, in1=sm[:, 3:4],
                                                             op0=ALU.add, op1=ALU.subtract), reads=[b_sm], writes=[b_sm])
                P.op("dve", lambda e: e.tensor_scalar(out=lp[:, 640:768], in0=lp[:, 256:384], scalar1=(1.0 - lam_init), scalar2=None,
                                                      op0=ALU.mult), reads=[b_lp], writes=[b_lp])
            else:
                P.dma("sp", out=lp[:, 0:128], in_=c_q_g[li:li + 1, :].broadcast_to([128, 128]), parts=[b_lp])
                P.dma("sp", out=lp[:, 128:256], in_=c_k_g[li:li + 1, :].broadcast_to([128, 128]), parts=[b_lp])
                P.dma("sp", out=lp[:, 256:768], in_=d_q_a_g[li:li + 1, :].broadcast_to([128, 512]), parts=[b_lp])
                P.dma("sp", out=lp[:, 768:1024], in_=d_kv_a_g[li:li + 1, :].broadcast_to([128, 256]), parts=[b_lp])

            if DEBUG_STOP == "L":
                dbg = nc.dram_tensor("dbg", [128, 64], F32, kind="ExternalOutput").ap()
                P.dma("sp", out=dbg, in_=sm[:], reads=[b_sm, b_lp])
                P.barrier()
                break
            wst = ExitStack()
            nwb[0] = 3 if even else 2
            wcnt[0] = 0
            for i_ in range(nwb[0]):
                wb[i_] = P.sb(wst, "wb%d" % i_, [128, 16, 512], BF16)
            first_c0 = 4096 if even else 1024
            pre_w = load_w(w_in, first_c0, 512)
            with ExitStack() as ph:
                xt = [P.sb(ph, "xt%d" % i, [128, DM], F32) for i in range(2)]
                xn = [P.sb(ph, "xn%d" % i, [128, DM], BF16) for i in range(2)]
                bxt = [Buf("xt0"), Buf("xt1")]
                bxn = [Buf("xn0"), Buf("xn1")]
                ss = P.sb(ph, "ss", [128, 4], F32)
                bss = Buf("ss")
                def nstats(t):
                    i = t % 2
                    so = 4 * i
                    rows = 128 if t < 8 else 64
                    if l == 0:
                        src = x_in[t * 128:(t + 1) * 128, :] if t < 8 else ctx_in[:, :]
                        rd = []
                    else:
                        src = xs[t * 128:t * 128 + rows, :]
                        rd = bxs[t]
                    P.dma("sp", out=xt[i][0:rows, :], in_=src, reads=rd, writes=[bxt[i]])
                    if t == 8:
                        P.op("pool", lambda e: e.memset(xt[i][64:128, :], 0.0), parts=[bxt[i]])
                    P.op("dve", lambda e: e.tensor_tensor(out=xn[i][:], in0=xt[i][:], in1=xt[i][:], op=ALU.mult),
                         reads=[bxt[i]], writes=[bxn[i]])
                    P.op("dve", lambda e: e.reduce_sum(out=ss8[:, so:so + 1], in_=xn[i][:], axis=AX.X), reads=[bxn[i]], writes=[bss8[i]])
                    rstd(ss8[:, so + 2:so + 3], ss8[:, so:so + 1], ss8[:, so + 1:so + 2], DM, bss8[i])
                    P.op("dve", lambda e: e.tensor_scalar(out=xn[i][:], in0=xt[i][:], scalar1=ss8[:, so + 2:so + 3], scalar2=None,
                                                          op0=ALU.mult), reads=[bxt[i], bss8[i]], writes=[bxn[i]])

                def nevac(t):
                    i = t % 2
                    mrow = 0 if t < 8 else 2
                    for half in range(2):
                        pst, bps = transposes(lambda k: xn[i][:, (half * 8 + k) * 128:(half * 8 + k + 1) * 128], 8, 128, [bxn[i]])
                        for k in range(8):
                            j = half * 8 + k
                            P.op("act", lambda e: e.activation(out=hT[:, t, j, :], in_=pst[:, k, :], func=AF.Identity,
                                                               bias=modv[:, l, mrow, j:j + 1], scale=modv[:, l, mrow + 1, j:j + 1]),
                                 reads=[bps, b_modv], parts=[bhT[t]])

                ss8 = P.sb(ph, "ss8", [128, 8], F32)
                bss8 = [Buf("ss8a"), Buf("ss8b")]
                nstats(0)
                for t in range(NTILE):
                    if t + 1 < NTILE:
                        nstats(t + 1)
                    nevac(t)
            P.barrier()
            if DEBUG_STOP == "N":
                dbg = nc.dram_tensor("dbg", [128, 64], F32, kind="ExternalOutput").ap()
                P.dma("sp", out=dbg, in_=sm[:], reads=[b_sm, b_lp])
                P.barrier()
                break

            with ExitStack() as pa:
                KTst = P.sb(pa, "KTst", [128, 11, TOK], BF16)
                bKT = [Buf("KT%d" % h) for h in range(11)]
                u32 = P.sb(pa, "u32", [128, 2048], F32)
                ta = P.sb(pa, "ta", [128, 512], F32)
                tb = P.sb(pa, "tb", [128, 512], F32)
                tn = P.sb(pa, "tn", [128, 512], F32)
                o16 = P.sb(pa, "o16", [128, 1024], BF16)
                v16 = P.sb(pa, "v16", [128, 1024], BF16)
                st8 = P.sb(pa, "st8", [128, 16], F32)
                bu, bta, btb, btn, bo16, bv16, bst8 = (Buf(n) for n in ("u32", "ta", "tb", "tn", "o16", "v16", "st8"))
                if not even:
                    wqb = P.sb(pa, "wqb", [128, 4, 1536], BF16)
                    wkvb = P.sb(pa, "wkvb", [128, 2, 2048], BF16)
                    cT16 = P.sb(pa, "cT16", [128, 4, 128], BF16)
                    bwq, bwkv, bcT16 = Buf("wqb"), Buf("wkvb"), Buf("cT16")
                    P.dma("pool", out=wqb[:], in_=d_w_q_b[li].rearrange("(j p) c -> p j c", p=128), writes=[bwq])
                    P.dma("pool", out=wkvb[:], in_=d_w_kv_b[li].rearrange("(j p) c -> p j c", p=128), writes=[bwkv])

                def rope(src, ncols, G, Dh, t, dst, bsrc, bdst):
                    h = Dh // 2
                    c0, s0, n0 = (0, 96, 192) if Dh == 64 else (32, 128, 224)
                    pat = "p (g two h) -> p g two h"
                    s4 = src[:, 0:ncols].rearrange(pat, two=2, h=h)
                    a4 = ta[:, 0:ncols].rearrange(pat, two=2, h=h)
                    b4 = tb[:, 0:ncols].rearrange(pat, two=2, h=h)
                    cosb = ropet[:, t, c0:c0 + h].unsqueeze(1).unsqueeze(1).broadcast_to([128, G, 2, h])
                    sinb = ropet[:, t, s0:s0 + h].unsqueeze(1).broadcast_to([128, G, h])
                    nsinb = ropet[:, t, n0:n0 + h].unsqueeze(1).broadcast_to([128, G, h])
                    P.op("dve" if even else "pool", lambda e: e.tensor_tensor(out=a4, in0=s4, in1=cosb, op=ALU.mult), reads=[bsrc, b_const], writes=[bta])
                    P.op("dve", lambda e: e.tensor_tensor(out=b4[:, :, 0, :], in0=s4[:, :, 1, :], in1=nsinb, op=ALU.mult),
                         reads=[bsrc, b_const], writes=[btb])
                    P.op("dve", lambda e: e.tensor_tensor(out=b4[:, :, 1, :], in0=s4[:, :, 0, :], in1=sinb, op=ALU.mult),
                         reads=[bsrc, b_const], parts=[btb])
                    P.op("dve", lambda e: e.tensor_tensor(out=dst[:, 0:ncols], in0=ta[:, 0:ncols], in1=tb[:, 0:ncols], op=ALU.add),
                         reads=[bta, btb], writes=[bdst])

                def rmsn(src, ncols, G, Dh, gain, bsrc):
                    s3 = src[:, 0:ncols].rearrange("p (g d) -> p g d", d=Dh)
                    t3 = tn[:, 0:ncols].rearrange("p (g d) -> p g d", d=Dh)
                    P.op("pool", lambda e: e.tensor_tensor(out=ta[:, 0:ncols], in0=src[:, 0:ncols], in1=src[:, 0:ncols], op=ALU.mult),
                         reads=[bsrc], writes=[bta])
                    P.op("dve", lambda e: e.reduce_sum(out=st8[:, 0:G], in_=ta[:, 0:ncols].rearrange("p (g d) -> p g d", d=Dh), axis=AX.X),
                         reads=[bta], writes=[bst8])
                    P.op("act", lambda e: e.activation(out=st8[:, 8:8 + G], in_=st8[:, 0:G], func=AF.Sqrt, bias=epst[:, 0:1], scale=1.0 / Dh),
                         reads=[bst8, b_const], writes=[bst8])
                    P.op("dve", lambda e: e.reciprocal(out=st8[:, 0:G], in_=st8[:, 8:8 + G]), reads=[bst8], writes=[bst8])
                    P.op("dve", lambda e: e.tensor_tensor(out=t3, in0=s3, in1=st8[:, 0:G].unsqueeze(2).broadcast_to([128, G, Dh]), op=ALU.mult),
                         reads=[bsrc, bst8], writes=[btn])
                    P.op("pool", lambda e: e.tensor_tensor(out=t3, in0=t3, in1=gain.unsqueeze(1).broadcast_to([128, G, Dh]), op=ALU.mult),
                         reads=[btn, b_lp], writes=[btn])

                def to_fm(src16, n, t, dstT, h0, bsrc, bdst_list, split=False):
                    ntk = 128 if t < 8 else 64
                    pst, bps = transposes(lambda k: src16[:, k * 128:(k + 1) * 128], n, 128, [bsrc])
                    if split:
                        P.op("act", lambda e: e.copy(out=dstT[0:64, h0:h0 + n, t * 128:t * 128 + ntk], in_=pst[0:64, :, 0:ntk]),
                             reads=[bps], parts=bdst_list)
                        P.op("act", lambda e: e.copy(out=dstT[64:128, 16 + h0:16 + h0 + n, t * 128:t * 128 + ntk], in_=pst[64:128, :, 0:ntk]),
                             reads=[bps], parts=bdst_list)
                    else:
                        P.op("act", lambda e: e.copy(out=dstT[:, h0:h0 + n, t * 128:t * 128 + ntk], in_=pst[:, :, 0:ntk]),
                             reads=[bps], parts=bdst_list)

                def qk_post(ps_ap, bps_, ncols, G, Dh, t, gain, dstT, h0, bdst_list, do_rope=True, split=False):
                    P.op("act", lambda e: e.copy(out=u32[:, 0:ncols], in_=ps_ap), reads=[bps_], writes=[bu])
                    src, bsrc = u32, bu
                    if gain is not None:
                        rmsn(u32, ncols, G, Dh, gain, bu)
                        src, bsrc = tn, btn
                    if do_rope and t < 8:
                        rope(src, ncols, G, Dh, t, o16, bsrc, bo16)
                    else:
                        P.op("dve", lambda e: e.tensor_copy(out=o16[:, 0:ncols], in_=src[:, 0:ncols]), reads=[bsrc], writes=[bo16])
                    to_fm(o16, ncols // 128, t, dstT, h0, bo16, bdst_list, split=split)

                def v_post(ps_ap, bps_, ncols, t, vcol0):
                    rows = 128 if t < 8 else 64
                    P.op("act", lambda e: e.copy(out=v16[:, 0:ncols], in_=ps_ap), reads=[bps_], writes=[bv16])
                    P.dma("sp", out=lbv(t, rows)[:, vcol0:vcol0 + ncols], in_=v16[0:rows, 0:ncols], reads=[bv16], parts=[bLB])

                acc = [0]

                jobs = []

                def inproj(c0, ncols, consume):
                    jobs.append((c0, ncols, consume))

                def flush():
                    wj = [k for k, jb in enumerate(jobs) if jb[0] != "call"]
                    assert jobs[wj[0]][0] == first_c0 and jobs[wj[0]][1] == 512
                    loaded = {wj[0]: pre_w}
                    depth = nwb[0] - 1
                    for k, jb in enumerate(jobs):
                        if jb[0] == "call":
                            jb[1]()
                            continue
                        c0, ncols, consume = jb
                        pos = wj.index(k)
                        for ahead in range(1, depth + 1):
                            if pos + ahead < len(wj):
                                kn = wj[pos + ahead]
                                if kn not in loaded:
                                    loaded[kn] = load_w(w_in, jobs[kn][0], jobs[kn][1])
                        wt, bw = loaded[k]
                        prev = None
                        for t in range(NTILE):
                            i = acc[0] % 2
                            acc[0] += 1
                            bank = psB[:, i * 512:i * 512 + ncols]
                            mm_group(bank, bB[i], [(hT[:, t, j, :], wt[:, j, 0:ncols]) for j in range(16)], [bhT[t], bw])
                            if prev is not None:
                                consume(*prev)
                            prev = (t, bank, bB[i])
                        consume(*prev)
                    del jobs[:]

                def xch(heads, kch, vch):
                    for h in heads:
                        P.dma("sp", out=lbk(h), in_=KTst[:, h, :], reads=[bKT[h]], parts=[bLB])
                    for c in kch:
                        allgather(LBK[c], GK[c], [bLB], [bG])
                    for c in vch:
                        allgather(LBV[c], GV[c], [bLB], [bG])

                if even:
                    def bkv(t, bk, bb):
                        qk_post(bk[:, 0:256], bb, 256, 2, 128, t, None, KTst, 8, bKT[8:10])
                        v_post(bk[:, 256:512], bb, 256, t, 1024)
                    inproj(4096, 512, bkv)
                    for hb in range(2):
                        inproj(2048 + hb * 512, 512, lambda t, bk, bb, hb=hb: v_post(bk, bb, 512, t, hb * 512))
                    jobs.append(("call", lambda: xch([9], [3], [0, 1])))
                    inproj(1024, 512, lambda t, bk, bb: qk_post(bk, bb, 512, 8, 64, t, None, KTst, 0, bKT[0:4]))
                    jobs.append(("call", lambda: xch([0, 1, 2], [0], [2, 3])))
                    inproj(1536, 512, lambda t, bk, bb: qk_post(bk, bb, 512, 8, 64, t, None, KTst, 4, bKT[4:8]))
                    jobs.append(("call", lambda: xch([3, 4, 5, 6, 7, 8], [1, 2], [])))
                    nkh = 10
                else:
                    def ckv(t, bk, bb):
                        qk_post(bk[:, 0:256], bb, 256, 2, 128, t, lp[:, 128:256], KTst, 0, bKT[0:2])
                        v_post(bk[:, 256:512], bb, 256, t, 0)
                    inproj(1024, 512, ckv)

                    def dkv(t, bk, bb):
                        P.op("act", lambda e: e.copy(out=u32[:, 0:320], in_=bk[:, 0:320]), reads=[bb], writes=[bu])
                        rmsn(u32, 256, 1, 256, lp[:, 768:1024], bu)
                        P.op("dve", lambda e: e.tensor_copy(out=o16[:, 0:256], in_=tn[:, 0:256]), reads=[btn], writes=[bo16])
                        pst, bps = transposes(lambda k: o16[:, k * 128:(k + 1) * 128], 2, 128, [bo16])
                        P.op("act", lambda e: e.copy(out=cT16[:, 0:2, :], in_=pst), reads=[bps], writes=[bcT16])
                        for nb in range(4):
                            mm_group(psA[:, nb * 512:(nb + 1) * 512], bA[nb],
                                     [(cT16[:, kc, :], wkvb[:, kc, nb * 512:(nb + 1) * 512]) for kc in range(2)], [bcT16, bwkv])
                        if t < 8:
                            rope(u32[:, 256:320], 64, 1, 64, t, o16[:, 0:64], bu, bo16)
                        else:
                            P.op("dve", lambda e: e.tensor_copy(out=o16[:, 0:64], in_=u32[:, 256:320]), reads=[bu], writes=[bo16])
                        P.op("dve", lambda e: e.tensor_copy(out=o16[:, 64:128], in_=o16[:, 0:64]), reads=[bo16], parts=[bo16])
                        to_fm(o16, 1, t, KTst, 10, bo16, [bKT[10]])
                        for nb in range(4):
                            kv4 = psA[:, nb * 512:(nb + 1) * 512].rearrange("p (h two d) -> p h two d", two=2, d=128)
                            P.op("act", lambda e: e.copy(out=o16[:, nb * 256:(nb + 1) * 256].rearrange("p (h d) -> p h d", d=128), in_=kv4[:, :, 0, :]),
                                 reads=[bA[nb]], writes=[bo16] if nb == 0 else (), parts=() if nb == 0 else [bo16])
                            P.op("dve", lambda e: e.tensor_copy(out=v16[:, nb * 256:(nb + 1) * 256].rearrange("p (h d) -> p h d", d=128), in_=kv4[:, :, 1, :]),
                                 reads=[bA[nb]], writes=[bv16] if nb == 0 else (), parts=() if nb == 0 else [bv16])
                        to_fm(o16, 8, t, KTst, 2, bo16, bKT[2:10])
                        rows = 128 if t < 8 else 64
                        P.dma("sp", out=lbv(t, rows)[:, 256:1280], in_=v16[0:rows, 0:1024], reads=[bv16], parts=[bLB])
                    inproj(2048, 320, dkv)
                    nkh = 11
                if not even:
                    jobs.append(("call", lambda: xch(list(range(11)), [0, 1, 2], [])))

                if even:
                    for hb in range(2):
                        inproj(hb * 512, 512, lambda t, bk, bb, hb=hb: qk_post(bk, bb, 512, 8, 64, t, None, QT, hb * 4, [bQT], split=True))
                    for hb in range(2):
                        inproj(3072 + hb * 512, 512, lambda t, bk, bb, hb=hb: qk_post(bk, bb, 512, 4, 128, t, None, QT, 8 + hb * 4, [bQT]))
                else:
                    for hb in range(2):
                        inproj(hb * 512, 512, lambda t, bk, bb, hb=hb: qk_post(bk, bb, 512, 4, 128, t, lp[:, 0:128], QT, hb * 4, [bQT]))
                        jobs.append(("call", (lambda: xch([], [3], [0, 1])) if hb == 0 else (lambda: xch([], [], [2, 3]))))

                    def dq(t, bk, bb):
                        P.op("act", lambda e: e.copy(out=u32[:, 0:512], in_=bk), reads=[bb], writes=[bu])
                        rmsn(u32, 512, 1, 512, lp[:, 256:768], bu)
                        P.op("dve", lambda e: e.tensor_copy(out=o16[:, 0:512], in_=tn[:, 0:512]), reads=[btn], writes=[bo16])
                        pst, bps = transposes(lambda k: o16[:, k * 128:(k + 1) * 128], 4, 128, [bo16])
                        P.op("act", lambda e: e.copy(out=cT16[:], in_=pst), reads=[bps], writes=[bcT16])
                        for nb in range(3):
                            mm_group(psA[:, nb * 512:(nb + 1) * 512], bA[nb],
                                     [(cT16[:, kc, :], wqb[:, kc, nb * 512:(nb + 1) * 512]) for kc in range(4)], [bcT16, bwq])
                        for nb in range(3):
                            P.op("act", lambda e: e.copy(out=u32[:, nb * 512:(nb + 1) * 512], in_=psA[:, nb * 512:(nb + 1) * 512]),
                                 reads=[bA[nb]], writes=[bu] if nb == 0 else (), parts=() if nb == 0 else [bu])
                        q3 = u32[:, 0:1536].rearrange("p (h d) -> p h d", d=192)
                        P.op("dve", lambda e: e.tensor_copy(out=o16[:, 0:1024].rearrange("p (h d) -> p h d", d=128), in_=q3[:, :, 0:128]),
                             reads=[bu], writes=[bo16])
                        P.op("pool", lambda e: e.tensor_copy(out=tn[:, 0:512].rearrange("p (h d) -> p h d", d=64), in_=q3[:, :, 128:192]),
                             reads=[bu], writes=[btn])
                        to_fm(o16, 8, t, QT, 8, bo16, [bQT])
                        if t < 8:
                            rope(tn, 512, 8, 64, t, v16, btn, bv16)
                        else:
                            P.op("dve", lambda e: e.tensor_copy(out=v16[:, 0:512], in_=tn[:, 0:512]), reads=[btn], writes=[bv16])
                        to_fm(v16, 4, t, QT, 16, bv16, [bQT])
                    inproj(1536, 512, dq)
                flush()
            wst.close()
            P.barrier()
            if DEBUG_STOP == "A":
                dbg = nc.dram_tensor("dbg", [128, 64], F32, kind="ExternalOutput").ap()
                P.dma("sp", out=dbg, in_=sm[:], reads=[b_sm, b_lp])
                P.barrier()
                break

            pbc = ExitStack()
            yT = P.sb(pbc, "yT", [128, 16, TOK], BF16)
            with ExitStack() as pb:
                KT = [P.sb(pb, "KT", [128, NKEY], BF16) for _ in range(2)]
                Vt = [P.sb(pb, "Vt", [128, NKT, 130], BF16) for _ in range(2)]
                PT = [P.sb(pb, "PT", [128, 2, 512], BF16) for _ in range(2)]
                osb = P.sb(pb, "osb", [128, 1, 4, 128], F32)
                y16 = P.sb(pb, "y16", [128, 4, 128], BF16)
                Pacc = P.sb(pb, "Pacc", [128, 2, 512], F32)
                rcs = P.sb(pb, "rcs", [128, 512], F32)
                onesf = P.sb(pb, "onesf", [128, 128], F32)
                bPacc, brcs, bones = [Buf("Pacc0"), Buf("Pacc1")], Buf("rcs"), Buf("onesf")
                PT4 = [P.sb(pb, "PT4", [128, 512], BF16) for _ in range(4)]
                pair16 = [P.sb(pb, "pair16", [128, 512], BF16) for _ in range(2)]
                bpair = [Buf("pair0"), Buf("pair1")]
                bPT4 = [Buf("PT4_%d" % i_) for i_ in range(4)]
                P.op("pool", lambda e: e.memset(onesf[:], 1.0), writes=[bones])
                if even:
                    oT = P.sb(pb, "oT", [128, 2, 512], F32)
                    dT = P.sb(pb, "dT", [128, 512], F32)
                    sqT = P.sb(pb, "sqT", [128, 512], F32)
                    subgc = P.sb(pb, "subgc", [128, 1], F32)
                    boT, bdT, bsqT, bsubg = Buf("oT"), Buf("dT"), Buf("sqT"), Buf("subgc")
                    P.dma("sp", out=subgc[:], in_=a_sub_g[li, :].rearrange("(p o) -> p o", o=1), writes=[bsubg])
                    P.op("dve", lambda e: e.tensor_scalar(out=subgc[:], in0=subgc[:], scalar1=(1.0 - lam_init), scalar2=None, op0=ALU.mult),
                         reads=[bsubg], writes=[bsubg])
                rc = P.sb(pb, "rc", [128, 16], F32)
                bK, bV, bPT = [Buf("K0"), Buf("K1")], [Buf("V0"), Buf("V1")], [Buf("PT0"), Buf("PT1")]
                bo, by16, brc, bkpe = (Buf(n) for n in ("osb", "y16", "rc", "kpe"))
                by16b = [Buf("y16b0"), Buf("y16b1")]
                for i in range(2):
                    P.op("pool", lambda e: e.memset(Vt[i][:, :, 128:130], 1.0), writes=[bV[i]])
                kvn = [0]
                pcn = [0]
                ocnt = [0]
                pend = [None]

                def load_k(dst, krow0, bdst):
                    P.dma("sp", out=dst[:, 0:4096].rearrange("p (s c) -> p s c", s=4), in_=gkr(krow0 // 128)[:, :, 0:1024],
                          reads=[bG], parts=[bdst])
                    P.dma("sp", out=dst[:, 4096:4352].rearrange("p (s c) -> p s c", s=4), in_=gkr(krow0 // 128)[:, :, 1024:1088],
                          reads=[bG], parts=[bdst])

                def load_kv(krow0, vcol0):
                    i = kvn[0] % 2
                    kvn[0] += 1
                    load_k(KT[i], krow0, bK[i])
                    for s_ in range(4):
                        for c in range(4):
                            P.dma("sp", out=Vt[i][:, s_ * 8 + 2 * c:s_ * 8 + 2 * c + 2, 0:128],
                                  in_=gv(s_, 2 * c, 256)[:, vcol0:vcol0 + 128].rearrange("(k p) c -> p k c", p=128), reads=[bG], parts=[bV[i]])
                        P.dma("sp", out=Vt[i][(s_ % 2) * 64:(s_ % 2) * 64 + 64, 32 + s_ // 2, 0:128],
                              in_=gv(s_, 8, 64)[:, vcol0:vcol0 + 128], reads=[bG], parts=[bV[i]])
                    return i

                def attend(pieces, rdK, Vi, kts, nq, sc, oi, sink_ap=None, masks=None, tilew=512):
                    nqs = max(1, nq // 128)
                    M = min(nq, 128)
                    gsz = 2 if tilew == 512 else 8
                    groups = [kts[a:a + gsz] for a in range(0, len(kts), gsz)]
                    ng = len(groups)
                    pbase = pcn[0]
                    pcn[0] += ng

                    def pv(gi, pview):
                        grp = groups[gi]
                        n = len(grp)
                        pk = (pbase + gi) % 2
                        for ki, kt in enumerate(grp):
                            for qs in range(nqs):
                                first = gi == 0 and ki == 0
                                last = gi == ng - 1 and ki == n - 1
                                ob = qs // 2
                                col = ob * 512 + (qs % 2) * 129
                                st_ = first and qs % 2 == 0
                                P.op("pe", lambda e: e.matmul(psB[0:M, col:col + 129], lhsT=pview(pk, ki, qs), rhs=Vt[Vi][:, kt, 0:129],
                                                              start=st_, stop=last, skip_group_check=True),
                                     reads=[bPT[pk], bV[Vi]], writes=[bB[ob]] if st_ else (), parts=() if st_ else [bB[ob]], sig=last)

                    if tilew == 512:
                        def qk(gi):
                            sg = gi % 2
                            for ki, kt in enumerate(groups[gi]):
                                bi = 2 * sg + ki
                                mm_group(psA[0:128, bi * 512:bi * 512 + nq], bA[bi], [(kf(kt), qa) for kf, qa in pieces], rdK)

                        def ex(gi):
                            sg = gi % 2
                            n = len(groups[gi])
                            pk = (pbase + gi) % 2
                            P.op("act", lambda e: e.activation(out=PT[pk][:, 0:n, 0:nq],
                                                               in_=psA[:, 2 * sg * 512:(2 * sg + n) * 512].rearrange("p (k c) -> p k c", c=512)[:, :, 0:nq],
                                                               func=AF.Exp, scale=sc),
                                 reads=[bA[2 * sg + a] for a in range(n)], writes=[bPT[pk]])

                        pview5 = lambda pk, ki, qs: PT[pk][:, ki, qs * 128:qs * 128 + M]
                        qk(0)
                        for gi in range(ng):
                            if gi + 1 < ng:
                                qk(gi + 1)
                            ex(gi)
                            pv(gi, pview5)
                    else:
                        for gi, grp in enumerate(groups):
                            n = len(grp)
                            pk = (pbase + gi) % 2
                            for ki, kt in enumerate(grp):
                                bi = ki // 4
                                mm_group(psA[0:128, ki * 128:ki * 128 + nq], bA[bi], [(kf(kt), qa) for kf, qa in pieces], rdK)
                            pflat = PT[pk][:].rearrange("p a b -> p (a b)")
                            P.op("act", lambda e: e.activation(out=pflat[:, 0:n * 128].rearrange("p (k c) -> p k c", c=128)[:, :, 0:nq],
                                                               in_=psA[:, 0:n * 128].rearrange("p (k c) -> p k c", c=128)[:, :, 0:nq],
                                                               func=AF.Exp, scale=sc),
                                 reads=[bA[0], bA[1]], writes=[bPT[pk]])
                            if masks is not None:
                                for ki, kt in enumerate(grp):
                                    if masks[ki] is not None:
                                        P.op("dve", lambda e: e.tensor_tensor(out=pflat[:, ki * 128:ki * 128 + nq], in0=pflat[:, ki * 128:ki * 128 + nq],
                                                                              in1=maskb[:, masks[ki], 0:nq], op=ALU.mult),
                                             reads=[bPT[pk], b_const], writes=[bPT[pk]])
                            pv(gi, lambda pk_, ki, qs: PT[pk_][:].rearrange("p a b -> p (a b)")[:, ki * 128:ki * 128 + M])
                    for qs in range(nqs):
                        ob = qs // 2
                        col = ob * 512 + (qs % 2) * 129
                        if sink_ap is None:
                            P.op("dve", lambda e: e.reciprocal(out=rc[0:M, qs:qs + 1], in_=psB[0:M, col + 128:col + 129]), reads=[bB[ob]], writes=[brc])
                        else:
                            P.op("dve", lambda e: e.tensor_tensor(out=rc[0:M, 8 + qs:9 + qs], in0=psB[0:M, col + 128:col + 129], in1=sink_ap, op=ALU.add),
                                 reads=[bB[ob], b_sm], writes=[brc])
                            P.op("dve", lambda e: e.reciprocal(out=rc[0:M, qs:qs + 1], in_=rc[0:M, 8 + qs:9 + qs]), reads=[brc], writes=[brc])
                        P.op("dve", lambda e: e.tensor_scalar(out=osb[0:M, oi, qs, :], in0=psB[0:M, col:col + 128], scalar1=rc[0:M, qs:qs + 1],
                                                              scalar2=None, op0=ALU.mult), reads=[bB[ob], brc], parts=[bo])

                def emit_y(src, bsrc, chunk, q0, nq):
                    nqs = max(1, nq // 128)
                    M = min(nq, 128)
                    P.op("pool", lambda e: e.tensor_copy(out=y16[0:M, 0:nqs, :], in_=src[0:M, 0:nqs, :]), reads=[bsrc], writes=[by16])
                    pst, bps = transposes(lambda k: y16[0:M, k, :], nqs, M, [by16])
                    P.op("act", lambda e: e.copy(out=yT[:, chunk, q0:q0 + nq].rearrange("p (k m) -> p k m", m=M), in_=pst), reads=[bps], parts=[byT])

                def attendT(pieces, rdK, Vi, kts, nq, sc, fin):
                    nk = len(kts)
                    pbase = pcn[0]
                    pcn[0] += nk

                    ob = ocnt[0] % 2
                    ocnt[0] += 1
                    o_ap, o_buf = psB[:, ob * 512:ob * 512 + nq], bB[ob]
                    pa_ = Pacc[:, ob, 0:nq]
                    bpa_ = bPacc[ob]

                    def qk(k):
                        bi = k % 3
                        mm_group(psA[0:128, bi * 512:bi * 512 + nq], bA[bi], [(kf(kts[k]), qa) for kf, qa in pieces], rdK)

                    for k in range(min(2, nk)):
                        qk(k)
                    if pend[0] is not None:
                        pend[0]()
                        pend[0] = None
                    for k in range(nk):
                        if k + 2 < nk:
                            qk(k + 2)
                        bi = k % 3
                        pk = (pbase + k) % 4
                        P.op("act", lambda e: e.activation(out=PT4[pk][:, 0:nq], in_=psA[:, bi * 512:bi * 512 + nq], func=AF.Exp, scale=sc),
                             reads=[bA[bi]], writes=[bPT4[pk]])
                        first = k == 0
                        last = k == nk - 1
                        P.op("pe", lambda e: e.matmul(o_ap, lhsT=Vt[Vi][:, kts[k], 0:128], rhs=PT4[pk][:, 0:nq], start=first, stop=last),
                             reads=[bPT4[pk], bV[Vi]], writes=[o_buf] if first else (), parts=() if first else [o_buf], sig=last)
                        if k % 2 == 1:
                            pkp = (pbase + k - 1) % 4
                            j2 = (k // 2) % 2
                            P.op("dve", lambda e: e.tensor_tensor(out=pair16[j2][:, 0:nq], in0=PT4[pkp][:, 0:nq], in1=PT4[pk][:, 0:nq], op=ALU.add),
                                 reads=[bPT4[pkp], bPT4[pk]], writes=[bpair[j2]])
                            if k == 1:
                                P.op("dve", lambda e: e.tensor_copy(out=pa_, in_=pair16[j2][:, 0:nq]), reads=[bpair[j2]], writes=[bpa_])
                            else:
                                P.op("dve", lambda e: e.tensor_tensor(out=pa_, in0=pa_, in1=pair16[j2][:, 0:nq], op=ALU.add),
                                     reads=[bpair[j2], bpa_], writes=[bpa_])
                        elif k == nk - 1:
                            if k == 0:
                                P.op("dve", lambda e: e.tensor_copy(out=pa_, in_=PT4[pk][:, 0:nq]), reads=[bPT4[pk]], writes=[bpa_])
                            else:
                                P.op("dve", lambda e: e.tensor_tensor(out=pa_, in0=pa_, in1=PT4[pk][:, 0:nq], op=ALU.add),
                                     reads=[bPT4[pk], bpa_], writes=[bpa_])
                    def epi():
                        mm_group(psA[:, 1536:1536 + nq], bA[3], [(onesf[:], pa_)], [bones, bpa_])
                        P.op("dve", lambda e: e.reciprocal(out=rcs[:, 0:nq], in_=psA[:, 1536:1536 + nq]), reads=[bA[3]], writes=[brcs])
                        fin(o_ap, o_buf)
                    pend[0] = epi

                def flush_pend():
                    if pend[0] is not None:
                        pend[0]()
                        pend[0] = None

                ALLK = list(range(NKT))
                qblocks = [(0, 512, ALLK), (512, 512, ALLK)] + ([(1024, 64, [32, 33])] if with_ctx else [])
                if even:
                    sc = 64 ** -0.5
                    for h in range(8):
                        Ki = load_kv(h * 128, h * 128)
                        for q0, nq, kts in qblocks:
                            nqs = max(1, nq // 128)
                            M = min(nq, 128)
                            for pr in range(2):
                                attendT([(lambda kt: KT[Ki][:, kt * 128:(kt + 1) * 128], QT[:, h + 16 * pr, q0:q0 + nq])],
                                        [bK[Ki], bQT], Ki, kts, nq, sc,
                                        lambda o_ap, o_buf, pr=pr, nq=nq: P.op("dve", lambda e: e.tensor_tensor(out=oT[:, pr, 0:nq], in0=o_ap, in1=rcs[:, 0:nq], op=ALU.mult),
                                                                           reads=[o_buf, brcs], parts=[boT]))
                            def subnorm(h=h, q0=q0, nq=nq):
                                P.op("dve", lambda e: e.scalar_tensor_tensor(out=dT[:, 0:nq], in0=oT[:, 1, 0:nq], scalar=sm[:, 0:1], in1=oT[:, 0, 0:nq],
                                                                             op0=ALU.mult, op1=ALU.add), reads=[boT, b_sm], writes=[bdT])
                                P.op("pool", lambda e: e.tensor_tensor(out=sqT[:, 0:nq], in0=dT[:, 0:nq], in1=dT[:, 0:nq], op=ALU.mult), reads=[bdT], writes=[bsqT])
                                mm_group(psA[:, 1536:1536 + nq], bA[3], [(onesf[:], sqT[:, 0:nq])], [bones, bsqT])
                                P.op("act", lambda e: e.activation(out=sqT[:, 0:nq], in_=psA[:, 1536:1536 + nq], func=AF.Ln, bias=epst[:, 0:1], scale=1.0 / 128),
                                     reads=[bA[3], b_const], writes=[bsqT])
                                P.op("act", lambda e: e.activation(out=sqT[:, 0:nq], in_=sqT[:, 0:nq], func=AF.Exp, scale=-0.5), reads=[bsqT], writes=[bsqT])
                                P.op("dve", lambda e: e.scalar_tensor_tensor(out=yT[:, h, q0:q0 + nq], in0=dT[:, 0:nq], scalar=subgc[:, 0:1], in1=sqT[:, 0:nq],
                                                                             op0=ALU.mult, op1=ALU.mult), reads=[bdT, bsqT, bsubg], parts=[byT])
                            prev_ = pend[0]
                            pend[0] = (lambda prev_=prev_, sn_=subnorm: (prev_(), sn_()))
                    flush_pend()
                    sc = 128 ** -0.5
                    for hk in range(2):
                        Ki = kvn[0] % 2
                        kvn[0] += 1
                        kr = (8 + hk) * 128
                        vc = 1024 + hk * 128
                        P.dma("sp", out=KT[Ki][:, 0:1024], in_=lbk(8 + hk)[:, 0:1024], reads=[bLB], parts=[bK[Ki]])
                        P.dma("sp", out=KT[Ki][:, 1024:1536].rearrange("p (s c) -> p s c", s=4), in_=gkr(8 + hk)[:, :, 896:1024], reads=[bG], parts=[bK[Ki]])
                        P.dma("sp", out=KT[Ki][:, 1536:2048].rearrange("p (s c) -> p s c", s=4), in_=gkr(8 + hk)[:, :, 0:128], reads=[bG], parts=[bK[Ki]])
                        P.dma("sp", out=KT[Ki][:, 2048:2304].rearrange("p (s c) -> p s c", s=4), in_=gkr(8 + hk)[:, :, 1024:1088], reads=[bG], parts=[bK[Ki]])
                        for c in range(4):
                            P.dma("sp", out=Vt[Ki][:, 2 * c:2 * c + 2, 0:128], in_=lbv(2 * c, 256)[:, vc:vc + 128].rearrange("(k p) c -> p k c", p=128),
                                  reads=[bLB], parts=[bV[Ki]])
                        for s_ in range(4):
                            P.dma("sp", out=Vt[Ki][:, 8 + s_, 0:128], in_=gv(s_, 7, 128)[:, vc:vc + 128], reads=[bG], parts=[bV[Ki]])
                            P.dma("sp", out=Vt[Ki][:, 12 + s_, 0:128], in_=gv(s_, 0, 128)[:, vc:vc + 128], reads=[bG], parts=[bV[Ki]])
                            P.dma("sp", out=Vt[Ki][(s_ % 2) * 64:(s_ % 2) * 64 + 64, 16 + s_ // 2, 0:128], in_=gv(s_, 8, 64)[:, vc:vc + 128],
                                  reads=[bG], parts=[bV[Ki]])
                        units = []
                        for g in range(4):
                            h = hk * 4 + g
                            for t in range(8 + (1 if with_ctx else 0)):
                                if t == 8:
                                    kts, mk, nq = [16, 17], [None, None], 64
                                else:
                                    kts, mk, nq = [16, 17, t], [None, None, None], 128
                                    if t > 0:
                                        kts.append(t - 1); mk.append(0)
                                    else:
                                        kts += [8, 9, 10, 11]; mk += [2, 3, 4, 5]
                                    if t < 7:
                                        kts.append(t + 1); mk.append(1)
                                    else:
                                        kts += [12, 13, 14, 15]; mk += [6, 7, 8, 9]
                                units.append((h, t, kts, mk, nq, len(units)))

                        def b_stage1(u):
                            h, t, kts, mk, nq, idx = u
                            so = (idx % 2) * 2
                            pk = idx % 2
                            n = len(kts)
                            for ki, kt in enumerate(kts):
                                col = so * 512 + ki * 128
                                mm_group(psA[0:128, col:col + nq], bA[so + ki // 4],
                                         [(KT[Ki][:, kt * 128:(kt + 1) * 128], QT[:, 8 + h, t * 128:t * 128 + nq])], [bK[Ki], bQT])
                            pflat = PT[pk][:].rearrange("p a b -> p (a b)")
                            P.op("act", lambda e: e.activation(out=pflat[:, 0:n * 128].rearrange("p (k c) -> p k c", c=128)[:, :, 0:nq],
                                                               in_=psA[:, so * 512:so * 512 + n * 128].rearrange("p (k c) -> p k c", c=128)[:, :, 0:nq],
                                                               func=AF.Exp, scale=sc),
                                 reads=[bA[so], bA[so + 1]], writes=[bPT[pk]])
                            for ki, kt in enumerate(kts):
                                if mk[ki] is not None:
                                    P.op("pool" if ki % 2 else "dve",
                                         lambda e: e.tensor_tensor(out=pflat[:, ki * 128:ki * 128 + nq], in0=pflat[:, ki * 128:ki * 128 + nq],
                                                                   in1=maskb[:, mk[ki], 0:nq], op=ALU.mult),
                                         reads=[bPT[pk], b_const], writes=[bPT[pk]])

                        def b_stage2(u):
                            h, t, kts, mk, nq, idx = u
                            pk = idx % 2
                            ob = idx % 2
                            M = min(nq, 128)
                            n = len(kts)
                            pflat = PT[pk][:].rearrange("p a b -> p (a b)")
                            acc_ = psB[0:M, ob * 512:ob * 512 + 129]
                            mm_group(acc_, bB[ob], [(pflat[:, ki * 128:ki * 128 + M], Vt[Ki][:, kt, 0:129]) for ki, kt in enumerate(kts)], [bPT[pk], bV[Ki]])
                            P.op("dve", lambda e: e.tensor_tensor(out=rc[0:M, 8 + ob:9 + ob], in0=acc_[:, 128:129], in1=sm[0:M, 8 + h:9 + h], op=ALU.add),
                                 reads=[bB[ob], b_sm], writes=[brc])
                            P.op("dve", lambda e: e.reciprocal(out=rc[0:M, ob:ob + 1], in_=rc[0:M, 8 + ob:9 + ob]), reads=[brc], writes=[brc])
                            P.op("dve", lambda e: e.tensor_scalar(out=y16[0:M, ob, :], in0=acc_[:, 0:128], scalar1=rc[0:M, ob:ob + 1], scalar2=None, op0=ALU.mult),
                                 reads=[bB[ob], brc], writes=[by16b[ob]])

                        def b_stage3(u):
                            h, t, kts, mk, nq, idx = u
                            ob = idx % 2
                            M = min(nq, 128)
                            pst, bps = transposes(lambda k_: y16[0:M, ob, :], 1, M, [by16b[ob]])
                            P.op("act", lambda e: e.copy(out=yT[:, 8 + h, t * 128:t * 128 + nq].rearrange("p (k m) -> p k m", m=M), in_=pst), reads=[bps], parts=[byT])

                        b_stage1(units[0])
                        for ui in range(len(units)):
                            if ui + 1 < len(units):
                                b_stage1(units[ui + 1])
                            b_stage2(units[ui])
                            if ui >= 1:
                                b_stage3(units[ui - 1])
                        b_stage3(units[-1])
                else:
                    sc = 128 ** -0.5
                    for hk in range(2):
                        Ki = load_kv(hk * 128, hk * 128)
                        for g in range(4):
                            h = hk * 4 + g
                            for q0, nq, kts in qblocks:
                                attendT([(lambda kt: KT[Ki][:, kt * 128:(kt + 1) * 128], QT[:, h, q0:q0 + nq])], [bK[Ki], bQT], Ki, kts, nq, sc,
                                        lambda o_ap, o_buf, h=h, q0=q0, nq=nq: P.op("dve", lambda e: e.tensor_tensor(out=yT[:, h, q0:q0 + nq], in0=o_ap, in1=rcs[:, 0:nq], op=ALU.mult),
                                                                                     reads=[o_buf, brcs], parts=[byT]))
                    kpeL = P.sb(pb, "kpeL", [128, NKEY], BF16)
                    kpeH = P.sb(pb, "kpeH", [128, NKEY], BF16)
                    P.op("pool", lambda e: e.memset(kpeL[64:128, :], 0.0), parts=[bkpe])
                    P.op("pool", lambda e: e.memset(kpeH[0:64, :], 0.0), parts=[bkpe])
                    for kx, r0 in ((kpeL, 0), (kpeH, 64)):
                        P.dma("sp", out=kx[r0:r0 + 64, 0:4096].rearrange("p (s c) -> p s c", s=4), in_=gkr(10)[r0:r0 + 64, :, 0:1024],
                              reads=[bG], parts=[bkpe])
                        P.dma("sp", out=kx[r0:r0 + 64, 4096:4352].rearrange("p (s c) -> p s c", s=4), in_=gkr(10)[r0:r0 + 64, :, 1024:1088],
                              reads=[bG], parts=[bkpe])
                    sc = 192 ** -0.5
                    for h in range(8):
                        Ki = load_kv(256 + h * 128, 256 + h * 128)
                        kpx = kpeL if h % 2 == 0 else kpeH
                        for q0, nq, kts in qblocks:
                            attendT([(lambda kt: KT[Ki][:, kt * 128:(kt + 1) * 128], QT[:, 8 + h, q0:q0 + nq]),
                                     (lambda kt: kpx[:, kt * 128:(kt + 1) * 128], QT[:, 16 + h // 2, q0:q0 + nq])],
                                    [bK[Ki], bkpe, bQT], Ki, kts, nq, sc,
                                    lambda o_ap, o_buf, h=h, q0=q0, nq=nq: P.op("dve", lambda e: e.tensor_tensor(out=yT[:, 8 + h, q0:q0 + nq], in0=o_ap, in1=rcs[:, 0:nq], op=ALU.mult),
                                                                                 reads=[o_buf, brcs], parts=[byT]))
                    flush_pend()
            P.barrier()
            if DEBUG_STOP == "B":
                pbc.close()
            if DEBUG_STOP == "B":
                dbg = nc.dram_tensor("dbg", [128, 64], F32, kind="ExternalOutput").ap()
                P.dma("sp", out=dbg, in_=sm[:], reads=[b_sm, b_lp])
                P.barrier()
                break

            with ExitStack() as pc_:
                nwb[0] = 2
                wcnt[0] = 0
                wb[0] = P.sb(pc_, "wb0", [128, 16, 512], BF16)
                wb[1] = P.sb(pc_, "wb1", [128, 16, 512], BF16)
                gate_bc = P.sb(pc_, "gate_bc", [128, 2, DM], F32)
                gs = P.sb(pc_, "gs", [128, 512], BF16)
                xp = [P.sb(pc_, "xp", [128, 512], F32) for _ in range(2)]
                xg = [P.sb(pc_, "xg", [128, 512], F32) for _ in range(2)]
                bgs, bxp, bxg = Buf("gs"), [Buf("xp0"), Buf("xp1")], [Buf("xg0"), Buf("xg1")]
                b_gate.w, b_gate.wf, b_gate.r = {}, {}, {}
                load_gate(l, gate_bc)
                ntl = NTILE if with_ctx else 8
                tgroups = [(0, 4, 512), (4, 4, 512)] + ([(8, 1, 64)] if with_ctx else [])
                ai = 0
                cjobs = [("g", k_) for k_ in range(4)] + [("o", k_) for k_ in range(4)]

                def cload(j):
                    kind, k_ = cjobs[j]
                    return load_w(w_in, GATE0 + k_ * 512, 512) if kind == "g" else load_w(w_o[l], k_ * 512, 512)
                cw = {0: cload(0)}
                for blk in range(4):
                    cw[blk + 1] = cload(blk + 1)
                    wt, bw = cw[blk]
                    for mm in range(4):
                        m = blk * 4 + mm
                        for t0, nt, ntok in tgroups:
                            i = ai % 2
                            ai += 1
                            if nt == 4:
                                rhs_f = lambda j: hT[:, t0:t0 + 4, j, :]
                            else:
                                rhs_f = lambda j: hT[:, 8, j, 0:64]
                            mm_group(psB[:, i * 512:i * 512 + ntok], bB[i], [(wt[:, j, mm * 128:(mm + 1) * 128], rhs_f(j)) for j in range(16)],
                                     [bw] + bhT)
                            P.op("act", lambda e: e.activation(out=gs[:, 0:ntok], in_=psB[:, i * 512:i * 512 + ntok], func=AF.Silu),
                                 reads=[bB[i]], writes=[bgs])
                            P.op("dve", lambda e: e.tensor_tensor(out=yT[:, m, t0 * 128:t0 * 128 + ntok], in0=yT[:, m, t0 * 128:t0 * 128 + ntok],
                                                                  in1=gs[:, 0:ntok], op=ALU.mult), reads=[bgs, byT], parts=[byT])
                for nb in range(4):
                    if nb + 1 < 4:
                        cw[4 + nb + 1] = cload(4 + nb + 1)
                    wt, bw = cw[4 + nb]
                    for t in range(ntl):
                        rows = 128 if t < 8 else 64
                        i = ai % 2
                        ai += 1
                        if l == 0:
                            src = x_in[t * 128:(t + 1) * 128, nb * 512:(nb + 1) * 512] if t < 8 else ctx_in[:, nb * 512:(nb + 1) * 512]
                            rd = []
                        else:
                            src = xs[t * 128:t * 128 + rows, nb * 512:(nb + 1) * 512]
                            rd = [bxs[t][nb]]
                        P.dma("sp", out=xp[i][0:rows, :], in_=src, reads=rd, writes=[bxp[i]])
                        mm_group(psB[0:rows, i * 512:(i + 1) * 512], bB[i],
                                 [(yT[:, m, t * 128:t * 128 + rows], wt[:, m, :]) for m in range(16)], [bw, byT])
                        grow = 0 if t < 8 else 1
                        P.op("dve", lambda e: e.tensor_tensor(out=xg[i][0:rows, :], in0=psB[0:rows, i * 512:(i + 1) * 512],
                                                              in1=gate_bc[0:rows, grow, nb * 512:(nb + 1) * 512], op=ALU.mult),
                             reads=[b_gate, bB[i]], writes=[bxg[i]])
                        P.op("pool", lambda e: e.tensor_tensor(out=xp[i][0:rows, :], in0=xp[i][0:rows, :], in1=xg[i][0:rows, :], op=ALU.add),
                             reads=[bxg[i]], writes=[bxp[i]])
                        P.dma("sp", out=xs[t * 128:t * 128 + rows, nb * 512:(nb + 1) * 512], in_=xp[i][0:rows, :], reads=[bxp[i]], writes=[bxs[t][nb]])
            pbc.close()
            P.barrier()
      except _Stop:
        P.barrier()
        dbg = nc.dram_tensor("dbg", [128, 64], F32, kind="ExternalOutput").ap()
        P.dma("sp", out=dbg, in_=sm[:], reads=[b_sm, b_lp])
        P.barrier()
        return P

    with ExitStack() as pf:
        fg = P.sb(pf, "fg", [128, DM], F32)
        xt = [P.sb(pf, "fxt", [128, DM], F32) for _ in range(2)]
        xq = [P.sb(pf, "fxq", [128, DM], F32) for _ in range(2)]
        ss = P.sb(pf, "fss", [128, 4], F32)
        bfg, bss, bxt, bxq = Buf("fg"), Buf("fss"), [Buf("a"), Buf("b")], [Buf("c"), Buf("d")]
        bout = Buf("out")
        P.dma("sp", out=fg[:], in_=final_g.broadcast_to([128, DM]), writes=[bfg])
        for t in range(8):
            i = t % 2
            P.dma("sp", out=xt[i][:], in_=xs[t * 128:(t + 1) * 128, :], reads=bxs[t], writes=[bxt[i]])
            P.op("pool", lambda e: e.tensor_tensor(out=xq[i][:], in0=xt[i][:], in1=xt[i][:], op=ALU.mult), reads=[bxt[i]], writes=[bxq[i]])
            P.op("dve", lambda e: e.reduce_sum(out=ss[:, 0:1], in_=xq[i][:], axis=AX.X), reads=[bxq[i]], writes=[bss])
            rstd(ss[:, 2:3], ss[:, 0:1], ss[:, 1:2], DM, bss)
            P.op("dve", lambda e: e.scalar_tensor_tensor(out=xq[i][:], in0=xt[i][:], scalar=ss[:, 2:3], in1=fg[:], op0=ALU.mult, op1=ALU.mult),
                 reads=[bxt[i], bss, bfg], writes=[bxq[i]])
            P.dma("sp", out=out[t * 128:(t + 1) * 128, :], in_=xq[i][:], reads=[bxq[i]], parts=[bout])
    P.barrier()
    return P


DEBUG_STOP = None


def _rope_tables(r):
    tok = r * 1024 + np.arange(1024)
    row = (tok // 64).astype(np.float32)
    col = (tok % 64).astype(np.float32)
    tabs = {}
    for dim in (64, 128):
        nf = dim // 4
        inv = (np.float32(10000.0) ** (-np.arange(nf, dtype=np.float32) / np.float32(nf))).astype(np.float32)
        ang = np.concatenate([row[:, None] * inv[None], col[:, None] * inv[None]], axis=-1).astype(np.float32)
        tabs[dim] = (np.cos(ang).astype(np.float32), np.sin(ang).astype(np.float32))
    t = np.zeros((1024, 288), np.float32)
    t[:, 0:32] = tabs[64][0]
    t[:, 32:96] = tabs[128][0]
    t[:, 96:128] = tabs[64][1]
    t[:, 128:192] = tabs[128][1]
    t[:, 192:224] = -tabs[64][1]
    t[:, 224:288] = -tabs[128][1]
    return np.ascontiguousarray(t.reshape(8, 128, 288).transpose(1, 0, 2).reshape(128, 8 * 288))


def _masks(r):
    i = np.arange(128)[:, None]
    j = np.arange(128)[None, :]
    triP = (i >= j).astype(np.float32)
    triN = (i <= j).astype(np.float32)
    m = np.zeros((128, 10, 128), np.float32)
    m[:, 0] = triP
    m[:, 1] = triN
    for s in range(4):
        if s == r - 1:
            m[:, 2 + s] = triP
        if s == r + 1:
            m[:, 6 + s] = triN
    return np.ascontiguousarray(m.reshape(128, 1280))


def make_in_maps(inp):
    f = lambda a: np.ascontiguousarray(np.asarray(a, dtype=np.float32))
    x, c, ctx, c_ctx = f(inp["x"]), f(inp["c"]), f(inp["ctx"]), f(inp["c_ctx"])
    w_mod, b_mod = f(inp["w_mod"]), f(inp["b_mod"])
    shared = {
        "norm_g": f(inp["norm_g"]), "final_g": f(inp["final_g"]).reshape(1, DM), "w_o": f(inp["w_o"]),
        "e_w_in": f(inp["e_w_in"]), "o_w_in": f(inp["o_w_in"]),
        "a_lam": np.ascontiguousarray(np.concatenate([f(inp["a_lam_q1"]), f(inp["a_lam_k1"]), f(inp["a_lam_q2"]), f(inp["a_lam_k2"])], axis=1)),
        "a_sub_g": f(inp["a_sub_g"]), "b_sink": f(inp["b_sink"]), "c_q_g": f(inp["c_q_g"]), "c_k_g": f(inp["c_k_g"]),
        "d_q_a_g": f(inp["d_q_a_g"]), "d_kv_a_g": f(inp["d_kv_a_g"]), "d_w_q_b": f(inp["d_w_q_b"]), "d_w_kv_b": f(inp["d_w_kv_b"]),
        "ident": np.eye(128, dtype=np.float32),
    }
    wm_sl = [np.ascontiguousarray(w_mod[:, :, r * 1536:(r + 1) * 1536]) for r in range(4)]
    bm_sl = [np.ascontiguousarray(b_mod[:, r * 1536:(r + 1) * 1536]) for r in range(4)]
    maps = []
    for core in range(8):
        b, r = divmod(core, 4)
        m = dict(shared)
        m["x"] = np.ascontiguousarray(x[b, r * 1024:(r + 1) * 1024])
        m["ctx"] = np.ascontiguousarray(ctx[b, r * 64:(r + 1) * 64])
        m["cvec"] = np.ascontiguousarray(np.stack([c[b], c_ctx]))
        m["wmod"] = wm_sl[r]
        m["bmod"] = bm_sl[r]
        m["rope"] = _rope_tables(r)
        m["bmask"] = _masks(r)
        maps.append(m)
    return maps


_NC_CACHE = {}


def kernel(**inputs):
    if "nc" not in _NC_CACHE:
        _NC_CACHE["nc"] = build(4).nc
    maps = make_in_maps(inputs)
    res = run_bass_kernel_spmd(_NC_CACHE["nc"], maps, core_ids=list(range(8)))
    out = np.zeros((2, 4096, DM), np.float32)
    for core in range(8):
        b, r = divmod(core, 4)
        out[b, r * 1024:(r + 1) * 1024] = res.results[core]["out"]
    return out
```

```python
import numpy as np
from contextlib import ExitStack
import concourse.bass as bass
import concourse.mybir as mybir
from concourse.bass_utils import run_bass_kernel_spmd

F32 = mybir.dt.float32
BF16 = mybir.dt.bfloat16
AF = mybir.ActivationFunctionType
ALU = mybir.AluOpType
AX = mybir.AxisListType

DM = 2048
NTILE = 9
TOK = 1088
EPS = 1e-6
NKEY = 4352
NKT = 34


class Buf:
    def __init__(self, name):
        self.name = name
        self.w = {}
        self.wf = {}
        self.r = {}


class Prog:
    def __init__(self):
        self.nc = bass.Bass("TRN2", target_bir_lowering=False)
        self.st = ExitStack()
        nc = self.nc
        self.E = {"pe": nc.tensor, "act": nc.scalar, "dve": nc.vector, "pool": nc.gpsimd, "sp": nc.sync}
        self.sem, self.inc, self.seq = {}, {}, {}
        for k in ("pe", "act", "dve", "pool", "cc"):
            self.sem[k] = self.st.enter_context(nc.semaphore(k))
            self.inc[k] = 1
            self.seq[k] = 0
        self.nslots = {"sp": 16, "pool": 8, "act": 4}
        self.slot = {"sp": 0, "pool": 0, "act": 0}
        for q, n in self.nslots.items():
            for i in range(n):
                k = "%s_d%d" % (q, i)
                self.sem[k] = self.st.enter_context(nc.semaphore(k))
                self.inc[k] = 16
                self.seq[k] = 0
        self.known = {k: {} for k in self.E}
        self.nwait = 0

    def _wait(self, eng, deps):
        for e, s in deps.items():
            if s <= 0:
                continue
            if eng == "pe" and e == "pe":
                continue
            if self.known[eng].get(e, 0) >= s:
                continue
            self.E[eng].wait_ge(self.sem[e], s * self.inc[e])
            self.known[eng][e] = s
            self.nwait += 1

    @staticmethod
    def _deps(reads, writes, parts):
        deps = {}

        def add(d):
            for e, s in d.items():
                if deps.get(e, 0) < s:
                    deps[e] = s
        for b in reads:
            add(b.w)
        for b in writes:
            add(b.w)
            add(b.r)
        for b in parts:
            add(b.r)
            add(b.wf)
        return deps

    @staticmethod
    def _commit(e, s, reads, writes, parts):
        for b in reads:
            if b.r.get(e, 0) < s:
                b.r[e] = s
        for b in writes:
            b.w = {e: s}
            b.wf = {e: s}
            b.r = {}
        for b in parts:
            if b.w.get(e, 0) < s:
                b.w[e] = s

    def op(self, eng, fn, reads=(), writes=(), parts=(), sig=True):
        self._wait(eng, self._deps(reads, writes, parts))
        inst = fn(self.E[eng])
        if sig:
            self.seq[eng] += 1
            inst.then_inc(self.sem[eng], 1)
            s = self.seq[eng]
        else:
            s = self.seq[eng] + 1
        self._commit(eng, s, reads, writes, parts)
        return inst

    def dma(self, q, out, in_, reads=(), writes=(), parts=(), **kw):
        k = "%s_d%d" % (q, self.slot[q])
        self.slot[q] = (self.slot[q] + 1) % self.nslots[q]
        deps = self._deps(reads, writes, parts)
        if deps.get(k, 0) < self.seq[k]:
            deps[k] = self.seq[k]
        self._wait(q, deps)
        inst = self.E[q].dma_start(out=out, in_=in_, **kw)
        self.seq[k] += 1
        inst.then_inc(self.sem[k], 16)
        self._commit(k, self.seq[k], reads, writes, parts)
        return inst

    def barrier(self):
        allv = dict(self.seq)
        for eng in self.E:
            self._wait(eng, allv)

    def sb(self, stack, name, shape, dt):
        self.uid = getattr(self, "uid", 0) + 1
        return stack.enter_context(self.nc.sbuf_tensor("%s_%d" % (name, self.uid), list(shape), dt))

    def ps(self, stack, name, shape, dt):
        return stack.enter_context(self.nc.psum_tensor(name, list(shape), dt))


class _Stop(Exception):
    pass


DEBUG_NSTEP = None


def build(n_layers=4):
    P = Prog()

    def chk(k):
        if DEBUG_NSTEP == k:
            raise _Stop()

    nc = P.nc
    st = P.st

    def din(name, shape):
        return nc.dram_tensor(name, list(shape), F32, kind="ExternalInput").ap()

    x_in = din("x", [1024, DM])
    ctx_in = din("ctx", [64, DM])
    cvec = din("cvec", [2, DM])
    wmod = din("wmod", [4, DM, 1536])
    bmod = din("bmod", [4, 1536])
    norm_g = din("norm_g", [4, DM])
    final_g = din("final_g", [1, DM])
    w_o = din("w_o", [4, DM, DM])
    e_w = din("e_w_in", [2, DM, 6656])
    o_w = din("o_w_in", [2, DM, 4416])
    a_lam = din("a_lam", [2, 256])
    a_sub_g = din("a_sub_g", [2, 128])
    b_sink = din("b_sink", [2, 8])
    c_q_g = din("c_q_g", [2, 128])
    c_k_g = din("c_k_g", [2, 128])
    d_q_a_g = din("d_q_a_g", [2, 512])
    d_kv_a_g = din("d_kv_a_g", [2, 256])
    d_w_q_b = din("d_w_q_b", [2, 512, 1536])
    d_w_kv_b = din("d_w_kv_b", [2, 256, 2048])
    rope_in = din("rope", [128, 8 * 288])
    ident_in = din("ident", [128, 128])
    bmask_in = din("bmask", [128, 10 * 128])
    out = nc.dram_tensor("out", [1024, DM], F32, kind="ExternalOutput").ap()

    xs = nc.dram_tensor("xs", [TOK, DM], F32).ap()
    modloc = nc.dram_tensor("modloc", [8, 1536], F32)
    modall = nc.dram_tensor("modall", [32, 1536], F32)
    GROUPS = [[0, 1, 2, 3], [4, 5, 6, 7]]

    ccn = [0]

    def allgather(src_t, dst_t, reads, writes):
        k = "cc%d" % ccn[0]
        ccn[0] += 1
        P.sem[k] = st.enter_context(nc.semaphore(k))
        P.inc[k] = 1
        P.seq[k] = 0
        P._wait("pool", P._deps(reads, (), writes))
        inst = nc.gpsimd.collective_compute("AllGather", ALU.bypass, replica_groups=GROUPS,
                                            ins=[src_t.ap().opt()], outs=[dst_t.ap().opt()])
        inst.then_inc(P.sem[k])
        P.seq[k] = 1
        P._commit(k, 1, reads, (), writes)

    identb = P.sb(st, "identb", [128, 128], BF16)
    maskb = P.sb(st, "maskb", [128, 10, 128], BF16)
    ropet = P.sb(st, "ropet", [128, 8, 288], F32)
    modv = P.sb(st, "modv", [128, 4, 5, 16], F32)
    hT = P.sb(st, "hT", [128, NTILE, 16, 128], BF16)
    wb = [None, None, None]
    nwb = [2]
    lp = P.sb(st, "lp", [128, 1280], F32)
    sm = P.sb(st, "sm", [128, 64], F32)
    b_const = Buf("const")
    b_modv = Buf("modv")
    b_gate = Buf("gate")
    bhT = [Buf("hT%d" % t) for t in range(NTILE)]
    bwb = [Buf("wb0"), Buf("wb1"), Buf("wb2")]
    b_lp = Buf("lp")
    b_sm = Buf("sm")
    wcnt = [0]

    psA = P.ps(st, "psA", [128, 2048], F32)
    psB = P.ps(st, "psB", [128, 1024], F32)
    psT = P.ps(st, "psT", [128, 2, 1024], BF16)
    bA = [Buf("psA%d" % i) for i in range(4)]
    bB = [Buf("psB%d" % i) for i in range(2)]
    bT = [Buf("psT%d" % i) for i in range(2)]
    tcnt = [0]

    bxs = [[Buf("xs%d_%d" % (t, nb)) for nb in range(4)] for t in range(NTILE)]

    P.dma("pool", out=identb[:], in_=ident_in, writes=[b_const])
    P.dma("pool", out=maskb[:], in_=bmask_in.rearrange("p (m k) -> p m k", k=128), parts=[b_const])
    P.dma("sp", out=ropet[:], in_=rope_in.rearrange("p (t k) -> p t k", k=288), parts=[b_const])

    def load_w(src2d, c0, ncols):
        i = wcnt[0] % nwb[0]
        wcnt[0] += 1
        P.dma("pool", out=wb[i][:, :, 0:ncols],
              in_=src2d[:, c0:c0 + ncols].rearrange("(j p) c -> p j c", p=128), writes=[bwb[i]])
        return wb[i], bwb[i]

    with ExitStack() as ph:
        cT = P.sb(ph, "cT", [128, 2, 16], F32)
        scT = P.sb(ph, "scT", [128, 2, 16], F32)
        wm = [P.sb(ph, "wm%d" % i, [128, 8, 512], F32) for i in range(4)]
        rowt = P.sb(ph, "rowt", [64, 128], F32)
        identf = P.sb(ph, "identf", [128, 128], F32)
        b_rowt = Buf("rowt")
        P.dma("sp", out=identf[:], in_=ident_in, parts=[b_const])
        modsb = P.sb(ph, "modsb", [2, 1536], F32)
        bmsb = P.sb(ph, "bmsb", [2, 1536], F32)
        b_cT, b_scT, b_modsb, b_bmsb, b_modloc, b_modall = (Buf(n) for n in ("cT", "scT", "modsb", "bmsb", "modloc", "modall"))
        bwm = [Buf("wm%d" % i) for i in range(4)]
        P.dma("sp", out=rowt[0:32, :], in_=cvec.rearrange("r (j p) -> (r j) p", p=128), writes=[b_rowt])
        P.op("pe", lambda e: e.transpose(out=psB[:, 0:32], in_=rowt[0:32, :], identity=identf[0:32, 0:32]),
             reads=[b_rowt, b_const], writes=[bB[0]])
        P.op("dve", lambda e: e.tensor_copy(out=cT[:].rearrange("p a b -> p (a b)"), in_=psB[:, 0:32]), reads=[bB[0]], writes=[b_cT])
        P.op("act", lambda e: e.activation(out=scT[:], in_=cT[:], func=AF.Silu), reads=[b_cT], writes=[b_scT])
        k = 0
        for l in range(4):
            P.dma("sp", out=bmsb[:], in_=bmod[l:l + 1, :].broadcast_to([2, 1536]),
                  writes=[b_bmsb])
            for pc in range(3):
                for hf in range(2):
                    i = k % 4
                    k += 1
                    P.dma("sp", out=wm[i][:],
                          in_=wmod[l, hf * 1024:(hf + 1) * 1024, pc * 512:(pc + 1) * 512].rearrange("(j p) c -> p j c", p=128),
                          writes=[bwm[i]])
                    for j in range(8):
                        first = hf == 0 and j == 0
                        last = hf == 1 and j == 7
                        P.op("pe", lambda e: e.matmul(psB[0:2, 0:512], lhsT=scT[:, :, hf * 8 + j], rhs=wm[i][:, j, :],
                                                      start=first, stop=last),
                             reads=[b_scT, bwm[i]], writes=[bB[0]] if first else (), parts=() if first else [bB[0]], sig=last)
                P.op("dve", lambda e: e.tensor_tensor(out=modsb[:, pc * 512:(pc + 1) * 512], in0=psB[0:2, 0:512],
                                                      in1=bmsb[:, pc * 512:(pc + 1) * 512], op=ALU.add),
                     reads=[bB[0], b_bmsb], parts=[b_modsb])
            P.dma("sp", out=modloc.ap()[l * 2:(l + 1) * 2, :], in_=modsb[:], reads=[b_modsb], parts=[b_modloc])
        allgather(modloc, modall, [b_modloc], [b_modall])

        def mod_pieces(col0, n):
            res = []
            off = 0
            while off < n:
                col = col0 + off
                rk, cc = divmod(col, 1536)
                ln = min(n - off, 1536 - cc)
                res.append((rk, cc, ln, off))
                off += ln
            return res

        for l in range(4):
            for row in range(2):
                for rk, cc, ln, off in mod_pieces(0, 4096):
                    P.dma("sp", out=rowt[off // 128:(off + ln) // 128, :],
                          in_=modall.ap()[rk * 8 + l * 2 + row, cc:cc + ln].rearrange("(j p) -> j p", p=128),
                          reads=[b_modall], parts=[b_rowt])
                P.op("pe", lambda e: e.transpose(out=psB[:, 0:32], in_=rowt[0:32, :], identity=identf[0:32, 0:32]),
                     reads=[b_rowt, b_const], writes=[bB[0]])
                P.op("dve", lambda e: e.tensor_copy(out=modv[:, l, row * 2:row * 2 + 2, :].rearrange("p a b -> p (a b)"), in_=psB[:, 0:32]),
                     reads=[bB[0]], parts=[b_modv])
        P.dma("sp", out=rowt[0:64, :], in_=norm_g.rearrange("l (j p) -> (l j) p", p=128), writes=[b_rowt])
        P.op("pe", lambda e: e.transpose(out=psB[:, 0:64], in_=rowt[0:64, :], identity=identf[0:64, 0:64]),
             reads=[b_rowt, b_const], writes=[bB[0]])
        for l in range(4):
            P.op("dve", lambda e: e.tensor_copy(out=modv[:, l, 4, :], in_=psB[:, l * 16:(l + 1) * 16]), reads=[bB[0]], parts=[b_modv])
        for idx in (1, 3):
            P.op("dve", lambda e: e.scalar_tensor_tensor(out=modv[:, :, idx, :], in0=modv[:, :, idx, :], scalar=1.0,
                                                         in1=modv[:, :, 4, :], op0=ALU.add, op1=ALU.mult),
                 reads=[b_modv], writes=[b_modv])

        def load_gate(l, gate_bc):
            for row in range(2):
                for rk, cc, ln, off in mod_pieces(4096, 2048):
                    P.dma("sp", out=gate_bc[:, row, off:off + ln],
                          in_=modall.ap()[rk * 8 + l * 2 + row:rk * 8 + l * 2 + row + 1, cc:cc + ln].broadcast_to([128, ln]),
                          reads=[b_modall], parts=[b_gate])
        P.barrier()

    if DEBUG_STOP == "P":
        dbg = nc.dram_tensor("dbg", [128, 320], F32, kind="ExternalOutput").ap()
        P.dma("sp", out=dbg, in_=modv[:].rearrange("p a b c -> p (a b c)"), reads=[b_modv])
        P.barrier()
        return P
    def mm_group(out_ap, bout, pairs, rd):
        n = len(pairs)
        for i, (lt, rh) in enumerate(pairs):
            P.op("pe", lambda e: e.matmul(out_ap, lhsT=lt, rhs=rh, start=(i == 0), stop=(i == n - 1)),
                 reads=rd, writes=[bout] if i == 0 else (), parts=() if i == 0 else [bout], sig=(i == n - 1))

    def transposes(src_fn, n, M, rd):
        i = tcnt[0] % 2
        tcnt[0] += 1
        for k in range(n):
            P.op("pe", lambda e: e.transpose(out=psT[:, i, k * 128:k * 128 + M], in_=src_fn(k), identity=identb[0:M, 0:M]),
                 reads=rd + [b_const], writes=[bT[i]] if k == 0 else (), parts=() if k == 0 else [bT[i]], sig=(k == n - 1))
        return psT[:, i, 0:n * 128].rearrange("p (h k) -> p h k", k=128)[:, :, 0:M], bT[i]

    epst = P.sb(st, "epst", [128, 1], F32)
    P.op("pool", lambda e: e.memset(epst[:], EPS), writes=[b_const])

    def rstd(out_ap, sum_ap, tmp_ap, n, buf):
        P.op("act", lambda e: e.activation(out=tmp_ap, in_=sum_ap, func=AF.Sqrt, bias=epst[0:out_ap.shape[0], 0:1], scale=1.0 / n),
             reads=[buf, b_const], writes=[buf])
        P.op("dve", lambda e: e.reciprocal(out=out_ap, in_=tmp_ap), reads=[buf], writes=[buf])

    for l in range(n_layers):
      try:
        even = (l % 2 == 0)
        li = l // 2
        with_ctx = l < n_layers - 1
        lam_init = 0.8 - 0.6 * float(np.exp(-0.3 * l))
        w_in = e_w[li] if even else o_w[li]
        NC_IN = 6656 if even else 4416
        GATE0 = NC_IN - 2048
        if even:
            KR, NVC, NQH = 1280, 1280, 24
        else:
            KR, NVC, NQH = 1408, 1280, 20
        R = KR + NVC
        bLB, bG = Buf("LB"), Buf("G")
        nkh_l = 10 if even else 11
        kchunks = [(c * 3, min(3, nkh_l - c * 3)) for c in range((nkh_l + 2) // 3)]
        LBK = [nc.dram_tensor("LBK%d_%d" % (l, c), [nh * 128, TOK], BF16) for c, (h0, nh) in enumerate(kchunks)]
        GK = [nc.dram_tensor("GK%d_%d" % (l, c), [4 * nh * 128, TOK], BF16) for c, (h0, nh) in enumerate(kchunks)]
        VCH = [(0, 256), (256, 256), (512, 256), (768, 320)]
        LBV = [nc.dram_tensor("LBV%d_%d" % (l, c), [n, NVC], BF16) for c, (t0, n) in enumerate(VCH)]
        GV = [nc.dram_tensor("GV%d_%d" % (l, c), [4 * n, NVC], BF16) for c, (t0, n) in enumerate(VCH)]

        def lbk(h):
            return LBK[h // 3].ap()[(h % 3) * 128:(h % 3) * 128 + 128, :]

        def gkr(h):
            return GK[h // 3].ap().rearrange("(s r) c -> r s c", s=4)[(h % 3) * 128:(h % 3) * 128 + 128]

        def vch(t):
            c = min(t // 2, 3)
            return c, t * 128 - VCH[c][0]

        def lbv(t, rows):
            c, r0 = vch(t)
            return LBV[c].ap()[r0:r0 + rows, :]

        def gv(s, t, rows):
            c, r0 = vch(t)
            n = VCH[c][1]
            return GV[c].ap()[s * n + r0:s * n + r0 + rows, :]

        with ExitStack() as ls:
            QT = P.sb(ls, "QT", [128, NQH, TOK], BF16)
            bQT, byT = Buf("QT"), Buf("yT")
            if even:
                P.op("pool", lambda e: e.memset(QT[:], 0.0), writes=[bQT])
            if even:
                P.dma("sp", out=lp[:, 0:256], in_=a_lam[li:li + 1, :].broadcast_to([128, 256]), parts=[b_lp])
                P.dma("sp", out=lp[:, 256:384], in_=a_sub_g[li:li + 1, :].broadcast_to([128, 128]), parts=[b_lp])
                P.dma("sp", out=lp[:, 384:392], in_=b_sink[li:li + 1, :].broadcast_to([128, 8]), parts=[b_lp])
                P.op("dve", lambda e: e.tensor_tensor(out=lp[:, 512:576], in0=lp[:, 0:64], in1=lp[:, 64:128], op=ALU.mult),
                     reads=[b_lp], parts=[b_lp])
                P.op("dve", lambda e: e.tensor_tensor(out=lp[:, 576:640], in0=lp[:, 128:192], in1=lp[:, 192:256], op=ALU.mult),
                     reads=[b_lp], parts=[b_lp])
                P.op("dve", lambda e: e.reduce_sum(out=sm[:, 1:3], in_=lp[:, 512:640].rearrange("p (a b) -> p a b", b=64), axis=AX.X),
                     reads=[b_lp], writes=[b_sm])
                P.op("act", lambda e: e.activation(out=sm[:, 3:5], in_=sm[:, 1:3], func=AF.Exp), reads=[b_sm], writes=[b_sm])
                P.op("act", lambda e: e.activation(out=sm[:, 8:16], in_=lp[:, 384:392], func=AF.Exp), reads=[b_lp, b_sm], writes=[b_sm])
                P.op("dve", lambda e: e.scalar_tensor_tensor(out=sm[:, 0:1], in0=sm[:, 4:5], scalar=-lam_init, in1=sm[:, 3:4],
                                                             op0=ALU.add, op1=ALU.subtract), reads=[b_sm], writes=[b_sm])
                P.op("dve", lambda e: e.tensor_scalar(out=lp[:, 640:768], in0=lp[:, 256:384], scalar1=(1.0 - lam_init), scalar2=None,
                                                      op0=ALU.mult), reads=[b_lp], writes=[b_lp])
            else:
                P.dma("sp", out=lp[:, 0:128], in_=c_q_g[li:li + 1, :].broadcast_to([128, 128]), parts=[b_lp])
                P.dma("sp", out=lp[:, 128:256], in_=c_k_g[li:li + 1, :].broadcast_to([128, 128]), parts=[b_lp])
                P.dma("sp", out=lp[:, 256:768], in_=d_q_a_g[li:li + 1, :].broadcast_to([128, 512]), parts=[b_lp])
                P.dma("sp", out=lp[:, 768:1024], in_=d_kv_a_g[li:li + 1, :].broadcast_to([128, 256]), parts=[b_lp])

            if DEBUG_STOP == "L":
                dbg = nc.dram_tensor("dbg", [128, 64], F32, kind="ExternalOutput").ap()
                P.dma("sp", out=dbg, in_=sm[:], reads=[b_sm, b_lp])
                P.barrier()
                break
            wst = ExitStack()
            nwb[0] = 3 if even else 2
            wcnt[0] = 0
            for i_ in range(nwb[0]):
                wb[i_] = P.sb(wst, "wb%d" % i_, [128, 16, 512], BF16)
            first_c0 = 4096 if even else 1024
            pre_w = load_w(w_in, first_c0, 512)
            with ExitStack() as ph:
                xt = [P.sb(ph, "xt%d" % i, [128, DM], F32) for i in range(2)]
                xn = [P.sb(ph, "xn%d" % i, [128, DM], BF16) for i in range(2)]
                bxt = [Buf("xt0"), Buf("xt1")]
                bxn = [Buf("xn0"), Buf("xn1")]
                ss = P.sb(ph, "ss", [128, 4], F32)
                bss = Buf("ss")
                def nstats(t):
                    i = t % 2
                    so = 4 * i
                    rows = 128 if t < 8 else 64
                    if l == 0:
                        src = x_in[t * 128:(t + 1) * 128, :] if t < 8 else ctx_in[:, :]
                        rd = []
                    else:
                        src = xs[t * 128:t * 128 + rows, :]
                        rd = bxs[t]
                    P.dma("sp", out=xt[i][0:rows, :], in_=src, reads=rd, writes=[bxt[i]])
                    if t == 8:
                        P.op("pool", lambda e: e.memset(xt[i][64:128, :], 0.0), parts=[bxt[i]])
                    P.op("dve", lambda e: e.tensor_tensor(out=xn[i][:], in0=xt[i][:], in1=xt[i][:], op=ALU.mult),
                         reads=[bxt[i]], writes=[bxn[i]])
                    P.op("dve", lambda e: e.reduce_sum(out=ss8[:, so:so + 1], in_=xn[i][:], axis=AX.X), reads=[bxn[i]], writes=[bss8[i]])
                    rstd(ss8[:, so + 2:so + 3], ss8[:, so:so + 1], ss8[:, so + 1:so + 2], DM, bss8[i])
                    P.op("dve", lambda e: e.tensor_scalar(out=xn[i][:], in0=xt[i][:], scalar1=ss8[:, so + 2:so + 3], scalar2=None,
                                                          op0=ALU.mult), reads=[bxt[i], bss8[i]], writes=[bxn[i]])

                def nevac(t):
                    i = t % 2
                    mrow = 0 if t < 8 else 2
                    for half in range(2):
                        pst, bps = transposes(lambda k: xn[i][:, (half * 8 + k) * 128:(half * 8 + k + 1) * 128], 8, 128, [bxn[i]])
                        for k in range(8):
                            j = half * 8 + k
                            P.op("act", lambda e: e.activation(out=hT[:, t, j, :], in_=pst[:, k, :], func=AF.Identity,
                                                               bias=modv[:, l, mrow, j:j + 1], scale=modv[:, l, mrow + 1, j:j + 1]),
                                 reads=[bps, b_modv], parts=[bhT[t]])

                ss8 = P.sb(ph, "ss8", [128, 8], F32)
                bss8 = [Buf("ss8a"), Buf("ss8b")]
                nstats(0)
                for t in range(NTILE):
                    if t + 1 < NTILE:
                        nstats(t + 1)
                    nevac(t)
            P.barrier()
            if DEBUG_STOP == "N":
                dbg = nc.dram_tensor("dbg", [128, 64], F32, kind="ExternalOutput").ap()
                P.dma("sp", out=dbg, in_=sm[:], reads=[b_sm, b_lp])
                P.barrier()
                break

            with ExitStack() as pa:
                KTst = P.sb(pa, "KTst", [128, 11, TOK], BF16)
                bKT = [Buf("KT%d" % h) for h in range(11)]
                u32 = P.sb(pa, "u32", [128, 2048], F32)
                ta = P.sb(pa, "ta", [128, 512], F32)
                tb = P.sb(pa, "tb", [128, 512], F32)
                tn = P.sb(pa, "tn", [128, 512], F32)
                o16 = P.sb(pa, "o16", [128, 1024], BF16)
                v16 = P.sb(pa, "v16", [128, 1024], BF16)
                st8 = P.sb(pa, "st8", [128, 16], F32)
                bu, bta, btb, btn, bo16, bv16, bst8 = (Buf(n) for n in ("u32", "ta", "tb", "tn", "o16", "v16", "st8"))
                if not even:
                    wqb = P.sb(pa, "wqb", [128, 4, 1536], BF16)
                    wkvb = P.sb(pa, "wkvb", [128, 2, 2048], BF16)
                    cT16 = P.sb(pa, "cT16", [128, 4, 128], BF16)
                    bwq, bwkv, bcT16 = Buf("wqb"), Buf("wkvb"), Buf("cT16")
                    P.dma("pool", out=wqb[:], in_=d_w_q_b[li].rearrange("(j p) c -> p j c", p=128), writes=[bwq])
                    P.dma("pool", out=wkvb[:], in_=d_w_kv_b[li].rearrange("(j p) c -> p j c", p=128), writes=[bwkv])

                def rope(src, ncols, G, Dh, t, dst, bsrc, bdst):
                    h = Dh // 2
                    c0, s0, n0 = (0, 96, 192) if Dh == 64 else (32, 128, 224)
                    pat = "p (g two h) -> p g two h"
                    s4 = src[:, 0:ncols].rearrange(pat, two=2, h=h)
                    a4 = ta[:, 0:ncols].rearrange(pat, two=2, h=h)
                    b4 = tb[:, 0:ncols].rearrange(pat, two=2, h=h)
                    cosb = ropet[:, t, c0:c0 + h].unsqueeze(1).unsqueeze(1).broadcast_to([128, G, 2, h])
                    sinb = ropet[:, t, s0:s0 + h].unsqueeze(1).broadcast_to([128, G, h])
                    nsinb = ropet[:, t, n0:n0 + h].unsqueeze(1).broadcast_to([128, G, h])
                    P.op("dve" if even else "pool", lambda e: e.tensor_tensor(out=a4, in0=s4, in1=cosb, op=ALU.mult), reads=[bsrc, b_const], writes=[bta])
                    P.op("dve", lambda e: e.tensor_tensor(out=b4[:, :, 0, :], in0=s4[:, :, 1, :], in1=nsinb, op=ALU.mult),
                         reads=[bsrc, b_const], writes=[btb])
                    P.op("dve", lambda e: e.tensor_tensor(out=b4[:, :, 1, :], in0=s4[:, :, 0, :], in1=sinb, op=ALU.mult),
                         reads=[bsrc, b_const], parts=[btb])
                    P.op("dve", lambda e: e.tensor_tensor(out=dst[:, 0:ncols], in0=ta[:, 0:ncols], in1=tb[:, 0:ncols], op=ALU.add),
                         reads=[bta, btb], writes=[bdst])

                def rmsn(src, ncols, G, Dh, gain, bsrc):
                    s3 = src[:, 0:ncols].rearrange("p (g d) -> p g d", d=Dh)
                    t3 = tn[:, 0:ncols].rearrange("p (g d) -> p g d", d=Dh)
                    P.op("pool", lambda e: e.tensor_tensor(out=ta[:, 0:ncols], in0=src[:, 0:ncols], in1=src[:, 0:ncols], op=ALU.mult),
                         reads=[bsrc], writes=[bta])
                    P.op("dve", lambda e: e.reduce_sum(out=st8[:, 0:G], in_=ta[:, 0:ncols].rearrange("p (g d) -> p g d", d=Dh), axis=AX.X),
                         reads=[bta], writes=[bst8])
                    P.op("act", lambda e: e.activation(out=st8[:, 8:8 + G], in_=st8[:, 0:G], func=AF.Sqrt, bias=epst[:, 0:1], scale=1.0 / Dh),
                         reads=[bst8, b_const], writes=[bst8])
                    P.op("dve", lambda e: e.reciprocal(out=st8[:, 0:G], in_=st8[:, 8:8 + G]), reads=[bst8], writes=[bst8])
                    P.op("dve", lambda e: e.tensor_tensor(out=t3, in0=s3, in1=st8[:, 0:G].unsqueeze(2).broadcast_to([128, G, Dh]), op=ALU.mult),
                         reads=[bsrc, bst8], writes=[btn])
                    P.op("pool", lambda e: e.tensor_tensor(out=t3, in0=t3, in1=gain.unsqueeze(1).broadcast_to([128, G, Dh]), op=ALU.mult),
                         reads=[btn, b_lp], writes=[btn])

                def to_fm(src16, n, t, dstT, h0, bsrc, bdst_list, split=False):
                    ntk = 128 if t < 8 else 64
                    pst, bps = transposes(lambda k: src16[:, k * 128:(k + 1) * 128], n, 128, [bsrc])
                    if split:
                        P.op("act", lambda e: e.copy(out=dstT[0:64, h0:h0 + n, t * 128:t * 128 + ntk], in_=pst[0:64, :, 0:ntk]),
                             reads=[bps], parts=bdst_list)
                        P.op("act", lambda e: e.copy(out=dstT[64:128, 16 + h0:16 + h0 + n, t * 128:t * 128 + ntk], in_=pst[64:128, :, 0:ntk]),
                             reads=[bps], parts=bdst_list)
                    else:
                        P.op("act", lambda e: e.copy(out=dstT[:, h0:h0 + n, t * 128:t * 128 + ntk], in_=pst[:, :, 0:ntk]),
                             reads=[bps], parts=bdst_list)

                def qk_post(ps_ap, bps_, ncols, G, Dh, t, gain, dstT, h0, bdst_list, do_rope=True, split=False):
                    P.op("act", lambda e: e.copy(out=u32[:, 0:ncols], in_=ps_ap), reads=[bps_], writes=[bu])
                    src, bsrc = u32, bu
                    if gain is not None:
                        rmsn(u32, ncols, G, Dh, gain, bu)
                        src, bsrc = tn, btn
                    if do_rope and t < 8:
                        rope(src, ncols, G, Dh, t, o16, bsrc, bo16)
                    else:
                        P.op("dve", lambda e: e.tensor_copy(out=o16[:, 0:ncols], in_=src[:, 0:ncols]), reads=[bsrc], writes=[bo16])
                    to_fm(o16, ncols // 128, t, dstT, h0, bo16, bdst_list, split=split)

                def v_post(ps_ap, bps_, ncols, t, vcol0):
                    rows = 128 if t < 8 else 64
                    P.op("act", lambda e: e.copy(out=v16[:, 0:ncols], in_=ps_ap), reads=[bps_], writes=[bv16])
                    P.dma("sp", out=lbv(t, rows)[:, vcol0:vcol0 + ncols], in_=v16[0:rows, 0:ncols], reads=[bv16], parts=[bLB])

                acc = [0]

                jobs = []

                def inproj(c0, ncols, consume):
                    jobs.append((c0, ncols, consume))

                def flush():
                    wj = [k for k, jb in enumerate(jobs) if jb[0] != "call"]
                    assert jobs[wj[0]][0] == first_c0 and jobs[wj[0]][1] == 512
                    loaded = {wj[0]: pre_w}
                    depth = nwb[0] - 1
                    for k, jb in enumerate(jobs):
                        if jb[0] == "call":
                            jb[1]()
                            continue
                        c0, ncols, consume = jb
                        pos = wj.index(k)
                        for ahead in range(1, depth + 1):
                            if pos + ahead < len(wj):
                                kn = wj[pos + ahead]
                                if kn not in loaded:
                                    loaded[kn] = load_w(w_in, jobs[kn][0], jobs[kn][1])
                        wt, bw = loaded[k]
                        prev = None
                        for t in range(NTILE):
                            i = acc[0] % 2
                            acc[0] += 1
                            bank = psB[:, i * 512:i * 512 + ncols]
                            mm_group(bank, bB[i], [(hT[:, t, j, :], wt[:, j, 0:ncols]) for j in range(16)], [bhT[t], bw])
                            if prev is not None:
                                consume(*prev)
                            prev = (t, bank, bB[i])
                        consume(*prev)
                    del jobs[:]

                def xch(heads, kch, vch):
                    for h in heads:
                        P.dma("sp", out=lbk(h), in_=KTst[:, h, :], reads=[bKT[h]], parts=[bLB])
                    for c in kch:
                        allgather(LBK[c], GK[c], [bLB], [bG])
                    for c in vch:
                        allgather(LBV[c], GV[c], [bLB], [bG])

                if even:
                    def bkv(t, bk, bb):
                        qk_post(bk[:, 0:256], bb, 256, 2, 128, t, None, KTst, 8, bKT[8:10])
                        v_post(bk[:, 256:512], bb, 256, t, 1024)
                    inproj(4096, 512, bkv)
                    for hb in range(2):
                        inproj(2048 + hb * 512, 512, lambda t, bk, bb, hb=hb: v_post(bk, bb, 512, t, hb * 512))
                    jobs.append(("call", lambda: xch([9], [3], [0, 1])))
                    inproj(1024, 512, lambda t, bk, bb: qk_post(bk, bb, 512, 8, 64, t, None, KTst, 0, bKT[0:4]))
                    jobs.append(("call", lambda: xch([0, 1, 2], [0], [2, 3])))
                    inproj(1536, 512, lambda t, bk, bb: qk_post(bk, bb, 512, 8, 64, t, None, KTst, 4, bKT[4:8]))
                    jobs.append(("call", lambda: xch([3, 4, 5, 6, 7, 8], [1, 2], [])))
                    nkh = 10
                else:
                    def ckv(t, bk, bb):
                        qk_post(bk[:, 0:256], bb, 256, 2, 128, t, lp[:, 128:256], KTst, 0, bKT[0:2])
                        v_post(bk[:, 256:512], bb, 256, t, 0)
                    inproj(1024, 512, ckv)

                    def dkv(t, bk, bb):
                        P.op("act", lambda e: e.copy(out=u32[:, 0:320], in_=bk[:, 0:320]), reads=[bb], writes=[bu])
                        rmsn(u32, 256, 1, 256, lp[:, 768:1024], bu)
                        P.op("dve", lambda e: e.tensor_copy(out=o16[:, 0:256], in_=tn[:, 0:256]), reads=[btn], writes=[bo16])
                        pst, bps = transposes(lambda k: o16[:, k * 128:(k + 1) * 128], 2, 128, [bo16])
                        P.op("act", lambda e: e.copy(out=cT16[:, 0:2, :], in_=pst), reads=[bps], writes=[bcT16])
                        for nb in range(4):
                            mm_group(psA[:, nb * 512:(nb + 1) * 512], bA[nb],
                                     [(cT16[:, kc, :], wkvb[:, kc, nb * 512:(nb + 1) * 512]) for kc in range(2)], [bcT16, bwkv])
                        if t < 8:
                            rope(u32[:, 256:320], 64, 1, 64, t, o16[:, 0:64], bu, bo16)
                        else:
                            P.op("dve", lambda e: e.tensor_copy(out=o16[:, 0:64], in_=u32[:, 256:320]), reads=[bu], writes=[bo16])
                        P.op("dve", lambda e: e.tensor_copy(out=o16[:, 64:128], in_=o16[:, 0:64]), reads=[bo16], parts=[bo16])
                        to_fm(o16, 1, t, KTst, 10, bo16, [bKT[10]])
                        for nb in range(4):
                            kv4 = psA[:, nb * 512:(nb + 1) * 512].rearrange("p (h two d) -> p h two d", two=2, d=128)
                            P.op("act", lambda e: e.copy(out=o16[:, nb * 256:(nb + 1) * 256].rearrange("p (h d) -> p h d", d=128), in_=kv4[:, :, 0, :]),
                                 reads=[bA[nb]], writes=[bo16] if nb == 0 else (), parts=() if nb == 0 else [bo16])
                            P.op("dve", lambda e: e.tensor_copy(out=v16[:, nb * 256:(nb + 1) * 256].rearrange("p (h d) -> p h d", d=128), in_=kv4[:, :, 1, :]),
                                 reads=[bA[nb]], writes=[bv16] if nb == 0 else (), parts=() if nb == 0 else [bv16])
                        to_fm(o16, 8, t, KTst, 2, bo16, bKT[2:10])
                        rows = 128 if t < 8 else 64
                        P.dma("sp", out=lbv(t, rows)[:, 256:1280], in_=v16[0:rows, 0:1024], reads=[bv16], parts=[bLB])
                    inproj(2048, 320, dkv)
                    nkh = 11
                if not even:
                    jobs.append(("call", lambda: xch(list(range(11)), [0, 1, 2], [])))

                if even:
                    for hb in range(2):
                        inproj(hb * 512, 512, lambda t, bk, bb, hb=hb: qk_post(bk, bb, 512, 8, 64, t, None, QT, hb * 4, [bQT], split=True))
                    for hb in range(2):
                        inproj(3072 + hb * 512, 512, lambda t, bk, bb, hb=hb: qk_post(bk, bb, 512, 4, 128, t, None, QT, 8 + hb * 4, [bQT]))
                else:
                    for hb in range(2):
                        inproj(hb * 512, 512, lambda t, bk, bb, hb=hb: qk_post(bk, bb, 512, 4, 128, t, lp[:, 0:128], QT, hb * 4, [bQT]))
                        jobs.append(("call", (lambda: xch([], [3], [0, 1])) if hb == 0 else (lambda: xch([], [], [2, 3]))))

                    def dq(t, bk, bb):
                        P.op("act", lambda e: e.copy(out=u32[:, 0:512], in_=bk), reads=[bb], writes=[bu])
                        rmsn(u32, 512, 1, 512, lp[:, 256:768], bu)
                        P.op("dve", lambda e: e.tensor_copy(out=o16[:, 0:512], in_=tn[:, 0:512]), reads=[btn], writes=[bo16])
                        pst, bps = transposes(lambda k: o16[:, k * 128:(k + 1) * 128], 4, 128, [bo16])
                        P.op("act", lambda e: e.copy(out=cT16[:], in_=pst), reads=[bps], writes=[bcT16])
                        for nb in range(3):
                            mm_group(psA[:, nb * 512:(nb + 1) * 512], bA[nb],
                                     [(cT16[:, kc, :], wqb[:, kc, nb * 512:(nb + 1) * 512]) for kc in range(4)], [bcT16, bwq])
                        for nb in range(3):
                            P.op("act", lambda e: e.copy(out=u32[:, nb * 512:(nb + 1) * 512], in_=psA[:, nb * 512:(nb + 1) * 512]),
                                 reads=[bA[nb]], writes=[bu] if nb == 0 else (), parts=() if nb == 0 else [bu])
                        q3 = u32[:, 0:1536].rearrange("p (h d) -> p h d", d=192)
                        P.op("dve", lambda e: e.tensor_copy(out=o16[:, 0:1024].rearrange("p (h d) -> p h d", d=128), in_=q3[:, :, 0:128]),
                             reads=[bu], writes=[bo16])
                        P.op("pool", lambda e: e.tensor_copy(out=tn[:, 0:512].rearrange("p (h d) -> p h d", d=64), in_=q3[:, :, 128:192]),
                             reads=[bu], writes=[btn])
                        to_fm(o16, 8, t, QT, 8, bo16, [bQT])
                        if t < 8:
                            rope(tn, 512, 8, 64, t, v16, btn, bv16)
                        else:
                            P.op("dve", lambda e: e.tensor_copy(out=v16[:, 0:512], in_=tn[:, 0:512]), reads=[btn], writes=[bv16])
                        to_fm(v16, 4, t, QT, 16, bv16, [bQT])
                    inproj(1536, 512, dq)
                flush()
            wst.close()
            P.barrier()
            if DEBUG_STOP == "A":
                dbg = nc.dram_tensor("dbg", [128, 64], F32, kind="ExternalOutput").ap()
                P.dma("sp", out=dbg, in_=sm[:], reads=[b_sm, b_lp])
                P.barrier()
                break

            pbc = ExitStack()
            yT = P.sb(pbc, "yT", [128, 16, TOK], BF16)
            with ExitStack() as pb:
                KT = [P.sb(pb, "KT", [128, NKEY], BF16) for _ in range(2)]
                Vt = [P.sb(pb, "Vt", [128, NKT, 130], BF16) for _ in range(2)]
                PT = [P.sb(pb, "PT", [128, 2, 512], BF16) for _ in range(2)]
                osb = P.sb(pb, "osb", [128, 1, 4, 128], F32)
                y16 = P.sb(pb, "y16", [128, 4, 128], BF16)
                Pacc = P.sb(pb, "Pacc", [128, 2, 512], F32)
                rcs = P.sb(pb, "rcs", [128, 512], F32)
                onesf = P.sb(pb, "onesf", [128, 128], F32)
                bPacc, brcs, bones = [Buf("Pacc0"), Buf("Pacc1")], Buf("rcs"), Buf("onesf")
                PT4 = [P.sb(pb, "PT4", [128, 512], BF16) for _ in range(4)]
                pair16 = [P.sb(pb, "pair16", [128, 512], BF16) for _ in range(2)]
                bpair = [Buf("pair0"), Buf("pair1")]
                bPT4 = [Buf("PT4_%d" % i_) for i_ in range(4)]
                P.op("pool", lambda e: e.memset(onesf[:], 1.0), writes=[bones])
                if even:
                    oT = P.sb(pb, "oT", [128, 2, 512], F32)
                    dT = P.sb(pb, "dT", [128, 512], F32)
                    sqT = P.sb(pb, "sqT", [128, 512], F32)
                    subgc = P.sb(pb, "subgc", [128, 1], F32)
                    boT, bdT, bsqT, bsubg = Buf("oT"), Buf("dT"), Buf("sqT"), Buf("subgc")
                    P.dma("sp", out=subgc[:], in_=a_sub_g[li, :].rearrange("(p o) -> p o", o=1), writes=[bsubg])
                    P.op("dve", lambda e: e.tensor_scalar(out=subgc[:], in0=subgc[:], scalar1=(1.0 - lam_init), scalar2=None, op0=ALU.mult),
                         reads=[bsubg], writes=[bsubg])
                rc = P.sb(pb, "rc", [128, 16], F32)
                bK, bV, bPT = [Buf("K0"), Buf("K1")], [Buf("V0"), Buf("V1")], [Buf("PT0"), Buf("PT1")]
                bo, by16, brc, bkpe = (Buf(n) for n in ("osb", "y16", "rc", "kpe"))
                by16b = [Buf("y16b0"), Buf("y16b1")]
                for i in range(2):
                    P.op("pool", lambda e: e.memset(Vt[i][:, :, 128:130], 1.0), writes=[bV[i]])
                kvn = [0]
                pcn = [0]
                ocnt = [0]
                pend = [None]

                def load_k(dst, krow0, bdst):
                    P.dma("sp", out=dst[:, 0:4096].rearrange("p (s c) -> p s c", s=4), in_=gkr(krow0 // 128)[:, :, 0:1024],
                          reads=[bG], parts=[bdst])
                    P.dma("sp", out=dst[:, 4096:4352].rearrange("p (s c) -> p s c", s=4), in_=gkr(krow0 // 128)[:, :, 1024:1088],
                          reads=[bG], parts=[bdst])

                def load_kv(krow0, vcol0):
                    i = kvn[0] % 2
                    kvn[0] += 1
                    load_k(KT[i], krow0, bK[i])
                    for s_ in range(4):
                        for c in range(4):
                            P.dma("sp", out=Vt[i][:, s_ * 8 + 2 * c:s_ * 8 + 2 * c + 2, 0:128],
                                  in_=gv(s_, 2 * c, 256)[:, vcol0:vcol0 + 128].rearrange("(k p) c -> p k c", p=128), reads=[bG], parts=[bV[i]])
                        P.dma("sp", out=Vt[i][(s_ % 2) * 64:(s_ % 2) * 64 + 64, 32 + s_ // 2, 0:128],
                              in_=gv(s_, 8, 64)[:, vcol0:vcol0 + 128], reads=[bG], parts=[bV[i]])
                    return i

                def attend(pieces, rdK, Vi, kts, nq, sc, oi, sink_ap=None, masks=None, tilew=512):
                    nqs = max(1, nq // 128)
                    M = min(nq, 128)
                    gsz = 2 if tilew == 512 else 8
                    groups = [kts[a:a + gsz] for a in range(0, len(kts), gsz)]
                    ng = len(groups)
                    pbase = pcn[0]
                    pcn[0] += ng

                    def pv(gi, pview):
                        grp = groups[gi]
                        n = len(grp)
                        pk = (pbase + gi) % 2
                        for ki, kt in enumerate(grp):
                            for qs in range(nqs):
                                first = gi == 0 and ki == 0
                                last = gi == ng - 1 and ki == n - 1
                                ob = qs // 2
                                col = ob * 512 + (qs % 2) * 129
                                st_ = first and qs % 2 == 0
                                P.op("pe", lambda e: e.matmul(psB[0:M, col:col + 129], lhsT=pview(pk, ki, qs), rhs=Vt[Vi][:, kt, 0:129],
                                                              start=st_, stop=last, skip_group_check=True),
                                     reads=[bPT[pk], bV[Vi]], writes=[bB[ob]] if st_ else (), parts=() if st_ else [bB[ob]], sig=last)

                    if tilew == 512:
                        def qk(gi):
                            sg = gi % 2
                            for ki, kt in enumerate(groups[gi]):
                                bi = 2 * sg + ki
                                mm_group(psA[0:128, bi * 512:bi * 512 + nq], bA[bi], [(kf(kt), qa) for kf, qa in pieces], rdK)

                        def ex(gi):
                            sg = gi % 2
                            n = len(groups[gi])
                            pk = (pbase + gi) % 2
                            P.op("act", lambda e: e.activation(out=PT[pk][:, 0:n, 0:nq],
                                                               in_=psA[:, 2 * sg * 512:(2 * sg + n) * 512].rearrange("p (k c) -> p k c", c=512)[:, :, 0:nq],
                                                               func=AF.Exp, scale=sc),
                                 reads=[bA[2 * sg + a] for a in range(n)], writes=[bPT[pk]])

                        pview5 = lambda pk, ki, qs: PT[pk][:, ki, qs * 128:qs * 128 + M]
                        qk(0)
                        for gi in range(ng):
                            if gi + 1 < ng:
                                qk(gi + 1)
                            ex(gi)
                            pv(gi, pview5)
                    else:
                        for gi, grp in enumerate(groups):
                            n = len(grp)
                            pk = (pbase + gi) % 2
                            for ki, kt in enumerate(grp):
                                bi = ki // 4
                                mm_group(psA[0:128, ki * 128:ki * 128 + nq], bA[bi], [(kf(kt), qa) for kf, qa in pieces], rdK)
                            pflat = PT[pk][:].rearrange("p a b -> p (a b)")
                            P.op("act", lambda e: e.activation(out=pflat[:, 0:n * 128].rearrange("p (k c) -> p k c", c=128)[:, :, 0:nq],
                                                               in_=psA[:, 0:n * 128].rearrange("p (k c) -> p k c", c=128)[:, :, 0:nq],
                                                               func=AF.Exp, scale=sc),
                                 reads=[bA[0], bA[1]], writes=[bPT[pk]])
                            if masks is not None:
                                for ki, kt in enumerate(grp):
                                    if masks[ki] is not None:
                                        P.op("dve", lambda e: e.tensor_tensor(out=pflat[:, ki * 128:ki * 128 + nq], in0=pflat[:, ki * 128:ki * 128 + nq],
                                                                              in1=maskb[:, masks[ki], 0:nq], op=ALU.mult),
                                             reads=[bPT[pk], b_const], writes=[bPT[pk]])
                            pv(gi, lambda pk_, ki, qs: PT[pk_][:].rearrange("p a b -> p (a b)")[:, ki * 128:ki * 128 + M])
                    for qs in range(nqs):
                        ob = qs // 2
                        col = ob * 512 + (qs % 2) * 129
                        if sink_ap is None:
                            P.op("dve", lambda e: e.reciprocal(out=rc[0:M, qs:qs + 1], in_=psB[0:M, col + 128:col + 129]), reads=[bB[ob]], writes=[brc])
                        else:
                            P.op("dve", lambda e: e.tensor_tensor(out=rc[0:M, 8 + qs:9 + qs], in0=psB[0:M, col + 128:col + 129], in1=sink_ap, op=ALU.add),
                                 reads=[bB[ob], b_sm], writes=[brc])
                            P.op("dve", lambda e: e.reciprocal(out=rc[0:M, qs:qs + 1], in_=rc[0:M, 8 + qs:9 + qs]), reads=[brc], writes=[brc])
                        P.op("dve", lambda e: e.tensor_scalar(out=osb[0:M, oi, qs, :], in0=psB[0:M, col:col + 128], scalar1=rc[0:M, qs:qs + 1],
                                                              scalar2=None, op0=ALU.mult), reads=[bB[ob], brc], parts=[bo])

                def emit_y(src, bsrc, chunk, q0, nq):
                    nqs = max(1, nq // 128)
                    M = min(nq, 128)
                    P.op("pool", lambda e: e.tensor_copy(out=y16[0:M, 0:nqs, :], in_=src[0:M, 0:nqs, :]), reads=[bsrc], writes=[by16])
                    pst, bps = transposes(lambda k: y16[0:M, k, :], nqs, M, [by16])
                    P.op("act", lambda e: e.copy(out=yT[:, chunk, q0:q0 + nq].rearrange("p (k m) -> p k m", m=M), in_=pst), reads=[bps], parts=[byT])

                def attendT(pieces, rdK, Vi, kts, nq, sc, fin):
                    nk = len(kts)
                    pbase = pcn[0]
                    pcn[0] += nk

                    ob = ocnt[0] % 2
                    ocnt[0] += 1
                    o_ap, o_buf = psB[:, ob * 512:ob * 512 + nq], bB[ob]
                    pa_ = Pacc[:, ob, 0:nq]
                    bpa_ = bPacc[ob]

                    def qk(k):
                        bi = k % 3
                        mm_group(psA[0:128, bi * 512:bi * 512 + nq], bA[bi], [(kf(kts[k]), qa) for kf, qa in pieces], rdK)

                    for k in range(min(2, nk)):
                        qk(k)
                    if pend[0] is not None:
                        pend[0]()
                        pend[0] = None
                    for k in range(nk):
                        if k + 2 < nk:
                            qk(k + 2)
                        bi = k % 3
                        pk = (pbase + k) % 4
                        P.op("act", lambda e: e.activation(out=PT4[pk][:, 0:nq], in_=psA[:, bi * 512:bi * 512 + nq], func=AF.Exp, scale=sc),
                             reads=[bA[bi]], writes=[bPT4[pk]])
                        first = k == 0
                        last = k == nk - 1
                        P.op("pe", lambda e: e.matmul(o_ap, lhsT=Vt[Vi][:, kts[k], 0:128], rhs=PT4[pk][:, 0:nq], start=first, stop=last),
                             reads=[bPT4[pk], bV[Vi]], writes=[o_buf] if first else (), parts=() if first else [o_buf], sig=last)
                        if k % 2 == 1:
                            pkp = (pbase + k - 1) % 4
                            j2 = (k // 2) % 2
                            P.op("dve", lambda e: e.tensor_tensor(out=pair16[j2][:, 0:nq], in0=PT4[pkp][:, 0:nq], in1=PT4[pk][:, 0:nq], op=ALU.add),
                                 reads=[bPT4[pkp], bPT4[pk]], writes=[bpair[j2]])
                            if k == 1:
                                P.op("dve", lambda e: e.tensor_copy(out=pa_, in_=pair16[j2][:, 0:nq]), reads=[bpair[j2]], writes=[bpa_])
                            else:
                                P.op("dve", lambda e: e.tensor_tensor(out=pa_, in0=pa_, in1=pair16[j2][:, 0:nq], op=ALU.add),
                                     reads=[bpair[j2], bpa_], writes=[bpa_])
                        elif k == nk - 1:
                            if k == 0:
                                P.op("dve", lambda e: e.tensor_copy(out=pa_, in_=PT4[pk][:, 0:nq]), reads=[bPT4[pk]], writes=[bpa_])
                            else:
                                P.op("dve", lambda e: e.tensor_tensor(out=pa_, in0=pa_, in1=PT4[pk][:, 0:nq], op=ALU.add),
                                     reads=[bPT4[pk], bpa_], writes=[bpa_])
                    def epi():
                        mm_group(psA[:, 1536:1536 + nq], bA[3], [(onesf[:], pa_)], [bones, bpa_])
                        P.op("dve", lambda e: e.reciprocal(out=rcs[:, 0:nq], in_=psA[:, 1536:1536 + nq]), reads=[bA[3]], writes=[brcs])
                        fin(o_ap, o_buf)
                    pend[0] = epi

                def flush_pend():
                    if pend[0] is not None:
                        pend[0]()
                        pend[0] = None

                ALLK = list(range(NKT))
                qblocks = [(0, 512, ALLK), (512, 512, ALLK)] + ([(1024, 64, [32, 33])] if with_ctx else [])
                if even:
                    sc = 64 ** -0.5
                    for h in range(8):
                        Ki = load_kv(h * 128, h * 128)
                        for q0, nq, kts in qblocks:
                            nqs = max(1, nq // 128)
                            M = min(nq, 128)
                            for pr in range(2):
                                attendT([(lambda kt: KT[Ki][:, kt * 128:(kt + 1) * 128], QT[:, h + 16 * pr, q0:q0 + nq])],
                                        [bK[Ki], bQT], Ki, kts, nq, sc,
                                        lambda o_ap, o_buf, pr=pr, nq=nq: P.op("dve", lambda e: e.tensor_tensor(out=oT[:, pr, 0:nq], in0=o_ap, in1=rcs[:, 0:nq], op=ALU.mult),
                                                                           reads=[o_buf, brcs], parts=[boT]))
                            def subnorm(h=h, q0=q0, nq=nq):
                                P.op("dve", lambda e: e.scalar_tensor_tensor(out=dT[:, 0:nq], in0=oT[:, 1, 0:nq], scalar=sm[:, 0:1], in1=oT[:, 0, 0:nq],
                                                                             op0=ALU.mult, op1=ALU.add), reads=[boT, b_sm], writes=[bdT])
                                P.op("pool", lambda e: e.tensor_tensor(out=sqT[:, 0:nq], in0=dT[:, 0:nq], in1=dT[:, 0:nq], op=ALU.mult), reads=[bdT], writes=[bsqT])
                                mm_group(psA[:, 1536:1536 + nq], bA[3], [(onesf[:], sqT[:, 0:nq])], [bones, bsqT])
                                P.op("act", lambda e: e.activation(out=sqT[:, 0:nq], in_=psA[:, 1536:1536 + nq], func=AF.Ln, bias=epst[:, 0:1], scale=1.0 / 128),
                                     reads=[bA[3], b_const], writes=[bsqT])
                                P.op("act", lambda e: e.activation(out=sqT[:, 0:nq], in_=sqT[:, 0:nq], func=AF.Exp, scale=-0.5), reads=[bsqT], writes=[bsqT])
                                P.op("dve", lambda e: e.scalar_tensor_tensor(out=yT[:, h, q0:q0 + nq], in0=dT[:, 0:nq], scalar=subgc[:, 0:1], in1=sqT[:, 0:nq],
                                                                             op0=ALU.mult, op1=ALU.mult), reads=[bdT, bsqT, bsubg], parts=[byT])
                            prev_ = pend[0]
                            pend[0] = (lambda prev_=prev_, sn_=subnorm: (prev_(), sn_()))
                    flush_pend()
                    sc = 128 ** -0.5
                    for hk in range(2):
                        Ki = kvn[0] % 2
                        kvn[0] += 1
                        kr = (8 + hk) * 128
                        vc = 1024 + hk * 128
                        P.dma("sp", out=KT[Ki][:, 0:1024], in_=lbk(8 + hk)[:, 0:1024], reads=[bLB], parts=[bK[Ki]])
                        P.dma("sp", out=KT[Ki][:, 1024:1536].rearrange("p (s c) -> p s c", s=4), in_=gkr(8 + hk)[:, :, 896:1024], reads=[bG], parts=[bK[Ki]])
                        P.dma("sp", out=KT[Ki][:, 1536:2048].rearrange("p (s c) -> p s c", s=4), in_=gkr(8 + hk)[:, :, 0:128], reads=[bG], parts=[bK[Ki]])
                        P.dma("sp", out=KT[Ki][:, 2048:2304].rearrange("p (s c) -> p s c", s=4), in_=gkr(8 + hk)[:, :, 1024:1088], reads=[bG], parts=[bK[Ki]])
                        for c in range(4):
                            P.dma("sp", out=Vt[Ki][:, 2 * c:2 * c + 2, 0:128], in_=lbv(2 * c, 256)[:, vc:vc + 128].rearrange("(k p) c -> p k c", p=128),
                                  reads=[bLB], parts=[bV[Ki]])
                        for s_ in range(4):
                            P.dma("sp", out=Vt[Ki][:, 8 + s_, 0:128], in_=gv(s_, 7, 128)[:, vc:vc + 128], reads=[bG], parts=[bV[Ki]])
                            P.dma("sp", out=Vt[Ki][:, 12 + s_, 0:128], in_=gv(s_, 0, 128)[:, vc:vc + 128], reads=[bG], parts=[bV[Ki]])
                            P.dma("sp", out=Vt[Ki][(s_ % 2) * 64:(s_ % 2) * 64 + 64, 16 + s_ // 2, 0:128], in_=gv(s_, 8, 64)[:, vc:vc + 128],
                                  reads=[bG], parts=[bV[Ki]])
                        units = []
                        for g in range(4):
                            h = hk * 4 + g
                            for t in range(8 + (1 if with_ctx else 0)):
                                if t == 8:
                                    kts, mk, nq = [16, 17], [None, None], 64
                                else:
                                    kts, mk, nq = [16, 17, t], [None, None, None], 128
                                    if t > 0:
                                        kts.append(t - 1); mk.append(0)
                                    else:
                                        kts += [8, 9, 10, 11]; mk += [2, 3, 4, 5]
                                    if t < 7:
                                        kts.append(t + 1); mk.append(1)
                                    else:
                                        kts += [12, 13, 14, 15]; mk += [6, 7, 8, 9]
                                units.append((h, t, kts, mk, nq, len(units)))

                        def b_stage1(u):
                            h, t, kts, mk, nq, idx = u
                            so = (idx % 2) * 2
                            pk = idx % 2
                            n = len(kts)
                            for ki, kt in enumerate(kts):
                                col = so * 512 + ki * 128
                                mm_group(psA[0:128, col:col + nq], bA[so + ki // 4],
                                         [(KT[Ki][:, kt * 128:(kt + 1) * 128], QT[:, 8 + h, t * 128:t * 128 + nq])], [bK[Ki], bQT])
                            pflat = PT[pk][:].rearrange("p a b -> p (a b)")
                            P.op("act", lambda e: e.activation(out=pflat[:, 0:n * 128].rearrange("p (k c) -> p k c", c=128)[:, :, 0:nq],
                                                               in_=psA[:, so * 512:so * 512 + n * 128].rearrange("p (k c) -> p k c", c=128)[:, :, 0:nq],
                                                               func=AF.Exp, scale=sc),
                                 reads=[bA[so], bA[so + 1]], writes=[bPT[pk]])
                            for ki, kt in enumerate(kts):
                                if mk[ki] is not None:
                                    P.op("pool" if ki % 2 else "dve",
                                         lambda e: e.tensor_tensor(out=pflat[:, ki * 128:ki * 128 + nq], in0=pflat[:, ki * 128:ki * 128 + nq],
                                                                   in1=maskb[:, mk[ki], 0:nq], op=ALU.mult),
                                         reads=[bPT[pk], b_const], writes=[bPT[pk]])

                        def b_stage2(u):
                            h, t, kts, mk, nq, idx = u
                            pk = idx % 2
                            ob = idx % 2
                            M = min(nq, 128)
                            n = len(kts)
                            pflat = PT[pk][:].rearrange("p a b -> p (a b)")
                            acc_ = psB[0:M, ob * 512:ob * 512 + 129]
                            mm_group(acc_, bB[ob], [(pflat[:, ki * 128:ki * 128 + M], Vt[Ki][:, kt, 0:129]) for ki, kt in enumerate(kts)], [bPT[pk], bV[Ki]])
                            P.op("dve", lambda e: e.tensor_tensor(out=rc[0:M, 8 + ob:9 + ob], in0=acc_[:, 128:129], in1=sm[0:M, 8 + h:9 + h], op=ALU.add),
                                 reads=[bB[ob], b_sm], writes=[brc])
                            P.op("dve", lambda e: e.reciprocal(out=rc[0:M, ob:ob + 1], in_=rc[0:M, 8 + ob:9 + ob]), reads=[brc], writes=[brc])
                            P.op("dve", lambda e: e.tensor_scalar(out=y16[0:M, ob, :], in0=acc_[:, 0:128], scalar1=rc[0:M, ob:ob + 1], scalar2=None, op0=ALU.mult),
                                 reads=[bB[ob], brc], writes=[by16b[ob]])

                        def b_stage3(u):
                            h, t, kts, mk, nq, idx = u
                            ob = idx % 2
                            M = min(nq, 128)
                            pst, bps = transposes(lambda k_: y16[0:M, ob, :], 1, M, [by16b[ob]])
                            P.op("act", lambda e: e.copy(out=yT[:, 8 + h, t * 128:t * 128 + nq].rearrange("p (k m) -> p k m", m=M), in_=pst), reads=[bps], parts=[byT])

                        b_stage1(units[0])
                        for ui in range(len(units)):
                            if ui + 1 < len(units):
                                b_stage1(units[ui + 1])
                            b_stage2(units[ui])
                            if ui >= 1:
                                b_stage3(units[ui - 1])
                        b_stage3(units[-1])
                else:
                    sc = 128 ** -0.5
                    for hk in range(2):
                        Ki = load_kv(hk * 128, hk * 128)
                        for g in range(4):
                            h = hk * 4 + g
                            for q0, nq, kts in qblocks:
                                attendT([(lambda kt: KT[Ki][:, kt * 128:(kt + 1) * 128], QT[:, h, q0:q0 + nq])], [bK[Ki], bQT], Ki, kts, nq, sc,
                                        lambda o_ap, o_buf, h=h, q0=q0, nq=nq: P.op("dve", lambda e: e.tensor_tensor(out=yT[:, h, q0:q0 + nq], in0=o_ap, in1=rcs[:, 0:nq], op=ALU.mult),
                                                                                     reads=[o_buf, brcs], parts=[byT]))
                    kpeL = P.sb(pb, "kpeL", [128, NKEY], BF16)
                    kpeH = P.sb(pb, "kpeH", [128, NKEY], BF16)
                    P.op("pool", lambda e: e.memset(kpeL[64:128, :], 0.0), parts=[bkpe])
                    P.op("pool", lambda e: e.memset(kpeH[0:64, :], 0.0), parts=[bkpe])
                    for kx, r0 in ((kpeL, 0), (kpeH, 64)):
                        P.dma("sp", out=kx[r0:r0 + 64, 0:4096].rearrange("p (s c) -> p s c", s=4), in_=gkr(10)[r0:r0 + 64, :, 0:1024],
                              reads=[bG], parts=[bkpe])
                        P.dma("sp", out=kx[r0:r0 + 64, 4096:4352].rearrange("p (s c) -> p s c", s=4), in_=gkr(10)[r0:r0 + 64, :, 1024:1088],
                              reads=[bG], parts=[bkpe])
                    sc = 192 ** -0.5
                    for h in range(8):
                        Ki = load_kv(256 + h * 128, 256 + h * 128)
                        kpx = kpeL if h % 2 == 0 else kpeH
                        for q0, nq, kts in qblocks:
                            attendT([(lambda kt: KT[Ki][:, kt * 128:(kt + 1) * 128], QT[:, 8 + h, q0:q0 + nq]),
                                     (lambda kt: kpx[:, kt * 128:(kt + 1) * 128], QT[:, 16 + h // 2, q0:q0 + nq])],
                                    [bK[Ki], bkpe, bQT], Ki, kts, nq, sc,
                                    lambda o_ap, o_buf, h=h, q0=q0, nq=nq: P.op("dve", lambda e: e.tensor_tensor(out=yT[:, 8 + h, q0:q0 + nq], in0=o_ap, in1=rcs[:, 0:nq], op=ALU.mult),
                                                                                 reads=[o_buf, brcs], parts=[byT]))
                    flush_pend()
            P.barrier()
            if DEBUG_STOP == "B":
                pbc.close()
            if DEBUG_STOP == "B":
                dbg = nc.dram_tensor("dbg", [128, 64], F32, kind="ExternalOutput").ap()
                P.dma("sp", out=dbg, in_=sm[:], reads=[b_sm, b_lp])
                P.barrier()
                break

            with ExitStack() as pc_:
                nwb[0] = 2
                wcnt[0] = 0
                wb[0] = P.sb(pc_, "wb0", [128, 16, 512], BF16)
                wb[1] = P.sb(pc_, "wb1", [128, 16, 512], BF16)
                gate_bc = P.sb(pc_, "gate_bc", [128, 2, DM], F32)
                gs = P.sb(pc_, "gs", [128, 512], BF16)
                xp = [P.sb(pc_, "xp", [128, 512], F32) for _ in range(2)]
                xg = [P.sb(pc_, "xg", [128, 512], F32) for _ in range(2)]
                bgs, bxp, bxg = Buf("gs"), [Buf("xp0"), Buf("xp1")], [Buf("xg0"), Buf("xg1")]
                b_gate.w, b_gate.wf, b_gate.r = {}, {}, {}
                load_gate(l, gate_bc)
                ntl = NTILE if with_ctx else 8
                tgroups = [(0, 4, 512), (4, 4, 512)] + ([(8, 1, 64)] if with_ctx else [])
                ai = 0
                cjobs = [("g", k_) for k_ in range(4)] + [("o", k_) for k_ in range(4)]

                def cload(j):
                    kind, k_ = cjobs[j]
                    return load_w(w_in, GATE0 + k_ * 512, 512) if kind == "g" else load_w(w_o[l], k_ * 512, 512)
                cw = {0: cload(0)}
                for blk in range(4):
                    cw[blk + 1] = cload(blk + 1)
                    wt, bw = cw[blk]
                    for mm in range(4):
                        m = blk * 4 + mm
                        for t0, nt, ntok in tgroups:
                            i = ai % 2
                            ai += 1
                            if nt == 4:
                                rhs_f = lambda j: hT[:, t0:t0 + 4, j, :]
                            else:
                                rhs_f = lambda j: hT[:, 8, j, 0:64]
                            mm_group(psB[:, i * 512:i * 512 + ntok], bB[i], [(wt[:, j, mm * 128:(mm + 1) * 128], rhs_f(j)) for j in range(16)],
                                     [bw] + bhT)
                            P.op("act", lambda e: e.activation(out=gs[:, 0:ntok], in_=psB[:, i * 512:i * 512 + ntok], func=AF.Silu),
                                 reads=[bB[i]], writes=[bgs])
                            P.op("dve", lambda e: e.tensor_tensor(out=yT[:, m, t0 * 128:t0 * 128 + ntok], in0=yT[:, m, t0 * 128:t0 * 128 + ntok],
                                                                  in1=gs[:, 0:ntok], op=ALU.mult), reads=[bgs, byT], parts=[byT])
                for nb in range(4):
                    if nb + 1 < 4:
                        cw[4 + nb + 1] = cload(4 + nb + 1)
                    wt, bw = cw[4 + nb]
                    for t in range(ntl):
                        rows = 128 if t < 8 else 64
                        i = ai % 2
                        ai += 1
                        if l == 0:
                            src = x_in[t * 128:(t + 1) * 128, nb * 512:(nb + 1) * 512] if t < 8 else ctx_in[:, nb * 512:(nb + 1) * 512]
                            rd = []
                        else:
                            src = xs[t * 128:t * 128 + rows, nb * 512:(nb + 1) * 512]
                            rd = [bxs[t][nb]]
                        P.dma("sp", out=xp[i][0:rows, :], in_=src, reads=rd, writes=[bxp[i]])
                        mm_group(psB[0:rows, i * 512:(i + 1) * 512], bB[i],
                                 [(yT[:, m, t * 128:t * 128 + rows], wt[:, m, :]) for m in range(16)], [bw, byT])
                        grow = 0 if t < 8 else 1
                        P.op("dve", lambda e: e.tensor_tensor(out=xg[i][0:rows, :], in0=psB[0:rows, i * 512:(i + 1) * 512],
                                                              in1=gate_bc[0:rows, grow, nb * 512:(nb + 1) * 512], op=ALU.mult),
                             reads=[b_gate, bB[i]], writes=[bxg[i]])
                        P.op("pool", lambda e: e.tensor_tensor(out=xp[i][0:rows, :], in0=xp[i][0:rows, :], in1=xg[i][0:rows, :], op=ALU.add),
                             reads=[bxg[i]], writes=[bxp[i]])
                        P.dma("sp", out=xs[t * 128:t * 128 + rows, nb * 512:(nb + 1) * 512], in_=xp[i][0:rows, :], reads=[bxp[i]], writes=[bxs[t][nb]])
            pbc.close()
            P.barrier()
      except _Stop:
        P.barrier()
        dbg = nc.dram_tensor("dbg", [128, 64], F32, kind="ExternalOutput").ap()
        P.dma("sp", out=dbg, in_=sm[:], reads=[b_sm, b_lp])
        P.barrier()
        return P

    with ExitStack() as pf:
        fg = P.sb(pf, "fg", [128, DM], F32)
        xt = [P.sb(pf, "fxt", [128, DM], F32) for _ in range(2)]
        xq = [P.sb(pf, "fxq", [128, DM], F32) for _ in range(2)]
        ss = P.sb(pf, "fss", [128, 4], F32)
        bfg, bss, bxt, bxq = Buf("fg"), Buf("fss"), [Buf("a"), Buf("b")], [Buf("c"), Buf("d")]
        bout = Buf("out")
        P.dma("sp", out=fg[:], in_=final_g.broadcast_to([128, DM]), writes=[bfg])
        for t in range(8):
            i = t % 2
            P.dma("sp", out=xt[i][:], in_=xs[t * 128:(t + 1) * 128, :], reads=bxs[t], writes=[bxt[i]])
            P.op("pool", lambda e: e.tensor_tensor(out=xq[i][:], in0=xt[i][:], in1=xt[i][:], op=ALU.mult), reads=[bxt[i]], writes=[bxq[i]])
            P.op("dve", lambda e: e.reduce_sum(out=ss[:, 0:1], in_=xq[i][:], axis=AX.X), reads=[bxq[i]], writes=[bss])
            rstd(ss[:, 2:3], ss[:, 0:1], ss[:, 1:2], DM, bss)
            P.op("dve", lambda e: e.scalar_tensor_tensor(out=xq[i][:], in0=xt[i][:], scalar=ss[:, 2:3], in1=fg[:], op0=ALU.mult, op1=ALU.mult),
                 reads=[bxt[i], bss, bfg], writes=[bxq[i]])
            P.dma("sp", out=out[t * 128:(t + 1) * 128, :], in_=xq[i][:], reads=[bxq[i]], parts=[bout])
    P.barrier()
    return P


DEBUG_STOP = None


def _rope_tables(r):
    tok = r * 1024 + np.arange(1024)
    row = (tok // 64).astype(np.float32)
    col = (tok % 64).astype(np.float32)
    tabs = {}
    for dim in (64, 128):
        nf = dim // 4
        inv = (np.float32(10000.0) ** (-np.arange(nf, dtype=np.float32) / np.float32(nf))).astype(np.float32)
        ang = np.concatenate([row[:, None] * inv[None], col[:, None] * inv[None]], axis=-1).astype(np.float32)
        tabs[dim] = (np.cos(ang).astype(np.float32), np.sin(ang).astype(np.float32))
    t = np.zeros((1024, 288), np.float32)
    t[:, 0:32] = tabs[64][0]
    t[:, 32:96] = tabs[128][0]
    t[:, 96:128] = tabs[64][1]
    t[:, 128:192] = tabs[128][1]
    t[:, 192:224] = -tabs[64][1]
    t[:, 224:288] = -tabs[128][1]
    return np.ascontiguousarray(t.reshape(8, 128, 288).transpose(1, 0, 2).reshape(128, 8 * 288))


def _masks(r):
    i = np.arange(128)[:, None]
    j = np.arange(128)[None, :]
    triP = (i >= j).astype(np.float32)
    triN = (i <= j).astype(np.float32)
    m = np.zeros((128, 10, 128), np.float32)
    m[:, 0] = triP
    m[:, 1] = triN
    for s in range(4):
        if s == r - 1:
            m[:, 2 + s] = triP
        if s == r + 1:
            m[:, 6 + s] = triN
    return np.ascontiguousarray(m.reshape(128, 1280))


def make_in_maps(inp):
    f = lambda a: np.ascontiguousarray(np.asarray(a, dtype=np.float32))
    x, c, ctx, c_ctx = f(inp["x"]), f(inp["c"]), f(inp["ctx"]), f(inp["c_ctx"])
    w_mod, b_mod = f(inp["w_mod"]), f(inp["b_mod"])
    shared = {
        "norm_g": f(inp["norm_g"]), "final_g": f(inp["final_g"]).reshape(1, DM), "w_o": f(inp["w_o"]),
        "e_w_in": f(inp["e_w_in"]), "o_w_in": f(inp["o_w_in"]),
        "a_lam": np.ascontiguousarray(np.concatenate([f(inp["a_lam_q1"]), f(inp["a_lam_k1"]), f(inp["a_lam_q2"]), f(inp["a_lam_k2"])], axis=1)),
        "a_sub_g": f(inp["a_sub_g"]), "b_sink": f(inp["b_sink"]), "c_q_g": f(inp["c_q_g"]), "c_k_g": f(inp["c_k_g"]),
        "d_q_a_g": f(inp["d_q_a_g"]), "d_kv_a_g": f(inp["d_kv_a_g"]), "d_w_q_b": f(inp["d_w_q_b"]), "d_w_kv_b": f(inp["d_w_kv_b"]),
        "ident": np.eye(128, dtype=np.float32),
    }
    wm_sl = [np.ascontiguousarray(w_mod[:, :, r * 1536:(r + 1) * 1536]) for r in range(4)]
    bm_sl = [np.ascontiguousarray(b_mod[:, r * 1536:(r + 1) * 1536]) for r in range(4)]
    maps = []
    for core in range(8):
        b, r = divmod(core, 4)
        m = dict(shared)
        m["x"] = np.ascontiguousarray(x[b, r * 1024:(r + 1) * 1024])
        m["ctx"] = np.ascontiguousarray(ctx[b, r * 64:(r + 1) * 64])
        m["cvec"] = np.ascontiguousarray(np.stack([c[b], c_ctx]))
        m["wmod"] = wm_sl[r]
        m["bmod"] = bm_sl[r]
        m["rope"] = _rope_tables(r)
        m["bmask"] = _masks(r)
        maps.append(m)
    return maps


_NC_CACHE = {}


def kernel(**inputs):
    if "nc" not in _NC_CACHE:
        _NC_CACHE["nc"] = build(4).nc
    maps = make_in_maps(inputs)
    res = run_bass_kernel_spmd(_NC_CACHE["nc"], maps, core_ids=list(range(8)))
    out = np.zeros((2, 4096, DM), np.float32)
    for core in range(8):
        b, r = divmod(core, 4)
        out[b, r * 1024:(r + 1) * 1024] = res.results[core]["out"]
    return out
```
